# Optimizing a Trainium2 kernel written in Bass

```python
import math
import jax, jax.numpy as jnp
from jax import lax
import numpy as np

D_MODEL = 1024
BATCH = 32
SEQ = 2048
DEPTH = 2

CHUNK = 64
D_MIX = D_MODEL
HEAD_DIM = 64
N_Q_HEADS = (D_MIX // 2) // HEAD_DIM
N_KV_HEADS = 2
GQA_GROUP = N_Q_HEADS // N_KV_HEADS
ATTN_WIDTH = N_Q_HEADS * HEAD_DIM
KV_WIDTH = N_KV_HEADS * HEAD_DIM
WINDOW = 128
LOOKBACK_CHUNKS = WINDOW // CHUNK
ROPE_THETA = 500000.0
ROPE_DIM = HEAD_DIM // 4
SCONV_WIDTH = D_MIX // 4
SCONV_K = 3
LRU_WIDTH = D_MIX // 4
LRU_BLOCKS = 4
LRU_BLOCK_DIM = LRU_WIDTH // LRU_BLOCKS
LRU_CONV_K = 4
LRU_C = 8.0
IN_SIZES = (ATTN_WIDTH, KV_WIDTH, KV_WIDTH, SCONV_WIDTH, SCONV_WIDTH, SCONV_WIDTH, LRU_WIDTH, LRU_WIDTH)
D_IN = sum(IN_SIZES)
IN_SPLITS = tuple(int(v) for v in np.cumsum(IN_SIZES)[:-1])
D_FF = 2816
FFN_CONV_K = 3
NORM_EPS = 1e-6

kernel_name = "hybrid_swa_sconv_rglru_convffn"


def rms_norm(x, g):
    xf = x.astype(jnp.float32)
    y = xf * lax.rsqrt(jnp.mean(xf * xf, axis=-1, keepdims=True) + NORM_EPS)
    return (y * g.astype(jnp.float32)).astype(x.dtype)


def causal_dwconv(x, w, b=None):
    k, c = w.shape
    y = lax.conv_general_dilated(
        x, w[:, None, :].astype(x.dtype), window_strides=(1,), padding=[(k - 1, 0)],
        dimension_numbers=("NWC", "WIO", "NWC"), feature_group_count=c)
    if b is not None:
        y = y + b.astype(x.dtype)
    return y


def rope_tables(positions):
    inv_freq = ROPE_THETA ** (-jnp.arange(0, ROPE_DIM, 2, dtype=jnp.float32) / ROPE_DIM)
    ang = positions.astype(jnp.float32)[..., None] * inv_freq
    return jnp.cos(ang)[:, :, None, :], jnp.sin(ang)[:, :, None, :]


def apply_partial_rope(t, cos, sin):
    half = ROPE_DIM // 2
    tf = t.astype(jnp.float32)
    t1, t2 = tf[..., :half], tf[..., half:ROPE_DIM]
    out = jnp.concatenate([t1 * cos - t2 * sin, t2 * cos + t1 * sin, tf[..., ROPE_DIM:]], axis=-1)
    return out.astype(t.dtype)


def chunk_window(t):
    b, nc = t.shape[0], t.shape[1]
    tp = jnp.pad(t, ((0, 0), (LOOKBACK_CHUNKS, 0), (0, 0), (0, 0), (0, 0)))
    win = jnp.stack([tp[:, j:j + nc] for j in range(LOOKBACK_CHUNKS + 1)], axis=2)
    return win.reshape(b, nc, (LOOKBACK_CHUNKS + 1) * CHUNK, N_KV_HEADS, HEAD_DIM)


def sliding_window_attention(q, k, v, sinks):
    b, s = q.shape[0], q.shape[1]
    nc = s // CHUNK
    qc = q.reshape(b, nc, CHUNK, N_KV_HEADS, GQA_GROUP, HEAD_DIM)
    kw = chunk_window(k.reshape(b, nc, CHUNK, N_KV_HEADS, HEAD_DIM))
    vw = chunk_window(v.reshape(b, nc, CHUNK, N_KV_HEADS, HEAD_DIM))
    scores = jnp.einsum("bnqkgd,bnskd->bnkgqs", qc.astype(jnp.float32), kw.astype(jnp.float32))
    scores = scores * (HEAD_DIM ** -0.5)
    key_chunk = jnp.arange(nc)[:, None] - LOOKBACK_CHUNKS + jnp.arange(LOOKBACK_CHUNKS + 1)[None, :]
    valid = jnp.repeat(key_chunk >= 0, CHUNK, axis=1)
    scores = jnp.where(valid[None, :, None, None, None, :], scores, -jnp.inf)
    sink = sinks.astype(jnp.float32).reshape(1, 1, N_KV_HEADS, GQA_GROUP, 1, 1)
    m = jnp.maximum(jnp.max(scores, axis=-1, keepdims=True), sink)
    p = jnp.exp(scores - m)
    p = p / (jnp.sum(p, axis=-1, keepdims=True) + jnp.exp(sink - m))
    o = jnp.einsum("bnkgqs,bnskd->bnqkgd", p.astype(v.dtype), vw)
    return o.reshape(b, s, ATTN_WIDTH)


def rg_lru(xc, w_a, b_a, w_x, b_x, lam):
    b, s = xc.shape[0], xc.shape[1]
    xf = xc.astype(jnp.float32)
    xb = xf.reshape(b, s, LRU_BLOCKS, LRU_BLOCK_DIM)
    r = jax.nn.sigmoid(jnp.einsum("bshi,hij->bshj", xb, w_a.astype(jnp.float32)) + b_a.astype(jnp.float32))
    i = jax.nn.sigmoid(jnp.einsum("bshi,hij->bshj", xb, w_x.astype(jnp.float32)) + b_x.astype(jnp.float32))
    r = r.reshape(b, s, LRU_WIDTH)
    i = i.reshape(b, s, LRU_WIDTH)
    log_a = -LRU_C * r * jax.nn.softplus(-lam.astype(jnp.float32))
    a = jnp.exp(log_a)
    u = xf * i * jnp.sqrt(-jnp.expm1(2.0 * log_a))

    def combine(left, right):
        a1, b1 = left
        a2, b2 = right
        return a1 * a2, a2 * b1 + b2

    _, h = lax.associative_scan(combine, (a, u), axis=1)
    return h


def hybrid_layer(x, cos, sin, norm_mix_g, w_in, attn_sinks, sconv_w, lru_conv_w, lru_conv_b,
                 lru_wa, lru_ba, lru_wx, lru_bx, lru_lambda, gn_attn_g, gn_sconv_g, gn_lru_g,
                 w_out, norm_ffn_g, ffn_w_up, ffn_conv_w, ffn_conv_b, ffn_w_down):
    b, s, _ = x.shape
    h = rms_norm(x, norm_mix_g)
    proj = h @ w_in
    q, k, v, cb, cc, cx, lx, lg = jnp.split(proj, IN_SPLITS, axis=-1)
    q = apply_partial_rope(q.reshape(b, s, N_Q_HEADS, HEAD_DIM), cos, sin)
    k = apply_partial_rope(k.reshape(b, s, N_KV_HEADS, HEAD_DIM), cos, sin)
    v = v.reshape(b, s, N_KV_HEADS, HEAD_DIM)
    attn_out = sliding_window_attention(q, k, v, attn_sinks)
    sconv_out = cb * causal_dwconv(cc * cx, sconv_w)
    lru_in = causal_dwconv(lx, lru_conv_w, lru_conv_b)
    lru_h = rg_lru(lru_in, lru_wa, lru_ba, lru_wx, lru_bx, lru_lambda)
    lru_out = (lru_h * jax.nn.gelu(lg.astype(jnp.float32), approximate=True)).astype(x.dtype)
    mixed = jnp.concatenate([rms_norm(attn_out, gn_attn_g),
                             rms_norm(sconv_out, gn_sconv_g),
                             rms_norm(lru_out, gn_lru_g)], axis=-1)
    x = x + mixed @ w_out
    h = rms_norm(x, norm_ffn_g)
    up = causal_dwconv(h @ ffn_w_up, ffn_conv_w, ffn_conv_b)
    gate, val = jnp.split(up, 2, axis=-1)
    x = x + (jax.nn.silu(gate) * val) @ ffn_w_down
    return x


def setup_inputs(seed: int = 0) -> dict:
    key = jax.random.key(seed)
    ks = jax.random.split(key, 24)
    f32 = jnp.float32
    L = DEPTH

    def nrm(k, shape, scale):
        return jax.random.normal(k, shape, f32) * scale

    x = jax.random.normal(ks[0], (BATCH, SEQ, D_MODEL), f32)
    offsets = jax.random.randint(ks[1], (BATCH, 1), 0, 64, dtype=jnp.int32) * CHUNK
    positions = offsets + jnp.arange(SEQ, dtype=jnp.int32)[None, :]
    a_c = jax.random.uniform(ks[2], (L, LRU_WIDTH), f32, 0.9, 0.999)
    s_a = a_c ** (1.0 / LRU_C)
    lru_lambda = jnp.log(s_a) - jnp.log1p(-s_a)
    return {
        "x": x,
        "positions": positions,
        "norm_mix_g": 1.0 + nrm(ks[3], (L, D_MODEL), 0.02),
        "w_in": nrm(ks[4], (L, D_MODEL, D_IN), D_MODEL ** -0.5),
        "attn_sinks": nrm(ks[5], (L, N_Q_HEADS), 1.0),
        "sconv_w": nrm(ks[6], (L, SCONV_K, SCONV_WIDTH), SCONV_K ** -0.5),
        "lru_conv_w": nrm(ks[7], (L, LRU_CONV_K, LRU_WIDTH), LRU_CONV_K ** -0.5),
        "lru_conv_b": nrm(ks[8], (L, LRU_WIDTH), 0.02),
        "lru_wa": nrm(ks[9], (L, LRU_BLOCKS, LRU_BLOCK_DIM, LRU_BLOCK_DIM), LRU_BLOCK_DIM ** -0.5),
        "lru_ba": nrm(ks[10], (L, LRU_BLOCKS, LRU_BLOCK_DIM), 0.1),
        "lru_wx": nrm(ks[11], (L, LRU_BLOCKS, LRU_BLOCK_DIM, LRU_BLOCK_DIM), LRU_BLOCK_DIM ** -0.5),
        "lru_bx": nrm(ks[12], (L, LRU_BLOCKS, LRU_BLOCK_DIM), 0.1),
        "lru_lambda": lru_lambda,
        "gn_attn_g": 1.0 + nrm(ks[13], (L, ATTN_WIDTH), 0.02),
        "gn_sconv_g": 1.0 + nrm(ks[14], (L, SCONV_WIDTH), 0.02),
        "gn_lru_g": 1.0 + nrm(ks[15], (L, LRU_WIDTH), 0.02),
        "w_out": nrm(ks[16], (L, D_MIX, D_MODEL), D_MIX ** -0.5),
        "norm_ffn_g": 1.0 + nrm(ks[17], (L, D_MODEL), 0.02),
        "ffn_w_up": nrm(ks[18], (L, D_MODEL, 2 * D_FF), D_MODEL ** -0.5),
        "ffn_conv_w": nrm(ks[19], (L, FFN_CONV_K, 2 * D_FF), FFN_CONV_K ** -0.5),
        "ffn_conv_b": nrm(ks[20], (L, 2 * D_FF), 0.02),
        "ffn_w_down": nrm(ks[21], (L, D_FF, D_MODEL), D_FF ** -0.5),
        "final_norm_g": 1.0 + nrm(ks[22], (D_MODEL,), 0.02),
    }


def reference(x, positions, norm_mix_g, w_in, attn_sinks, sconv_w, lru_conv_w, lru_conv_b,
              lru_wa, lru_ba, lru_wx, lru_bx, lru_lambda, gn_attn_g, gn_sconv_g, gn_lru_g,
              w_out, norm_ffn_g, ffn_w_up, ffn_conv_w, ffn_conv_b, ffn_w_down, final_norm_g):
    cos, sin = rope_tables(positions)
    for l in range(DEPTH):
        x = hybrid_layer(x, cos, sin, norm_mix_g[l], w_in[l], attn_sinks[l], sconv_w[l],
                         lru_conv_w[l], lru_conv_b[l], lru_wa[l], lru_ba[l], lru_wx[l], lru_bx[l],
                         lru_lambda[l], gn_attn_g[l], gn_sconv_g[l], gn_lru_g[l], w_out[l],
                         norm_ffn_g[l], ffn_w_up[l], ffn_conv_w[l], ffn_conv_b[l], ffn_w_down[l])
    return rms_norm(x, final_norm_g)
```

```python
import math
from contextlib import ExitStack

import numpy as np
import concourse.bass as bass
import concourse.mybir as mybir
from concourse.bass_utils import run_bass_kernel_spmd

F32 = mybir.dt.float32
BF16 = mybir.dt.bfloat16
I32 = mybir.dt.int32
AF = mybir.ActivationFunctionType
ALU = mybir.AluOpType

D = 1024
L = 2
TT = 512
NU = 98
NCOL = 230
NSLOT = 8
NSC = 10
EPS = 1e-6
ENGS = ("sp", "pe", "act", "dve", "pool")

C_NMIX, C_NFFN, C_GN, C_SCW, C_LCW, C_LCB, C_BA, C_BX, C_LAM, C_FCW, C_FCB, C_FIN = 0, 8, 16, 24, 30, 38, 40, 42, 44, 46, 178, 222


def _unit_from_cols(W, cols):
    cols = np.asarray(cols)
    sel = np.zeros((W.shape[0], 128), np.float32)
    ok = cols >= 0
    sel[:, ok] = W[:, cols[ok]]
    kc = W.shape[0] // 128
    u = np.zeros((128, 8, 128), np.float32)
    u[:, :kc, :] = sel.reshape(kc, 128, 128).transpose(1, 0, 2)
    return u.reshape(128, 1024)


def _mix_perm():
    perm = np.zeros(1024, np.int64)
    for j in range(4):
        for p in range(128):
            perm[j * 128 + p] = j * 64 + p if p < 64 else (j + 4) * 64 + (p - 64)
    perm[512:] = np.arange(512, 1024)
    return perm


def _prep_weights(inp):
    perm = _mix_perm()
    wst = np.zeros((L * NU, 128, 1024), np.float32)
    cvec = np.zeros((128, L * NCOL), np.float32)
    sinkr = np.zeros((1, L * 512), np.float32)
    for l in range(L):
        Win = np.asarray(inp["w_in"][l], np.float32)
        units = []

        def head_cols(base, h):
            return [base + h * 64 + d for d in range(64)]

        def swap_cols(base, h):
            out = []
            for d in range(64):
                if d < 8:
                    out.append(base + h * 64 + d + 8)
                elif d < 16:
                    out.append(base + h * 64 + d - 8)
                else:
                    out.append(-1)
            return out

        for j in range(4):
            units.append(_unit_from_cols(Win, head_cols(0, j) + head_cols(0, j + 4)))
            units.append(_unit_from_cols(Win, swap_cols(0, j) + swap_cols(0, j + 4)))
        units.append(_unit_from_cols(Win, head_cols(512, 0) + head_cols(512, 1)))
        units.append(_unit_from_cols(Win, swap_cols(512, 0) + swap_cols(512, 1)))
        units.append(_unit_from_cols(Win, list(range(640, 768))))
        for ch in range(2):
            units.append(_unit_from_cols(Win, list(range(768 + ch * 128, 768 + (ch + 1) * 128))))
        for ch in range(2):
            units.append(_unit_from_cols(Win, list(range(1280 + ch * 128, 1280 + (ch + 1) * 128))))
            units.append(_unit_from_cols(Win, list(range(1024 + ch * 128, 1024 + (ch + 1) * 128))))
        for ch in range(2):
            units.append(_unit_from_cols(Win, list(range(1536 + ch * 128, 1536 + (ch + 1) * 128))))
        for ch in range(2):
            units.append(_unit_from_cols(Win, list(range(1792 + ch * 128, 1792 + (ch + 1) * 128))))
        g = np.zeros((128, 1024), np.float32)
        for gi, W in enumerate((inp["lru_wa"][l], inp["lru_wx"][l])):
            W = np.asarray(W, np.float32)
            for ch in range(2):
                col0 = (gi * 2 + ch) * 128
                g[0:64, col0:col0 + 64] = W[2 * ch]
                g[64:128, col0 + 64:col0 + 128] = W[2 * ch + 1]
        units.append(g)
        Wo = np.asarray(inp["w_out"][l], np.float32)[perm, :]
        for n in range(8):
            units.append(_unit_from_cols(Wo, list(range(n * 128, (n + 1) * 128))))
        Wu = np.asarray(inp["ffn_w_up"][l], np.float32)
        for jj in range(22):
            for s in range(2):
                units.append(_unit_from_cols(Wu, list(range(s * 2816 + jj * 128, s * 2816 + (jj + 1) * 128))))
        Wd = np.asarray(inp["ffn_w_down"][l], np.float32)
        for m in range(8):
            for gq in range(3):
                u = np.zeros((128, 8, 128), np.float32)
                for fi in range(8):
                    f = gq * 8 + fi
                    if f < 22:
                        u[:, fi, :] = Wd[f * 128:(f + 1) * 128, m * 128:(m + 1) * 128]
                units.append(u.reshape(128, 1024))
        assert len(units) == NU
        wst[l * NU:(l + 1) * NU] = np.stack(units)

        cv = cvec[:, l * NCOL:(l + 1) * NCOL]

        def colmat(v):
            return np.asarray(v, np.float32).reshape(-1, 128).T

        cv[:, C_NMIX:C_NMIX + 8] = colmat(inp["norm_mix_g"][l])
        cv[:, C_NFFN:C_NFFN + 8] = colmat(inp["norm_ffn_g"][l])
        gn = np.concatenate([np.asarray(inp["gn_attn_g"][l]), np.asarray(inp["gn_sconv_g"][l]),
                             np.asarray(inp["gn_lru_g"][l])]).astype(np.float32)[perm]
        cv[:, C_GN:C_GN + 8] = colmat(gn)
        for j in range(3):
            cv[:, C_SCW + j * 2:C_SCW + j * 2 + 2] = colmat(inp["sconv_w"][l][j])
        for j in range(4):
            cv[:, C_LCW + j * 2:C_LCW + j * 2 + 2] = colmat(inp["lru_conv_w"][l][j])
        cv[:, C_LCB:C_LCB + 2] = colmat(inp["lru_conv_b"][l])
        cv[:, C_BA:C_BA + 2] = colmat(np.asarray(inp["lru_ba"][l]).reshape(-1))
        cv[:, C_BX:C_BX + 2] = colmat(np.asarray(inp["lru_bx"][l]).reshape(-1))
        cv[:, C_LAM:C_LAM + 2] = colmat(inp["lru_lambda"][l])
        fperm = np.zeros(5632, np.int64)
        for uidx in range(44):
            jj, s = uidx // 2, uidx % 2
            fperm[uidx * 128:(uidx + 1) * 128] = s * 2816 + jj * 128 + np.arange(128)
        for j in range(3):
            cv[:, C_FCW + j * 44:C_FCW + (j + 1) * 44] = colmat(np.asarray(inp["ffn_conv_w"][l][j])[fperm])
        cv[:, C_FCB:C_FCB + 44] = colmat(np.asarray(inp["ffn_conv_b"][l])[fperm])
        cv[:, C_FIN:C_FIN + 8] = colmat(inp["final_norm_g"])
        sk = np.asarray(inp["attn_sinks"][l], np.float32)
        for kv in range(2):
            for gq in range(4):
                o = l * 512 + kv * 256 + gq * 64
                sinkr[0, o:o + 64] = sk[kv * 4 + gq]
    ropec = np.zeros((128, 2), np.float32)
    ropec[:, 1] = 1.0
    for p in range(128):
        d = p % 64
        if d < 16:
            ropec[p, 0] = np.float32(500000.0) ** np.float32(-(2 * (d % 8)) / 16.0)
            ropec[p, 1] = -1.0 if d < 8 else 1.0
    return wst, cvec, sinkr, ropec


class Buf:
    __slots__ = ("name", "w", "r")

    def __init__(self, name):
        self.name = name
        self.w = None
        self.r = {}


class Tracker:
    def __init__(self):
        self.plan = {e: [] for e in ENGS}
        self.cnt = {}
        self.seen = {e: {} for e in ENGS}
        self.epoch = 0

    def new_epoch(self):
        self.epoch += 1

    def _waits(self, eng, reads, writes):
        toks = []
        for b in reads:
            if b.w is not None:
                toks.append(b.w)
        for b in writes:
            if b.w is not None:
                toks.append(b.w)
            toks.extend(b.r.values())
        waits = []
        seen = self.seen[eng]
        for key, val in toks:
            if eng == "pe" and key[0] == "pe":
                continue
            if seen.get(key, 0) < val:
                seen[key] = val
                waits.append((key, val))
        best = {}
        for key, val in waits:
            best[key] = max(best.get(key, 0), val)
        return list(best.items())

    def _mark(self, rkey, tok, reads, writes):
        for b in reads:
            b.r[rkey] = tok
        for b in writes:
            b.w = tok
            b.r = {}

    def op(self, eng, fn, reads=(), writes=(), inc=True):
        waits = self._waits(eng, reads, writes)
        key = (eng, self.epoch)
        if inc:
            self.cnt[key] = self.cnt.get(key, 0) + 1
            tok = (key, self.cnt[key])
        else:
            tok = (key, self.cnt.get(key, 0) + 1)
        self.plan[eng].append((waits, fn, key if inc else None, 1))
        self._mark(eng, tok, reads, writes)
        return tok

    def dma(self, fn, semname, reads=(), writes=(), queue="sp"):
        waits = self._waits(queue, reads, writes)
        key = ("dma", semname)
        prev = self.cnt.get(key, 0)
        if prev and self.seen[queue].get(key, 0) < prev:
            self.seen[queue][key] = prev
            waits.append((key, prev))
        self.cnt[key] = prev + 16
        tok = (key, prev + 16)
        self.plan[queue].append((waits, fn, key, 16))
        self._mark(key, tok, reads, writes)
        return tok

    def final_wait(self, queue, toks):
        self.plan[queue].append((list(toks), None, None, 0))


def build_program(NSEQ, S, debug=False):
    NT = S // TT
    dbg_outs = {}
    nc = bass.Bass("TRN2", target_bir_lowering=False)
    xT = nc.dram_tensor("xT", [NSEQ, D, S], F32, kind="ExternalInput").ap()
    posd = nc.dram_tensor("pos", [NSEQ, S], I32, kind="ExternalInput").ap()
    wst = nc.dram_tensor("wst", [L * NU, 128, 1024], F32, kind="ExternalInput").ap()
    cvd = nc.dram_tensor("cvec", [128, L * NCOL], F32, kind="ExternalInput").ap()
    roped = nc.dram_tensor("ropec", [128, 2], F32, kind="ExternalInput").ap()
    sinkd = nc.dram_tensor("sinkr", [1, L * 512], F32, kind="ExternalInput").ap()
    identd = nc.dram_tensor("ident", [128, 128], F32, kind="ExternalInput").ap()
    outT = nc.dram_tensor("outT", [NSEQ, D, S], F32, kind="ExternalOutput").ap()
    wbf = nc.dram_tensor("wbf", [L * NU, 128, 1024], BF16, kind="Internal").ap()

    TR = Tracker()
    es = ExitStack()

    def sb(name, shape, dt):
        return es.enter_context(nc.sbuf_tensor(name, shape, dt))

    def bufs(name, n):
        return [Buf(f"{name}{i}") for i in range(n)]

    xbuf = sb("xbuf", [128, 8, TT], F32); xB = bufs("x", 8)
    xbuf_b = sb("xbuf_b", [128, 8, TT], F32); xB_b = bufs("xb_", 8)
    xbufs = [xbuf, xbuf_b]; xBs = [xB, xB_b]
    hT = sb("hT", [128, 8, TT], BF16); hB = bufs("h", 8)
    sqb = sb("sqb", [128, 8, TT], BF16); sqB = bufs("sq", 8)
    qT = sb("qT", [128, 4, TT], BF16); qB = bufs("q", 4)
    kT = [sb(f"kT{l}", [128, 128 + TT], BF16) for l in range(L)]; kB = bufs("k", L)
    vaug = [sb(f"vaug{l}", [128, 5, 2, 128], BF16) for l in range(L)]; vB = bufs("v", L)
    cbs = sb("cbs", [128, 2, TT], F32); cbB = bufs("cb", 2)
    prodbuf = sb("prodbuf", [128, 2, TT + 2], F32); prB = bufs("pr", 2)
    prodc = [sb(f"prodc{l}", [128, 2, 2], F32) for l in range(L)]; pcB = [bufs(f"pc{l}_", 2) for l in range(L)]
    lxbuf = sb("lxbuf", [128, 2, TT + 3], F32); lxB = bufs("lx", 2)
    lxc = [sb(f"lxc{l}", [128, 2, 3], F32) for l in range(L)]; lcB = [bufs(f"lc{l}_", 2) for l in range(L)]
    mixbuf = sb("mixbuf", [128, 8, TT], F32); mixB = bufs("mix", 8); aoB = bufs("ao", 8)
    SC = [sb(f"sc{i}", [128, TT], F32) for i in range(NSC)]; scB = bufs("sc", NSC)
    lru_in = sb("lru_in", [128, 2, TT], F32); liB = bufs("li", 2)
    lrub = sb("lrub", [128, 2, TT], BF16); lbB = bufs("lb", 2)
    glb = sb("glb", [128, 2, TT], F32); glB = bufs("gl", 2)
    hstate = [sb(f"hstate{l}", [128, 2], F32) for l in range(L)]; hsB = [bufs(f"hs{l}_", 2) for l in range(L)]
    Pbuf = [sb(f"P{i}", [128, 512], BF16) for i in range(2)]; PB = bufs("P", 2)
    denall = sb("denall", [128, 8, 4, 64], F32); denB = bufs("den", 8)
    cosT = sb("cosT", [128, TT], F32); cosB = Buf("cos")
    sinT = sb("sinT", [128, TT], F32); sinB = Buf("sin")
    posi = sb("posi", [128, TT], I32); posB = Buf("posi")
    ubuf = [sb(f"ub{i}", [128, TT + 2], F32) for i in range(4)]; ubB = bufs("ub", 4)
    ucarry = [sb(f"ucarry{l}", [128, 44, 2], F32) for l in range(L)]; ucB = [bufs(f"uc{l}_", 44) for l in range(L)]
    gvb = [sb(f"gv{i}", [128, TT], F32) for i in range(6)]; gvB = bufs("gv", 6)
    actT = sb("actT", [128, 22, TT], BF16); actB = bufs("act", 22)
    wring = [sb(f"wr{i}", [128, 1024], BF16) for i in range(NSLOT)]; wrB = bufs("wr", NSLOT)
    cv = sb("cv", [128, L * NCOL], F32); cvB = Buf("cv")
    ropec = sb("ropecs", [128, 2], F32); ropeB = Buf("ropec")
    der = sb("der", [128, L * 8], F32); derB = Buf("der")
    dtmp = sb("dtmp", [128, 8], F32); dtB = Buf("dtmp")
    sink32 = sb("sink32", [1, L * 512], F32); s32B = Buf("sink32")
    sinkRb = sb("sinkRb", [1, L * 512], BF16); sRB = Buf("sinkRb")
    sinkL = sb("sinkL", [1, 2, 128], BF16); sLB = Buf("sinkL")
    ones = sb("ones", [128, 128], BF16); onesB = Buf("ones")
    cpow = sb("cpow", [128, 4], F32); cpB = Buf("cpow")
    ident = sb("ident_sb", [128, 128], F32); identB = Buf("ident")
    onesf = sb("onesf", [128, 128], F32); onesfB = Buf("onesf")
    stat = [sb(f"stat{i}", [128, 8], F32) for i in range(2)]; statB = bufs("stat", 2)
    dg = [sb(f"dg{i}", [128, 4, 128], F32) for i in range(2)]; dgB = bufs("dg", 2)
    PS = [es.enter_context(nc.psum_tensor(f"ps{i}", [128, 512], F32)) for i in range(8)]
    psB = bufs("ps", 8)
    wbfB = bufs("wbf", L * NU)

    st = {"sc": 0, "mm": 0, "P": 0, "ub": 0, "stat": 0}

    def DBG(name, tile, shape, dt, bl):
        if not debug or name in dbg_outs:
            return
        d = nc.dram_tensor("dbg_" + name, list(shape), dt, kind="ExternalOutput").ap()
        dbg_outs[name] = d
        TR.dma(lambda e: e.dma_start(out=d, in_=tile), "dbg_" + name, bl, ())

    def scratch():
        i = st["sc"] % NSC
        st["sc"] += 1
        return SC[i], scB[i]

    MM_SETS = {"mix": [0, 1, 2], "ffn": [0, 1, 2, 4, 5, 6, 7]}
    st["mmset"] = "mix"

    def mmps():
        bs = MM_SETS[st["mmset"]]
        i = bs[st["mm"] % len(bs)]
        st["mm"] += 1
        return PS[i], psB[i]

    PSN, PSNB = PS[3], psB[3]

    def mixdeps(j):
        return [mixB[j]] + (aoB if j < 4 else [])

    def act(out, in_, func, reads, writes, bias=None, scale=None):
        kw = {}
        if bias is not None:
            kw["bias"] = bias
        if scale is not None:
            kw["scale"] = scale
        TR.op("act", lambda e: e.activation(out=out, in_=in_, func=func, **kw), reads, writes)

    def dve_tt(out, in0, in1, op, reads, writes):
        TR.op("dve", lambda e: e.tensor_tensor(out=out, in0=in0, in1=in1, op=op), reads, writes)

    def dve_stt(out, in0, scalar, in1, op0, op1, reads, writes):
        TR.op("dve", lambda e: e.scalar_tensor_tensor(out=out, in0=in0, scalar=scalar, in1=in1, op0=op0, op1=op1),
              reads, writes)

    def dve_ts(out, in0, s1, s2, op0, op1, reads, writes):
        if s2 is None:
            TR.op("dve", lambda e: e.tensor_scalar(out=out, in0=in0, scalar1=s1, scalar2=None, op0=op0), reads, writes)
        else:
            TR.op("dve", lambda e: e.tensor_scalar(out=out, in0=in0, scalar1=s1, scalar2=s2, op0=op0, op1=op1),
                  reads, writes)

    def dve_copy(out, in_, reads, writes):
        TR.op("dve", lambda e: e.tensor_copy(out=out, in_=in_), reads, writes)

    def pool_tt(out, in0, in1, op, reads, writes):
        TR.op("pool", lambda e: e.tensor_tensor(out=out, in0=in0, in1=in1, op=op), reads, writes)

    def pool_ts(out, in0, s1, s2, op0, op1, reads, writes):
        if s2 is None:
            TR.op("pool", lambda e: e.tensor_scalar(out=out, in0=in0, scalar1=s1, scalar2=None, op0=op0), reads, writes)
        else:
            TR.op("pool", lambda e: e.tensor_scalar(out=out, in0=in0, scalar1=s1, scalar2=s2, op0=op0, op1=op1),
                  reads, writes)

    def pool_copy(out, in_, reads, writes):
        TR.op("pool", lambda e: e.tensor_copy(out=out, in_=in_), reads, writes)

    def pool_memset(ap, val, writes):
        TR.op("pool", lambda e: e.memset(ap, val), (), writes)

    def mm_group(out, outB, items, extra_reads=()):
        n = len(items)
        for i, (lhsT, rhs, rb) in enumerate(items):
            last = i == n - 1
            TR.op("pe", (lambda e, lhsT=lhsT, rhs=rhs, i=i, last=last:
                         e.matmul(out, lhsT=lhsT, rhs=rhs, start=(i == 0), stop=last)),
                  list(rb) + list(extra_reads), [outB], inc=last)

    total_units = NSEQ * NT * L * NU
    wstate = {"issued": 0, "consumed": 0}

    CAST_ENG = ["act", "dve", "pool", "act", "dve"]

    def issue_load(n):
        u = n % (L * NU)
        s = n % NSLOT
        if n >= L * NU:
            TR.dma(lambda e, u=u, s=s: e.dma_start(out=wring[s][:], in_=wbf[u]), f"w{s}", [wbfB[u]], [wrB[s]])
            return
        k4 = n % 4
        a32 = xbuf_b[:, 2 * k4:2 * k4 + 2, :]
        b32 = [xB_b[2 * k4], xB_b[2 * k4 + 1]]
        a16 = wring[s][:].rearrange("p (a t) -> p a t", a=2)
        TR.dma(lambda e, u=u, a32=a32: e.dma_start(out=a32, in_=wst[u].rearrange("p (a t) -> p a t", a=2)),
               f"pin{k4}", (), b32)
        ce = CAST_ENG[n % 5]
        if ce == "act":
            act(a16, a32, AF.Copy, b32, [wrB[s]])
        elif ce == "dve":
            dve_copy(a16, a32, b32, [wrB[s]])
        else:
            pool_copy(a16, a32, b32, [wrB[s]])
        TR.dma(lambda e, u=u, s=s: e.dma_start(out=wbf[u], in_=wring[s][:]), f"pout{k4}", [wrB[s]], [wbfB[u]])

    def next_unit():
        n = wstate["consumed"]
        while wstate["issued"] < min(total_units, n + NSLOT):
            issue_load(wstate["issued"])
            wstate["issued"] += 1
        wstate["consumed"] += 1
        s = n % NSLOT
        return wring[s], wrB[s]

    def proj_unit(rhs_tile, rhs_bufs):
        w, wb = next_unit()
        ps, pb = mmps()
        mm_group(ps[:], pb, [(w[:, kc * 128:(kc + 1) * 128], rhs_tile[:, kc, :], [wb, rhs_bufs[kc]]) for kc in range(8)])
        return ps, pb

    TR.dma(lambda e: e.dma_start(out=cv[:], in_=cvd[:, :]), "c0", (), [cvB])
    TR.dma(lambda e: e.dma_start(out=ropec[:], in_=roped[:, :]), "c1", (), [ropeB])
    TR.dma(lambda e: e.dma_start(out=sink32[:], in_=sinkd[:, :]), "c2", (), [s32B])
    pool_memset(ones[:], 1.0, [onesB])
    pool_memset(cpow[:], -0.5, [cpB])
    pool_memset(onesf[:], 1.0, [onesfB])
    TR.dma(lambda e: e.dma_start(out=ident[:], in_=identd[:, :]), "c3", (), [identB])
    pool_memset(sinkL[:, 0, 0:64], 0.0, [sLB]); pool_memset(sinkL[:, 0, 64:128], 1.0, [sLB])
    pool_memset(sinkL[:, 1, 0:64], 1.0, [sLB]); pool_memset(sinkL[:, 1, 64:128], 0.0, [sLB])
    for l in range(L):
        pool_memset(vaug[l][:], 1.0, [vB[l]])
    act(sinkRb[:], sink32[:], AF.Exp, [s32B], [sRB])
    for l in range(L):
        o = l * NCOL
        d0 = l * 8
        dve_ts(der[:, d0:d0 + 2], cv[:, o + C_BA:o + C_BA + 2], 0.5, None, ALU.mult, None, [cvB], [derB])
        dve_ts(der[:, d0 + 2:d0 + 4], cv[:, o + C_BX:o + C_BX + 2], 0.5, None, ALU.mult, None, [cvB], [derB])
        act(dtmp[:, 0:2], cv[:, o + C_LAM:o + C_LAM + 2], AF.Exp, [cvB], [dtB], scale=-1.0)
        dve_ts(dtmp[:, 2:4], dtmp[:, 0:2], -0.25, 1.0 / 3.0, ALU.mult, ALU.add, [dtB], [dtB])
        dve_tt(dtmp[:, 4:6], dtmp[:, 2:4], dtmp[:, 0:2], ALU.mult, [dtB], [dtB])
        dve_ts(dtmp[:, 2:4], dtmp[:, 4:6], -1.0, 0.5, ALU.mult, ALU.add, [dtB], [dtB])
        dve_tt(dtmp[:, 4:6], dtmp[:, 2:4], dtmp[:, 0:2], ALU.mult, [dtB], [dtB])
        dve_ts(dtmp[:, 2:4], dtmp[:, 4:6], -1.0, 1.0, ALU.mult, ALU.add, [dtB], [dtB])
        dve_tt(dtmp[:, 4:6], dtmp[:, 2:4], dtmp[:, 0:2], ALU.mult, [dtB], [dtB])
        dve_ts(der[:, d0 + 6:d0 + 8], dtmp[:, 4:6], -8.0, None, ALU.mult, None, [dtB], [derB])
        dve_ts(der[:, d0 + 4:d0 + 6], dtmp[:, 4:6], -4.0, None, ALU.mult, None, [dtB], [derB])

    PSNS = [Buf(f"psn{i}") for i in range(4)]

    accx = sb("accx", [128, 4], F32); accB = Buf("accx")

    def stats_square(n):
        act(sqb[:, n, :], xbuf[:, n, :], AF.Square, [xB[n]], [sqB[n]])

    def stats_reduce(n):
        for blk in range(4):
            TR.op("pe", (lambda e, n=n, blk=blk: e.matmul(PSN[:, 12 + blk:13 + blk], lhsT=sqb[:, n, blk * 128:(blk + 1) * 128],
                                                         rhs=ones[:, 0:1], start=True, stop=True)),
                  [sqB[n], onesB], [PSNS[3]], inc=(blk == 3))
        if n == 0:
            dve_copy(accx[:], PSN[:, 12:16], [PSNS[3]], [accB])
        else:
            dve_tt(accx[:], PSN[:, 12:16], accx[:], ALU.add, [PSNS[3], accB], [accB])

    def stats_accum(n, lag=2):
        stats_square(n)
        if n >= lag:
            stats_reduce(n - lag)
        if n == 7:
            for j in range(8 - lag, 8):
                stats_reduce(j)

    def multi_norm(groups):
        base = 0
        infos = []
        pre = st.pop("pre_stats", False)
        modes = []
        for gi, (src_list, nfeat, eps, consume) in enumerate(groups):
            if pre and gi == 0:
                modes.append("acc")
                continue
            if src_list is None:
                modes.append("early")
                continue
            modes.append("now")
            n = len(src_list)
            for i, (ap, bl) in enumerate(src_list):
                act(sqb[:, base + i, :], ap, AF.Square, bl, [sqB[base + i]])
            infos.append((gi, base, n))
            base += n
        for gi, b0, n in infos:
            for blk in range(4):
                mm_group(PSN[:, gi * 4 + blk:gi * 4 + blk + 1], PSNS[gi],
                         [(sqb[:, b0 + i, blk * 128:(blk + 1) * 128], ones[:, 0:1], [sqB[b0 + i], onesB]) for i in range(n)])
        ks = []
        for gi, (src_list, nfeat, eps, consume) in enumerate(groups):
            k = st["stat"] % 2
            st["stat"] += 1
            ks.append(k)
            if modes[gi] == "acc":
                dve_ts(stat[k][:, 0:4], accx[:], 1.0 / nfeat, eps, ALU.mult, ALU.add, [accB], [statB[k]])
            elif modes[gi] == "early":
                dve_ts(stat[k][:, 0:4], PSN[:, 8:12], 1.0 / nfeat, eps, ALU.mult, ALU.add, [PSNS[2]], [statB[k]])
            else:
                dve_ts(stat[k][:, 0:4], PSN[:, gi * 4:gi * 4 + 4], 1.0 / nfeat, eps, ALU.mult, ALU.add, [PSNS[gi]], [statB[k]])
        for k in ks:
            pool_tt(stat[k][:, 4:8], stat[k][:, 0:4], cpow[:, 0:4], ALU.pow, [statB[k], cpB], [statB[k]])
        for k in ks:
            for blk in range(4):
                dve_ts(dg[k][:, blk, :], ident[:], stat[k][:, 4 + blk:5 + blk], None, ALU.mult, None, [identB, statB[k]], [dgB[k]])
        outs = []
        for k in ks:
            ps, pb = mmps()
            mm_group(ps[:], pb, [(onesf[:], dg[k][:].rearrange("p b t -> p (b t)"), [onesfB, dgB[k]])])
            outs.append((ps, pb))
        for (src_list, nfeat, eps, consume), (ps, pb) in zip(groups, outs):
            consume(ps, pb)

    def early_next_stats(xb, xbl):
        for c in range(8):
            act(sqb[:, c, :], xb[:, c, :], AF.Square, [xbl[c]], [sqB[c]])
        for blk in range(4):
            mm_group(PSN[:, 8 + blk:9 + blk], PSNS[2],
                     [(sqb[:, c, blk * 128:(blk + 1) * 128], ones[:, 0:1], [sqB[c], onesB]) for c in range(8)])

    def cossin_tables(seq, it):
        t0 = it * TT
        TR.dma(lambda e: e.dma_start(out=posi[:], in_=posd[seq:seq + 1, t0:t0 + TT].partition_broadcast(128)),
               "pos", (), [posB])
        pf, pfb = scratch()
        dve_copy(pf[:], posi[:], [posB], [pfb])
        ang, angb = scratch()
        dve_ts(ang[:], pf[:], ropec[:, 0:1], None, ALU.mult, None, [pfb, ropeB], [angb])
        dve_ts(posi[:], ang[:], 1.0 / (2 * math.pi), None, ALU.mult, None, [angb], [posB])
        kf, kfb = scratch()
        dve_copy(kf[:], posi[:], [posB], [kfb])
        C1 = 6.28125
        C2 = 2 * math.pi - C1
        r1, r1b = scratch()
        dve_stt(r1[:], kf[:], -C1, ang[:], ALU.mult, ALU.add, [kfb, angb], [r1b])
        r, rb = scratch()
        dve_stt(r[:], kf[:], -C2, r1[:], ALU.mult, ALU.add, [kfb, r1b], [rb])
        m, mb = scratch()
        dve_ts(m[:], r[:], math.pi, -2 * math.pi, ALU.is_gt, ALU.mult, [rb], [mb])
        r2, r2b = scratch()
        dve_tt(r2[:], r[:], m[:], ALU.add, [rb, mb], [r2b])
        m2, m2b = scratch()
        dve_ts(m2[:], r2[:], -math.pi, 2 * math.pi, ALU.is_lt, ALU.mult, [r2b], [m2b])
        rs_, rsb = scratch()
        dve_tt(rs_[:], r2[:], m2[:], ALU.add, [r2b, m2b], [rsb])
        m3, m3b = scratch()
        dve_ts(m3[:], rs_[:], math.pi / 2, -2 * math.pi, ALU.is_gt, ALU.mult, [rsb], [m3b])
        rc, rcb = scratch()
        dve_stt(rc[:], rs_[:], math.pi / 2, m3[:], ALU.add, ALU.add, [rsb, m3b], [rcb])
        PI_S = 3.1415925
        rs2, rs2b = scratch()
        dve_ts(rs2[:], rs_[:], -PI_S, PI_S, ALU.max, ALU.min, [rsb], [rs2b])
        rc2, rc2b = scratch()
        dve_ts(rc2[:], rc[:], -PI_S, PI_S, ALU.max, ALU.min, [rcb], [rc2b])
        act(sinT[:], rs2[:], AF.Sin, [rs2b, ropeB], [sinB], scale=ropec[:, 1:2])
        act(cosT[:], rc2[:], AF.Sin, [rc2b], [cosB])

    def mixer(l, seq, it, skip_norm=False):
        o = l * NCOL
        d0 = l * 8
        def cons_a(rs, rsb):
            for c in range(8):
                dve_stt(hT[:, c, :], xbuf[:, c, :], cv[:, o + C_NMIX + c:o + C_NMIX + c + 1], rs[:], ALU.mult, ALU.mult,
                        [xB[c], cvB, rsb], [hB[c]])
        if not skip_norm:
            multi_norm([([(xbuf[:, c, :], [xB[c]]) for c in range(8)], D, EPS, cons_a)])
        if l == 0 and it == 0 and seq == 0:
            DBG('h0', hT[:], [128, 8, TT], BF16, hB)
        for j in range(5):
            psa, pba = proj_unit(hT, hB)
            psb_, pbb = proj_unit(hT, hB)
            t1, t1b = scratch()
            dve_tt(t1[:], psa[:], cosT[:], ALU.mult, [pba, cosB], [t1b])
            t2, t2b = scratch()
            dve_tt(t2[:], psb_[:], sinT[:], ALU.mult, [pbb, sinB], [t2b])
            if j < 4:
                pool_tt(qT[:, j, :], t1[:], t2[:], ALU.add, [t1b, t2b], [qB[j]])
            else:
                pool_tt(kT[l][:, 128:128 + TT], t1[:], t2[:], ALU.add, [t1b, t2b], [kB[l]])
        w, wb = next_unit()
        ps, pb = mmps()
        for blk in range(4):
            mm_group(ps[:, blk * 128:(blk + 1) * 128], pb,
                     [(hT[:, kc, blk * 128:(blk + 1) * 128], w[:, kc * 128:(kc + 1) * 128], [wb, hB[kc]]) for kc in range(8)])
        psv = ps[:].rearrange("p (b k d) -> p b k d", b=4, k=2, d=64)
        act(vaug[l][:, 1:5, 0, 0:64], psv[:, :, 0, :], AF.Copy, [pb], [vB[l]])
        act(vaug[l][:, 1:5, 1, 64:128], psv[:, :, 1, :], AF.Copy, [pb], [vB[l]])
        for ch in range(2):
            ps, pb = proj_unit(hT, hB)
            act(cbs[:, ch, :], ps[:], AF.Copy, [pb], [cbB[ch]])
        for ch in range(2):
            ps, pb = proj_unit(hT, hB)
            cx, cxb = scratch()
            act(cx[:], ps[:], AF.Copy, [pb], [cxb])
            ps2, pb2 = proj_unit(hT, hB)
            pool_copy(prodbuf[:, ch, 0:2], prodc[l][:, ch, :], [pcB[l][ch]], [prB[ch]])
            dve_tt(prodbuf[:, ch, 2:TT + 2], ps2[:], cx[:], ALU.mult, [pb2, cxb], [prB[ch]])
            pool_copy(prodc[l][:, ch, :], prodbuf[:, ch, TT:TT + 2], [prB[ch]], [pcB[l][ch]])
        for ch in range(2):
            ps, pb = proj_unit(hT, hB)
            pool_copy(lxbuf[:, ch, 0:3], lxc[l][:, ch, :], [lcB[l][ch]], [lxB[ch]])
            act(lxbuf[:, ch, 3:TT + 3], ps[:], AF.Copy, [pb], [lxB[ch]])
            pool_copy(lxc[l][:, ch, :], lxbuf[:, ch, TT:TT + 3], [lxB[ch]], [lcB[l][ch]])
        GK = 0.7978845608028654 * 0.044715
        for ch in range(2):
            ps, pb = proj_unit(hT, hB)
            x2, x2b = scratch()
            act(x2[:], ps[:], AF.Square, [pb], [x2b])
            lgs, lgb = scratch()
            act(lgs[:], ps[:], AF.Copy, [pb], [lgb])
            zp, zpb = scratch()
            dve_stt(zp[:], x2[:], 1.0 / 0.044715, lgs[:], ALU.add, ALU.mult, [x2b, lgb], [zpb])
            th, thb = scratch()
            act(th[:], zp[:], AF.Tanh, [zpb], [thb], scale=GK)
            dve_stt(glb[:, ch, :], th[:], 1.0, lgs[:], ALU.add, ALU.mult, [thb, lgb], [glB[ch]])
        gw, gwb = next_unit()
        if l == 0 and it == 0 and seq == 0:
            DBG('qT', qT[:], [128, 4, TT], BF16, qB)
            DBG('kT', kT[0][:], [128, 128 + TT], BF16, [kB[0]])
            DBG('vaug', vaug[0][:], [128, 5, 2, 128], BF16, [vB[0]])
            DBG('cbs', cbs[:], [128, 2, TT], F32, cbB)
            DBG('prod', prodbuf[:], [128, 2, TT + 2], F32, prB)
            DBG('lxb', lxbuf[:], [128, 2, TT + 3], F32, lxB)
            DBG('gl', glb[:], [128, 2, TT], F32, glB)
            DBG('cos', cosT[:], [128, TT], F32, [cosB])
            DBG('sin', sinT[:], [128, TT], F32, [sinB])
        tiT = [gvb[0], gvb[1]]; tiBf = [gvB[0], gvB[1]]
        aaT = [gvb[2], gvb[3]]; aaBf = [gvB[2], gvB[3]]
        wqT = [ubuf[0], ubuf[1]]; wqBf = [ubB[0], ubB[1]]
        trT = [gvb[4], gvb[5]]; trBf = [gvB[4], gvB[5]]
        def gn_consumer(chunks):
            def cons(rs, rsb):
                for j in chunks:
                    dve_stt(hT[:, j, :], mixbuf[:, j, :], cv[:, o + C_GN + j:o + C_GN + j + 1], rs[:], ALU.mult, ALU.mult,
                            mixdeps(j) + [cvB, rsb], [hB[j]])
            return cons

        def st_sconv(ch):
            wc = lambda j: cv[:, o + C_SCW + j * 2 + ch:o + C_SCW + j * 2 + ch + 1]
            t0, t0b = scratch()
            pool_ts(t0[:], prodbuf[:, ch, 2:TT + 2], wc(2), 0.0, ALU.mult, ALU.add, [prB[ch], cvB], [t0b])
            t1, t1b = scratch()
            dve_stt(t1[:], prodbuf[:, ch, 1:TT + 1], wc(1), t0[:], ALU.mult, ALU.add, [prB[ch], cvB, t0b], [t1b])
            t2, t2b = scratch()
            dve_stt(t2[:], prodbuf[:, ch, 0:TT], wc(0), t1[:], ALU.mult, ALU.add, [prB[ch], cvB, t1b], [t2b])
            dve_tt(mixbuf[:, 4 + ch, :], t2[:], cbs[:, ch, :], ALU.mult, [t2b, cbB[ch]], mixdeps(4 + ch))

        def st_lru_conv():
            for ch in range(2):
                wc = lambda j: cv[:, o + C_LCW + j * 2 + ch:o + C_LCW + j * 2 + ch + 1]
                t0, t0b = scratch()
                pool_ts(t0[:], lxbuf[:, ch, 3:TT + 3], wc(3), cv[:, o + C_LCB + ch:o + C_LCB + ch + 1], ALU.mult, ALU.add,
                        [lxB[ch], cvB], [t0b])
                t1, t1b = scratch()
                dve_stt(t1[:], lxbuf[:, ch, 2:TT + 2], wc(2), t0[:], ALU.mult, ALU.add, [lxB[ch], cvB, t0b], [t1b])
                t2, t2b = scratch()
                dve_stt(t2[:], lxbuf[:, ch, 1:TT + 1], wc(1), t1[:], ALU.mult, ALU.add, [lxB[ch], cvB, t1b], [t2b])
                dve_stt(lru_in[:, ch, :], lxbuf[:, ch, 0:TT], wc(0), t2[:], ALU.mult, ALU.add, [lxB[ch], cvB, t2b], [liB[ch]])
                act(lrub[:, ch, :], lru_in[:, ch, :], AF.Copy, [liB[ch]], [lbB[ch]])

        def st_lru_gates():
            pss = []
            for ch in range(2):
                psr, pbr = mmps()
                mm_group(psr[:], pbr, [(gw[:, ch * 128:(ch + 1) * 128], lrub[:, ch, :], [gwb, lbB[ch]])])
                act(trT[ch][:], psr[:], AF.Tanh, [pbr, derB], [trBf[ch]], bias=der[:, d0 + ch:d0 + ch + 1], scale=0.5)
                psi, pbi = mmps()
                mm_group(psi[:], pbi, [(gw[:, (2 + ch) * 128:(3 + ch) * 128], lrub[:, ch, :], [gwb, lbB[ch]])])
                act(tiT[ch][:], psi[:], AF.Tanh, [pbi, derB], [tiBf[ch]], bias=der[:, d0 + 2 + ch:d0 + 3 + ch], scale=0.5)
            for ch in range(2):
                act(aaT[ch][:], trT[ch][:], AF.Exp, [trBf[ch], derB], [aaBf[ch]],
                    bias=der[:, d0 + 4 + ch:d0 + 5 + ch], scale=der[:, d0 + 4 + ch:d0 + 5 + ch])
                a2, a2b = scratch()
                act(a2[:], trT[ch][:], AF.Exp, [trBf[ch], derB], [a2b],
                    bias=der[:, d0 + 6 + ch:d0 + 7 + ch], scale=der[:, d0 + 6 + ch:d0 + 7 + ch])
                dve_ts(wqT[ch][:, 0:TT], a2[:], -0.25, 0.25, ALU.mult, ALU.add, [a2b], [wqBf[ch]])
            for ch in range(2):
                act(wqT[ch][:, 0:TT], wqT[ch][:, 0:TT], AF.Sqrt, [wqBf[ch]], [wqBf[ch]])

        def st_lru_tail(ch):
            u1, u1b = scratch()
            dve_stt(u1[:], tiT[ch][:], 1.0, lru_in[:, ch, :], ALU.add, ALU.mult, [tiBf[ch], liB[ch]], [u1b])
            uu, uub = scratch()
            dve_tt(uu[:], u1[:], wqT[ch][:, 0:TT], ALU.mult, [u1b, wqBf[ch]], [uub])
            hh, hhb = scratch()
            TR.op("dve", lambda e, hh=hh, aa=aaT[ch], uu=uu, ch=ch: e.tensor_tensor_scan(
                out=hh[:], data0=aa[:], data1=uu[:], initial=hstate[l][:, ch:ch + 1], op0=ALU.mult, op1=ALU.add),
                [aaBf[ch], uub, hsB[l][ch]], [hhb])
            dve_tt(mixbuf[:, 6 + ch, :], hh[:], glb[:, ch, :], ALU.mult, [hhb, glB[ch]], mixdeps(6 + ch))
            dve_copy(hstate[l][:, ch:ch + 1], hh[:, TT - 1:TT], [hhb], [hsB[l][ch]])

        def st_side_norms():
            multi_norm([([(mixbuf[:, j, :], mixdeps(j)) for j in (4, 5)], 256, EPS, gn_consumer((4, 5))),
                        ([(mixbuf[:, j, :], mixdeps(j)) for j in (6, 7)], 256, 4 * EPS, gn_consumer((6, 7)))])

        side = [lambda: st_sconv(0), lambda: st_sconv(1), st_lru_conv, st_lru_gates,
                lambda: st_lru_tail(0), lambda: st_lru_tail(1), st_side_norms, None]
        iters = [(c, kv) for c in range(8) for kv in range(2)]
        ctx = {}

        def geom(c):
            return c // 2, (c % 2 == 0), (it * 8 + c >= 2)

        def emit_S(i):
            c, kv = iters[i]
            b, even, prev_valid = geom(c)
            kp = slice(kv * 64, kv * 64 + 64)
            k2 = st["P"] % 2
            k3 = st["P"] % 2
            st["P"] += 1
            Sps, Sb = PS[4 + k2], psB[4 + k2]
            ctx[i] = (Sps, Sb, PS[6 + k2], psB[6 + k2], Pbuf[k3], PB[k3])
            qv = qT[kp, 0:4, 64 * c:64 * c + 64]
            if prev_valid:
                mm_group(Sps[:, 0:256], Sb, [(kT[l][kp, 128 * b:128 * b + 128], qv, [kB[l]] + qB)])
            mcur = 64 if even else 128
            mm_group(Sps[0:mcur, 256:512], Sb, [(kT[l][kp, 128 * (b + 1):128 * (b + 1) + mcur], qv, [kB[l]] + qB)])

        def emit_exp(i):
            c, kv = iters[i]
            b, even, prev_valid = geom(c)
            Sps, Sb, Ops, Ob, P, Pb_ = ctx[i]
            if prev_valid:
                rows = slice(0, 128) if even else slice(64, 128)
                act(P[rows, 0:256], Sps[rows, 0:256], AF.Exp, [Sb], [Pb_], scale=0.125)
            rows = slice(0, 64) if even else slice(0, 128)
            act(P[rows, 256:512], Sps[rows, 256:512], AF.Exp, [Sb], [Pb_], scale=0.125)

        def emit_O(i):
            c, kv = iters[i]
            b, even, prev_valid = geom(c)
            Sps, Sb, Ops, Ob, P, Pb_ = ctx[i]
            items = []
            if prev_valid:
                if even:
                    items.append((vaug[l][:, b, kv, :], P[:, 0:256], [vB[l], Pb_]))
                else:
                    items.append((vaug[l][64:128, b, kv, :], P[64:128, 0:256], [vB[l], Pb_]))
            if even:
                items.append((vaug[l][0:64, b + 1, kv, :], P[0:64, 256:512], [vB[l], Pb_]))
            else:
                items.append((vaug[l][:, b + 1, kv, :], P[:, 256:512], [vB[l], Pb_]))
            so = l * 512 + kv * 256
            items.append((sinkL[0:1, kv, :], sinkRb[0:1, so:so + 256], [sLB, sRB]))
            mm_group(Ops[:, 0:256], Ob, items)

        def emit_evac(i):
            c, kv = iters[i]
            Sps, Sb, Ops, Ob, P, Pb_ = ctx[i]
            orow = slice(0, 64) if kv == 0 else slice(64, 128)
            drow = slice(64, 128) if kv == 0 else slice(0, 64)
            dve_copy(mixbuf[orow, 0:4, 64 * c:64 * c + 64], Ops[orow, 0:256].rearrange("p (g q) -> p g q", g=4),
                     [Ob], [aoB[c]])
            dve_copy(denall[orow, c, :, :], Ops[drow, 0:256].rearrange("p (g q) -> p g q", g=4), [Ob], [denB[c]])

        emit_S(0)
        for i, (c, kv) in enumerate(iters):
            emit_exp(i)
            if i + 1 < len(iters):
                emit_S(i + 1)
            emit_O(i)
            emit_evac(i)
            if kv == 1 and side[c] is not None:
                side[c]()
        dflat = denall[:].rearrange("p c g q -> p (c g q)")
        act(dflat, dflat, AF.Ln, denB, denB)
        act(dflat, dflat, AF.Exp, denB, denB, scale=-1.0)
        for j in range(4):
            mv = mixbuf[:, j, :].rearrange("p (c q) -> p c q", c=8)
            dve_tt(mv, mv, denall[:, :, j, :], ALU.mult, mixdeps(j) + denB, mixdeps(j))
        if l == 0 and it == 0 and seq == 0:
            DBG('mix', mixbuf[:], [128, 8, TT], F32, mixB + aoB)
            DBG('lru_in', lru_in[:], [128, 2, TT], F32, liB)
        pool_copy(kT[l][:, 0:128], kT[l][:, TT:TT + 128], [kB[l]], [kB[l]])
        pool_copy(vaug[l][:, 0, :, :], vaug[l][:, 4, :, :], [vB[l]], [vB[l]])
        multi_norm([([(mixbuf[:, j, :], mixdeps(j)) for j in (0, 1, 2, 3)], 512, EPS, gn_consumer((0, 1, 2, 3)))])
        for n in range(8):
            ps, pb = proj_unit(hT, hB)
            dve_tt(xbuf[:, n, :], ps[:], xbuf[:, n, :], ALU.add, [pb, xB[n]], [xB[n]])
            stats_accum(n)
        st["pre_stats"] = True

    def ffn(l, seq, it, early=None):
        o = l * NCOL
        if l == 0 and it == 0 and seq == 0:
            DBG('x_mix', xbuf[:], [128, 8, TT], F32, xB)
        def cons_h(rs, rsb):
            for c in range(8):
                dve_stt(hT[:, c, :], xbuf[:, c, :], cv[:, o + C_NFFN + c:o + C_NFFN + c + 1], rs[:], ALU.mult, ALU.mult,
                        [xB[c], cvB, rsb], [hB[c]])
        multi_norm([([(xbuf[:, c, :], [xB[c]]) for c in range(8)], D, EPS, cons_h)])
        st["mmset"] = "ffn"
        pending = None

        def finish_pair(jj, G, V):
            sg, sgb = scratch()
            act(sg[:], gvb[G][:], AF.Silu, [gvB[G]], [sgb])
            dve_tt(actT[:, jj, :], sg[:], gvb[V][:], ALU.mult, [sgb, gvB[V]], [actB[jj]])

        for jj in range(22):
            pair = []
            for s in range(2):
                uidx = jj * 2 + s
                ps, pb = proj_unit(hT, hB)
                ub, ubb = ubuf[st["ub"] % 4], ubB[st["ub"] % 4]
                st["ub"] += 1
                wc = lambda j: cv[:, o + C_FCW + j * 44 + uidx:o + C_FCW + j * 44 + uidx + 1]
                pool_copy(ub[:, 0:2], ucarry[l][:, uidx, :], [ucB[l][uidx]], [ubb])
                act(ub[:, 2:TT + 2], ps[:], AF.Copy, [pb], [ubb])
                t0, t0b = scratch()
                act(t0[:], ps[:], AF.Identity, [pb, cvB], [t0b], bias=cv[:, o + C_FCB + uidx:o + C_FCB + uidx + 1], scale=wc(2))
                pool_copy(ucarry[l][:, uidx, :], ub[:, TT:TT + 2], [ubb], [ucB[l][uidx]])
                t1, t1b = scratch()
                dve_stt(t1[:], ub[:, 1:TT + 1], wc(1), t0[:], ALU.mult, ALU.add, [ubb, cvB, t0b], [t1b])
                gi = (jj % 3) * 2 + s
                dve_stt(gvb[gi][:], ub[:, 0:TT], wc(0), t1[:], ALU.mult, ALU.add, [ubb, cvB, t1b], [gvB[gi]])
                pair.append(gi)
                if s == 0 and pending is not None:
                    finish_pair(*pending)
                    pending = None
            pending = (jj, pair[0], pair[1])
        finish_pair(*pending)
        if early is not None:
            nseq, nit, nxb, nxbl = early
            early_next_stats(nxb, nxbl)
            cossin_tables(nseq, nit)
        for m in range(8):
            ps, pb = mmps()
            for gq in range(3):
                w, wb = next_unit()
                nf = 8 if gq < 2 else 6
                for fi in range(nf):
                    f = gq * 8 + fi
                    first = (f == 0)
                    last = (f == 21)
                    TR.op("pe", (lambda e, ps=ps, w=w, fi=fi, f=f, first=first, last=last:
                                 e.matmul(ps[:], lhsT=w[:, fi * 128:(fi + 1) * 128], rhs=actT[:, f, :], start=first, stop=last)),
                          [wb, actB[f]], [pb], inc=(fi == nf - 1))
            dve_tt(xbuf[:, m, :], ps[:], xbuf[:, m, :], ALU.add, [pb, xB[m]], [xB[m]])
            stats_accum(m)
        st["mmset"] = "mix"
        st["pre_stats"] = True

    out_toks = []
    tiles = [(seq, it) for seq in range(NSEQ) for it in range(NT)]

    def load_x(k):
        sq_, it_ = tiles[k]
        xb_, xbl_ = xbufs[k % 2], xBs[k % 2]
        TR.dma(lambda e, sq_=sq_, t0=it_ * TT, xb_=xb_: e.dma_start(
            out=xb_[:], in_=xT[sq_, :, t0:t0 + TT].rearrange("(c p) t -> p c t", p=128)), f"xin{k % 2}", (), xbl_)

    for k, (seq, it) in enumerate(tiles):
        xbuf, xB = xbufs[k % 2], xBs[k % 2]
        has_next = k + 1 < len(tiles)
        if it == 0:
            TR.new_epoch()
            for l in range(L):
                for ch in range(2):
                    pool_memset(prodc[l][:, ch, :], 0.0, [pcB[l][ch]])
                    pool_memset(lxc[l][:, ch, :], 0.0, [lcB[l][ch]])
                    pool_memset(hstate[l][:, ch:ch + 1], 0.0, [hsB[l][ch]])
                pool_memset(ucarry[l][:], 0.0, ucB[l])
        t0 = it * TT
        warm = k >= 2
        can_prep = has_next and k >= 1
        if k == 0:
            load_x(0)
        if not warm:
            cossin_tables(seq, it)
        if can_prep:
            load_x(k + 1)
        for l in range(L):
            mixer(l, seq, it, skip_norm=(l == 0 and warm))
            early = None
            if l == L - 1 and can_prep:
                early = (tiles[k + 1][0], tiles[k + 1][1], xbufs[(k + 1) % 2], xBs[(k + 1) % 2])
            ffn(l, seq, it, early=early)
            if l == 0 and it == 0 and seq == 0:
                DBG('x_ffn', xbuf[:], [128, 8, TT], F32, xB)
                DBG('act', actT[:], [128, 22, TT], BF16, actB)
        o = (L - 1) * NCOL

        def cons_f(rs, rsb, o=o, xb_=xbuf, xbl_=xB):
            for c in range(8):
                dve_stt(mixbuf[:, c, :], xb_[:, c, :], cv[:, o + C_FIN + c:o + C_FIN + c + 1], rs[:], ALU.mult, ALU.mult,
                        [xbl_[c], cvB, rsb], mixdeps(c))

        groups = [([(xbuf[:, c, :], [xB[c]]) for c in range(8)], D, EPS, cons_f)]
        if can_prep:
            def cons_n(rs, rsb, xb_=xbufs[(k + 1) % 2], xbl_=xBs[(k + 1) % 2]):
                for c in range(8):
                    dve_stt(hT[:, c, :], xb_[:, c, :], cv[:, C_NMIX + c:C_NMIX + c + 1], rs[:], ALU.mult, ALU.mult,
                            [xbl_[c], cvB, rsb], [hB[c]])
            groups.append((None, D, EPS, cons_n))
        multi_norm(groups)
        rd_all = []
        for c in range(8):
            rd_all += mixdeps(c)
        tok = TR.dma(lambda e, seq=seq, t0=t0: e.dma_start(
            out=outT[seq, :, t0:t0 + TT].rearrange("(c p) t -> p c t", p=128), in_=mixbuf[:]), "oout", rd_all, ())
        out_toks.append(tok)
        if k == 0 and has_next:
            load_x(1)
    TR.final_wait("sp", [out_toks[-1]])

    sems = {}
    for key in TR.cnt:
        nm = "s_" + "_".join(str(k) for k in key)
        sems[key] = es.enter_context(nc.semaphore(nm))
    block = es.enter_context(nc.Block())

    def run(eng_name, e):
        for waits, fn, inckey, incval in TR.plan[eng_name]:
            for key, val in waits:
                e.wait_ge(sems[key], val)
            if fn is None:
                continue
            ins = fn(e)
            if inckey is not None:
                ins.then_inc(sems[inckey], incval)

    @block.sync
    def _(e):
        run("sp", e)

    @block.tensor
    def _(e):
        run("pe", e)

    @block.scalar
    def _(e):
        run("act", e)

    @block.vector
    def _(e):
        run("dve", e)

    @block.gpsimd
    def _(e):
        run("pool", e)

    es.close()
    stats = {e: len(TR.plan[e]) for e in ENGS}
    stats["sems"] = len(sems)
    stats["maxcnt"] = max(TR.cnt.values())
    stats['dbg'] = list(dbg_outs)
    return nc, stats


def run_cores(inputs, NSEQ, S, n_cores=8, debug=False):
    wst, cvec, sinkr, ropec = _prep_weights(inputs)
    x = np.asarray(inputs["x"], np.float32)
    pos = np.asarray(inputs["positions"], np.int32)
    nc, stats = build_program(NSEQ, S, debug)
    in_maps = []
    for c in range(n_cores):
        xs = x[c * NSEQ:(c + 1) * NSEQ]
        in_maps.append({
            "xT": np.ascontiguousarray(xs.transpose(0, 2, 1)),
            "pos": np.ascontiguousarray(pos[c * NSEQ:(c + 1) * NSEQ]),
            "wst": wst, "cvec": cvec, "ropec": ropec, "sinkr": sinkr, "ident": np.eye(128, dtype=np.float32),
        })
    res = run_bass_kernel_spmd(nc, in_maps, core_ids=list(range(n_cores)))
    outs = [np.asarray(r["outT"]).transpose(0, 2, 1) for r in res.results]
    if debug:
        stats['dbgvals'] = [{n: np.asarray(r['dbg_' + n]) for n in stats['dbg']} for r in res.results]
    return np.ascontiguousarray(np.concatenate(outs, axis=0)).astype(np.float32), stats


def kernel(**inputs):
    out, _ = run_cores(inputs, NSEQ=4, S=2048)
    return out
```

```python
import math
from contextlib import ExitStack

import numpy as np
import concourse.bass as bass
import concourse.mybir as mybir
from concourse.bass_utils import run_bass_kernel_spmd

F32 = mybir.dt.float32
BF16 = mybir.dt.bfloat16
I32 = mybir.dt.int32
AF = mybir.ActivationFunctionType
ALU = mybir.AluOpType

D = 1024
L = 2
TT = 512
NU = 98
NCOL = 230
NSLOT = 8
NSC = 10
EPS = 1e-6
ENGS = ("sp", "pe", "act", "dve", "pool")

C_NMIX, C_NFFN, C_GN, C_SCW, C_LCW, C_LCB, C_BA, C_BX, C_LAM, C_FCW, C_FCB, C_FIN = 0, 8, 16, 24, 30, 38, 40, 42, 44, 46, 178, 222


def _unit_from_cols(W, cols):
    cols = np.asarray(cols)
    sel = np.zeros((W.shape[0], 128), np.float32)
    ok = cols >= 0
    sel[:, ok] = W[:, cols[ok]]
    kc = W.shape[0] // 128
    u = np.zeros((128, 8, 128), np.float32)
    u[:, :kc, :] = sel.reshape(kc, 128, 128).transpose(1, 0, 2)
    return u.reshape(128, 1024)


def _mix_perm():
    perm = np.zeros(1024, np.int64)
    for j in range(4):
        for p in range(128):
            perm[j * 128 + p] = j * 64 + p if p < 64 else (j + 4) * 64 + (p - 64)
    perm[512:] = np.arange(512, 1024)
    return perm


def _prep_weights(inp):
    perm = _mix_perm()
    wst = np.zeros((L * NU, 128, 1024), np.float32)
    cvec = np.zeros((128, L * NCOL), np.float32)
    sinkr = np.zeros((1, L * 512), np.float32)
    for l in range(L):
        Win = np.asarray(inp["w_in"][l], np.float32)
        units = []

        def head_cols(base, h):
            return [base + h * 64 + d for d in range(64)]

        def swap_cols(base, h):
            out = []
            for d in range(64):
                if d < 8:
                    out.append(base + h * 64 + d + 8)
                elif d < 16:
                    out.append(base + h * 64 + d - 8)
                else:
                    out.append(-1)
            return out

        for j in range(4):
            units.append(_unit_from_cols(Win, head_cols(0, j) + head_cols(0, j + 4)))
            units.append(_unit_from_cols(Win, swap_cols(0, j) + swap_cols(0, j + 4)))
        units.append(_unit_from_cols(Win, head_cols(512, 0) + head_cols(512, 1)))
        units.append(_unit_from_cols(Win, swap_cols(512, 0) + swap_cols(512, 1)))
        units.append(_unit_from_cols(Win, list(range(640, 768))))
        for ch in range(2):
            units.append(_unit_from_cols(Win, list(range(768 + ch * 128, 768 + (ch + 1) * 128))))
        for ch in range(2):
            units.append(_unit_from_cols(Win, list(range(1280 + ch * 128, 1280 + (ch + 1) * 128))))
            units.append(_unit_from_cols(Win, list(range(1024 + ch * 128, 1024 + (ch + 1) * 128))))
        for ch in range(2):
            units.append(_unit_from_cols(Win, list(range(1536 + ch * 128, 1536 + (ch + 1) * 128))))
        for ch in range(2):
            units.append(_unit_from_cols(Win, list(range(1792 + ch * 128, 1792 + (ch + 1) * 128))))
        g = np.zeros((128, 1024), np.float32)
        for gi, W in enumerate((inp["lru_wa"][l], inp["lru_wx"][l])):
            W = np.asarray(W, np.float32)
            for ch in range(2):
                col0 = (gi * 2 + ch) * 128
                g[0:64, col0:col0 + 64] = W[2 * ch]
                g[64:128, col0 + 64:col0 + 128] = W[2 * ch + 1]
        units.append(g)
        Wo = np.asarray(inp["w_out"][l], np.float32)[perm, :]
        for n in range(8):
            units.append(_unit_from_cols(Wo, list(range(n * 128, (n + 1) * 128))))
        Wu = np.asarray(inp["ffn_w_up"][l], np.float32)
        for jj in range(22):
            for s in range(2):
                units.append(_unit_from_cols(Wu, list(range(s * 2816 + jj * 128, s * 2816 + (jj + 1) * 128))))
        Wd = np.asarray(inp["ffn_w_down"][l], np.float32)
        for m in range(8):
            for gq in range(3):
                u = np.zeros((128, 8, 128), np.float32)
                for fi in range(8):
                    f = gq * 8 + fi
                    if f < 22:
                        u[:, fi, :] = Wd[f * 128:(f + 1) * 128, m * 128:(m + 1) * 128]
                units.append(u.reshape(128, 1024))
        assert len(units) == NU
        wst[l * NU:(l + 1) * NU] = np.stack(units)

        cv = cvec[:, l * NCOL:(l + 1) * NCOL]

        def colmat(v):
            return np.asarray(v, np.float32).reshape(-1, 128).T

        cv[:, C_NMIX:C_NMIX + 8] = colmat(inp["norm_mix_g"][l])
        cv[:, C_NFFN:C_NFFN + 8] = colmat(inp["norm_ffn_g"][l])
        gn = np.concatenate([np.asarray(inp["gn_attn_g"][l]), np.asarray(inp["gn_sconv_g"][l]),
                             np.asarray(inp["gn_lru_g"][l])]).astype(np.float32)[perm]
        cv[:, C_GN:C_GN + 8] = colmat(gn)
        for j in range(3):
            cv[:, C_SCW + j * 2:C_SCW + j * 2 + 2] = colmat(inp["sconv_w"][l][j])
        for j in range(4):
            cv[:, C_LCW + j * 2:C_LCW + j * 2 + 2] = colmat(inp["lru_conv_w"][l][j])
        cv[:, C_LCB:C_LCB + 2] = colmat(inp["lru_conv_b"][l])
        cv[:, C_BA:C_BA + 2] = colmat(np.asarray(inp["lru_ba"][l]).reshape(-1))
        cv[:, C_BX:C_BX + 2] = colmat(np.asarray(inp["lru_bx"][l]).reshape(-1))
        cv[:, C_LAM:C_LAM + 2] = colmat(inp["lru_lambda"][l])
        fperm = np.zeros(5632, np.int64)
        for uidx in range(44):
            jj, s = uidx // 2, uidx % 2
            fperm[uidx * 128:(uidx + 1) * 128] = s * 2816 + jj * 128 + np.arange(128)
        for j in range(3):
            cv[:, C_FCW + j * 44:C_FCW + (j + 1) * 44] = colmat(np.asarray(inp["ffn_conv_w"][l][j])[fperm])
        cv[:, C_FCB:C_FCB + 44] = colmat(np.asarray(inp["ffn_conv_b"][l])[fperm])
        cv[:, C_FIN:C_FIN + 8] = colmat(inp["final_norm_g"])
        sk = np.asarray(inp["attn_sinks"][l], np.float32)
        for kv in range(2):
            for gq in range(4):
                o = l * 512 + kv * 256 + gq * 64
                sinkr[0, o:o + 64] = sk[kv * 4 + gq]
    ropec = np.zeros((128, 2), np.float32)
    ropec[:, 1] = 1.0
    for p in range(128):
        d = p % 64
        if d < 16:
            ropec[p, 0] = np.float32(500000.0) ** np.float32(-(2 * (d % 8)) / 16.0)
            ropec[p, 1] = -1.0 if d < 8 else 1.0
    return wst, cvec, sinkr, ropec


class Buf:
    __slots__ = ("name", "w", "r")

    def __init__(self, name):
        self.name = name
        self.w = None
        self.r = {}


class Tracker:
    def __init__(self):
        self.plan = {e: [] for e in ENGS}
        self.cnt = {}
        self.seen = {e: {} for e in ENGS}
        self.epoch = 0

    def new_epoch(self):
        self.epoch += 1

    def _waits(self, eng, reads, writes):
        toks = []
        for b in reads:
            if b.w is not None:
                toks.append(b.w)
        for b in writes:
            if b.w is not None:
                toks.append(b.w)
            toks.extend(b.r.values())
        waits = []
        seen = self.seen[eng]
        for key, val in toks:
            if eng == "pe" and key[0] == "pe":
                continue
            if seen.get(key, 0) < val:
                seen[key] = val
                waits.append((key, val))
        best = {}
        for key, val in waits:
            best[key] = max(best.get(key, 0), val)
        return list(best.items())

    def _mark(self, rkey, tok, reads, writes):
        for b in reads:
            b.r[rkey] = tok
        for b in writes:
            b.w = tok
            b.r = {}

    def op(self, eng, fn, reads=(), writes=(), inc=True):
        waits = self._waits(eng, reads, writes)
        key = (eng, self.epoch)
        if inc:
            self.cnt[key] = self.cnt.get(key, 0) + 1
            tok = (key, self.cnt[key])
        else:
            tok = (key, self.cnt.get(key, 0) + 1)
        self.plan[eng].append((waits, fn, key if inc else None, 1))
        self._mark(eng, tok, reads, writes)
        return tok

    def dma(self, fn, semname, reads=(), writes=(), queue="sp"):
        waits = self._waits(queue, reads, writes)
        key = ("dma", semname)
        prev = self.cnt.get(key, 0)
        if prev and self.seen[queue].get(key, 0) < prev:
            self.seen[queue][key] = prev
            waits.append((key, prev))
        self.cnt[key] = prev + 16
        tok = (key, prev + 16)
        self.plan[queue].append((waits, fn, key, 16))
        self._mark(key, tok, reads, writes)
        return tok

    def final_wait(self, queue, toks):
        self.plan[queue].append((list(toks), None, None, 0))


def build_program(NSEQ, S, debug=False):
    NT = S // TT
    dbg_outs = {}
    nc = bass.Bass("TRN2", target_bir_lowering=False)
    xT = nc.dram_tensor("xT", [NSEQ, D, S], F32, kind="ExternalInput").ap()
    posd = nc.dram_tensor("pos", [NSEQ, S], I32, kind="ExternalInput").ap()
    wst = nc.dram_tensor("wst", [L * NU, 128, 1024], F32, kind="ExternalInput").ap()
    cvd = nc.dram_tensor("cvec", [128, L * NCOL], F32, kind="ExternalInput").ap()
    roped = nc.dram_tensor("ropec", [128, 2], F32, kind="ExternalInput").ap()
    sinkd = nc.dram_tensor("sinkr", [1, L * 512], F32, kind="ExternalInput").ap()
    identd = nc.dram_tensor("ident", [128, 128], F32, kind="ExternalInput").ap()
    outT = nc.dram_tensor("outT", [NSEQ, D, S], F32, kind="ExternalOutput").ap()
    wbf = nc.dram_tensor("wbf", [L * NU, 128, 1024], BF16, kind="Internal").ap()

    TR = Tracker()
    es = ExitStack()

    def sb(name, shape, dt):
        return es.enter_context(nc.sbuf_tensor(name, shape, dt))

    def bufs(name, n):
        return [Buf(f"{name}{i}") for i in range(n)]

    xbuf = sb("xbuf", [128, 8, TT], F32); xB = bufs("x", 8)
    xbuf_b = sb("xbuf_b", [128, 8, TT], F32); xB_b = bufs("xb_", 8)
    xbufs = [xbuf, xbuf_b]; xBs = [xB, xB_b]
    hT = sb("hT", [128, 8, TT], BF16); hB = bufs("h", 8)
    sqb = sb("sqb", [128, 8, TT], BF16); sqB = bufs("sq", 8)
    qT = sb("qT", [128, 4, TT], BF16); qB = bufs("q", 4)
    kT = [sb(f"kT{l}", [128, 128 + TT], BF16) for l in range(L)]; kB = bufs("k", L)
    vaug = [sb(f"vaug{l}", [128, 5, 2, 128], BF16) for l in range(L)]; vB = bufs("v", L)
    cbs = sb("cbs", [128, 2, TT], F32); cbB = bufs("cb", 2)
    prodbuf = sb("prodbuf", [128, 2, TT + 2], F32); prB = bufs("pr", 2)
    prodc = [sb(f"prodc{l}", [128, 2, 2], F32) for l in range(L)]; pcB = [bufs(f"pc{l}_", 2) for l in range(L)]
    lxbuf = sb("lxbuf", [128, 2, TT + 3], F32); lxB = bufs("lx", 2)
    lxc = [sb(f"lxc{l}", [128, 2, 3], F32) for l in range(L)]; lcB = [bufs(f"lc{l}_", 2) for l in range(L)]
    mixbuf = sb("mixbuf", [128, 8, TT], F32); mixB = bufs("mix", 8); aoB = bufs("ao", 8)
    SC = [sb(f"sc{i}", [128, TT], F32) for i in range(NSC)]; scB = bufs("sc", NSC)
    lru_in = sb("lru_in", [128, 2, TT], F32); liB = bufs("li", 2)
    lrub = sb("lrub", [128, 2, TT], BF16); lbB = bufs("lb", 2)
    glb = sb("glb", [128, 2, TT], F32); glB = bufs("gl", 2)
    hstate = [sb(f"hstate{l}", [128, 2], F32) for l in range(L)]; hsB = [bufs(f"hs{l}_", 2) for l in range(L)]
    Pbuf = [sb(f"P{i}", [128, 512], BF16) for i in range(2)]; PB = bufs("P", 2)
    denall = sb("denall", [128, 8, 4, 64], F32); denB = bufs("den", 8)
    cosT = sb("cosT", [128, TT], F32); cosB = Buf("cos")
    sinT = sb("sinT", [128, TT], F32); sinB = Buf("sin")
    posi = sb("posi", [128, TT], I32); posB = Buf("posi")
    ubuf = [sb(f"ub{i}", [128, TT + 2], F32) for i in range(4)]; ubB = bufs("ub", 4)
    ucarry = [sb(f"ucarry{l}", [128, 44, 2], F32) for l in range(L)]; ucB = [bufs(f"uc{l}_", 44) for l in range(L)]
    gvb = [sb(f"gv{i}", [128, TT], F32) for i in range(6)]; gvB = bufs("gv", 6)
    actT = sb("actT", [128, 22, TT], BF16); actB = bufs("act", 22)
    wring = [sb(f"wr{i}", [128, 1024], BF16) for i in range(NSLOT)]; wrB = bufs("wr", NSLOT)
    cv = sb("cv", [128, L * NCOL], F32); cvB = Buf("cv")
    ropec = sb("ropecs", [128, 2], F32); ropeB = Buf("ropec")
    der = sb("der", [128, L * 8], F32); derB = Buf("der")
    dtmp = sb("dtmp", [128, 8], F32); dtB = Buf("dtmp")
    sink32 = sb("sink32", [1, L * 512], F32); s32B = Buf("sink32")
    sinkRb = sb("sinkRb", [1, L * 512], BF16); sRB = Buf("sinkRb")
    sinkL = sb("sinkL", [1, 2, 128], BF16); sLB = Buf("sinkL")
    ones = sb("ones", [128, 128], BF16); onesB = Buf("ones")
    cpow = sb("cpow", [128, 4], F32); cpB = Buf("cpow")
    ident = sb("ident_sb", [128, 128], F32); identB = Buf("ident")
    onesf = sb("onesf", [128, 128], F32); onesfB = Buf("onesf")
    stat = [sb(f"stat{i}", [128, 8], F32) for i in range(2)]; statB = bufs("stat", 2)
    dg = [sb(f"dg{i}", [128, 4, 128], F32) for i in range(2)]; dgB = bufs("dg", 2)
    PS = [es.enter_context(nc.psum_tensor(f"ps{i}", [128, 512], F32)) for i in range(8)]
    psB = bufs("ps", 8)
    wbfB = bufs("wbf", L * NU)

    st = {"sc": 0, "mm": 0, "P": 0, "ub": 0, "stat": 0}

    def DBG(name, tile, shape, dt, bl):
        if not debug or name in dbg_outs:
            return
        d = nc.dram_tensor("dbg_" + name, list(shape), dt, kind="ExternalOutput").ap()
        dbg_outs[name] = d
        TR.dma(lambda e: e.dma_start(out=d, in_=tile), "dbg_" + name, bl, ())

    def scratch():
        i = st["sc"] % NSC
        st["sc"] += 1
        return SC[i], scB[i]

    MM_SETS = {"mix": [0, 1, 2], "ffn": [0, 1, 2, 4, 5, 6, 7]}
    st["mmset"] = "mix"

    def mmps():
        bs = MM_SETS[st["mmset"]]
        i = bs[st["mm"] % len(bs)]
        st["mm"] += 1
        return PS[i], psB[i]

    PSN, PSNB = PS[3], psB[3]

    def mixdeps(j):
        return [mixB[j]] + (aoB if j < 4 else [])

    def act(out, in_, func, reads, writes, bias=None, scale=None):
        kw = {}
        if bias is not None:
            kw["bias"] = bias
        if scale is not None:
            kw["scale"] = scale
        TR.op("act", lambda e: e.activation(out=out, in_=in_, func=func, **kw), reads, writes)

    def dve_tt(out, in0, in1, op, reads, writes):
        TR.op("dve", lambda e: e.tensor_tensor(out=out, in0=in0, in1=in1, op=op), reads, writes)

    def dve_stt(out, in0, scalar, in1, op0, op1, reads, writes):
        TR.op("dve", lambda e: e.scalar_tensor_tensor(out=out, in0=in0, scalar=scalar, in1=in1, op0=op0, op1=op1),
              reads, writes)

    def dve_ts(out, in0, s1, s2, op0, op1, reads, writes):
        if s2 is None:
            TR.op("dve", lambda e: e.tensor_scalar(out=out, in0=in0, scalar1=s1, scalar2=None, op0=op0), reads, writes)
        else:
            TR.op("dve", lambda e: e.tensor_scalar(out=out, in0=in0, scalar1=s1, scalar2=s2, op0=op0, op1=op1),
                  reads, writes)

    def dve_copy(out, in_, reads, writes):
        TR.op("dve", lambda e: e.tensor_copy(out=out, in_=in_), reads, writes)

    def pool_tt(out, in0, in1, op, reads, writes):
        TR.op("pool", lambda e: e.tensor_tensor(out=out, in0=in0, in1=in1, op=op), reads, writes)

    def pool_ts(out, in0, s1, s2, op0, op1, reads, writes):
        if s2 is None:
            TR.op("pool", lambda e: e.tensor_scalar(out=out, in0=in0, scalar1=s1, scalar2=None, op0=op0), reads, writes)
        else:
            TR.op("pool", lambda e: e.tensor_scalar(out=out, in0=in0, scalar1=s1, scalar2=s2, op0=op0, op1=op1),
                  reads, writes)

    def pool_copy(out, in_, reads, writes):
        TR.op("pool", lambda e: e.tensor_copy(out=out, in_=in_), reads, writes)

    def pool_memset(ap, val, writes):
        TR.op("pool", lambda e: e.memset(ap, val), (), writes)

    def mm_group(out, outB, items, extra_reads=()):
        n = len(items)
        for i, (lhsT, rhs, rb) in enumerate(items):
            last = i == n - 1
            TR.op("pe", (lambda e, lhsT=lhsT, rhs=rhs, i=i, last=last:
                         e.matmul(out, lhsT=lhsT, rhs=rhs, start=(i == 0), stop=last)),
                  list(rb) + list(extra_reads), [outB], inc=last)

    total_units = NSEQ * NT * L * NU
    wstate = {"issued": 0, "consumed": 0}

    def issue_load(n):
        u = n % (L * NU)
        s = n % NSLOT
        TR.dma(lambda e, u=u, s=s: e.dma_start(out=wring[s][:], in_=wbf[u]), f"w{s}", [wbfB[u]], [wrB[s]])

    def next_unit():
        n = wstate["consumed"]
        while wstate["issued"] < min(total_units, n + NSLOT):
            issue_load(wstate["issued"])
            wstate["issued"] += 1
        wstate["consumed"] += 1
        s = n % NSLOT
        return wring[s], wrB[s]

    def proj_unit(rhs_tile, rhs_bufs):
        w, wb = next_unit()
        ps, pb = mmps()
        mm_group(ps[:], pb, [(w[:, kc * 128:(kc + 1) * 128], rhs_tile[:, kc, :], [wb, rhs_bufs[kc]]) for kc in range(8)])
        return ps, pb

    TR.dma(lambda e: e.dma_start(out=cv[:], in_=cvd[:, :]), "c0", (), [cvB])
    TR.dma(lambda e: e.dma_start(out=ropec[:], in_=roped[:, :]), "c1", (), [ropeB])
    TR.dma(lambda e: e.dma_start(out=sink32[:], in_=sinkd[:, :]), "c2", (), [s32B])
    pool_memset(ones[:], 1.0, [onesB])
    pool_memset(cpow[:], -0.5, [cpB])
    pool_memset(onesf[:], 1.0, [onesfB])
    TR.dma(lambda e: e.dma_start(out=ident[:], in_=identd[:, :]), "c3", (), [identB])
    pool_memset(sinkL[:, 0, 0:64], 0.0, [sLB]); pool_memset(sinkL[:, 0, 64:128], 1.0, [sLB])
    pool_memset(sinkL[:, 1, 0:64], 1.0, [sLB]); pool_memset(sinkL[:, 1, 64:128], 0.0, [sLB])
    for l in range(L):
        pool_memset(vaug[l][:], 1.0, [vB[l]])
    act(sinkRb[:], sink32[:], AF.Exp, [s32B], [sRB])
    for l in range(L):
        o = l * NCOL
        d0 = l * 8
        dve_ts(der[:, d0:d0 + 2], cv[:, o + C_BA:o + C_BA + 2], 0.5, None, ALU.mult, None, [cvB], [derB])
        dve_ts(der[:, d0 + 2:d0 + 4], cv[:, o + C_BX:o + C_BX + 2], 0.5, None, ALU.mult, None, [cvB], [derB])
        act(dtmp[:, 0:2], cv[:, o + C_LAM:o + C_LAM + 2], AF.Exp, [cvB], [dtB], scale=-1.0)
        dve_ts(dtmp[:, 2:4], dtmp[:, 0:2], -0.25, 1.0 / 3.0, ALU.mult, ALU.add, [dtB], [dtB])
        dve_tt(dtmp[:, 4:6], dtmp[:, 2:4], dtmp[:, 0:2], ALU.mult, [dtB], [dtB])
        dve_ts(dtmp[:, 2:4], dtmp[:, 4:6], -1.0, 0.5, ALU.mult, ALU.add, [dtB], [dtB])
        dve_tt(dtmp[:, 4:6], dtmp[:, 2:4], dtmp[:, 0:2], ALU.mult, [dtB], [dtB])
        dve_ts(dtmp[:, 2:4], dtmp[:, 4:6], -1.0, 1.0, ALU.mult, ALU.add, [dtB], [dtB])
        dve_tt(dtmp[:, 4:6], dtmp[:, 2:4], dtmp[:, 0:2], ALU.mult, [dtB], [dtB])
        dve_ts(der[:, d0 + 6:d0 + 8], dtmp[:, 4:6], -8.0, None, ALU.mult, None, [dtB], [derB])
        dve_ts(der[:, d0 + 4:d0 + 6], dtmp[:, 4:6], -4.0, None, ALU.mult, None, [dtB], [derB])

    def st32_slot(i):
        t, tb = (mixbuf, lambda j: mixdeps(j)) if i < 4 else (xbuf, lambda j: [xB[j]])
        k = i % 4
        return t[:, 2 * k:2 * k + 2, :], tb(2 * k) + tb(2 * k + 1)

    def st16_slot(i):
        t, tb = (sqb, sqB) if i < 4 else (hT, hB)
        k = i % 4
        return t[:, 2 * k:2 * k + 2, :], [tb[2 * k], tb[2 * k + 1]]

    NPU = L * NU
    cast_eng = ["act", "dve", "act", "dve", "act", "pool", "dve", "act"]

    def pre_load(u):
        a32, b32 = st32_slot(u % 8)
        TR.dma(lambda e, u=u, a32=a32: e.dma_start(out=a32, in_=wst[u].rearrange("p (a t) -> p a t", a=2)),
               f"pin{u % 8}", (), b32)

    for u in range(min(8, NPU)):
        pre_load(u)
    for u in range(NPU):
        a32, b32 = st32_slot(u % 8)
        a16, b16 = st16_slot(u % 8)
        ce = cast_eng[u % 8]
        if ce == "act":
            act(a16, a32, AF.Copy, b32, b16)
        elif ce == "dve":
            dve_copy(a16, a32, b32, b16)
        else:
            pool_copy(a16, a32, b32, b16)
        TR.dma(lambda e, u=u, a16=a16: e.dma_start(out=wbf[u].rearrange("p (a t) -> p a t", a=2), in_=a16),
               f"pout{u % 8}", b16, [wbfB[u]])
        if u + 8 < NPU:
            pre_load(u + 8)

    PSNS = [Buf(f"psn{i}") for i in range(4)]

    accx = sb("accx", [128, 4], F32); accB = Buf("accx")

    def stats_square(n):
        act(sqb[:, n, :], xbuf[:, n, :], AF.Square, [xB[n]], [sqB[n]])

    def stats_reduce(n):
        for blk in range(4):
            TR.op("pe", (lambda e, n=n, blk=blk: e.matmul(PSN[:, 12 + blk:13 + blk], lhsT=sqb[:, n, blk * 128:(blk + 1) * 128],
                                                         rhs=ones[:, 0:1], start=True, stop=True)),
                  [sqB[n], onesB], [PSNS[3]], inc=(blk == 3))
        if n == 0:
            dve_copy(accx[:], PSN[:, 12:16], [PSNS[3]], [accB])
        else:
            dve_tt(accx[:], PSN[:, 12:16], accx[:], ALU.add, [PSNS[3], accB], [accB])

    def stats_accum(n, lag=2):
        stats_square(n)
        if n >= lag:
            stats_reduce(n - lag)
        if n == 7:
            for j in range(8 - lag, 8):
                stats_reduce(j)

    def multi_norm(groups):
        base = 0
        infos = []
        pre = st.pop("pre_stats", False)
        modes = []
        for gi, (src_list, nfeat, eps, consume) in enumerate(groups):
            if pre and gi == 0:
                modes.append("acc")
                continue
            if src_list is None:
                modes.append("early")
                continue
            modes.append("now")
            n = len(src_list)
            for i, (ap, bl) in enumerate(src_list):
                act(sqb[:, base + i, :], ap, AF.Square, bl, [sqB[base + i]])
            infos.append((gi, base, n))
            base += n
        for gi, b0, n in infos:
            for blk in range(4):
                mm_group(PSN[:, gi * 4 + blk:gi * 4 + blk + 1], PSNS[gi],
                         [(sqb[:, b0 + i, blk * 128:(blk + 1) * 128], ones[:, 0:1], [sqB[b0 + i], onesB]) for i in range(n)])
        ks = []
        for gi, (src_list, nfeat, eps, consume) in enumerate(groups):
            k = st["stat"] % 2
            st["stat"] += 1
            ks.append(k)
            if modes[gi] == "acc":
                dve_ts(stat[k][:, 0:4], accx[:], 1.0 / nfeat, eps, ALU.mult, ALU.add, [accB], [statB[k]])
            elif modes[gi] == "early":
                dve_ts(stat[k][:, 0:4], PSN[:, 8:12], 1.0 / nfeat, eps, ALU.mult, ALU.add, [PSNS[2]], [statB[k]])
            else:
                dve_ts(stat[k][:, 0:4], PSN[:, gi * 4:gi * 4 + 4], 1.0 / nfeat, eps, ALU.mult, ALU.add, [PSNS[gi]], [statB[k]])
        for k in ks:
            pool_tt(stat[k][:, 4:8], stat[k][:, 0:4], cpow[:, 0:4], ALU.pow, [statB[k], cpB], [statB[k]])
        for k in ks:
            for blk in range(4):
                dve_ts(dg[k][:, blk, :], ident[:], stat[k][:, 4 + blk:5 + blk], None, ALU.mult, None, [identB, statB[k]], [dgB[k]])
        outs = []
        for k in ks:
            ps, pb = mmps()
            mm_group(ps[:], pb, [(onesf[:], dg[k][:].rearrange("p b t -> p (b t)"), [onesfB, dgB[k]])])
            outs.append((ps, pb))
        for (src_list, nfeat, eps, consume), (ps, pb) in zip(groups, outs):
            consume(ps, pb)

    def early_next_stats(xb, xbl):
        for c in range(8):
            act(sqb[:, c, :], xb[:, c, :], AF.Square, [xbl[c]], [sqB[c]])
        for blk in range(4):
            mm_group(PSN[:, 8 + blk:9 + blk], PSNS[2],
                     [(sqb[:, c, blk * 128:(blk + 1) * 128], ones[:, 0:1], [sqB[c], onesB]) for c in range(8)])

    def cossin_tables(seq, it):
        t0 = it * TT
        TR.dma(lambda e: e.dma_start(out=posi[:], in_=posd[seq:seq + 1, t0:t0 + TT].partition_broadcast(128)),
               "pos", (), [posB])
        pf, pfb = scratch()
        dve_copy(pf[:], posi[:], [posB], [pfb])
        ang, angb = scratch()
        dve_ts(ang[:], pf[:], ropec[:, 0:1], None, ALU.mult, None, [pfb, ropeB], [angb])
        dve_ts(posi[:], ang[:], 1.0 / (2 * math.pi), None, ALU.mult, None, [angb], [posB])
        kf, kfb = scratch()
        dve_copy(kf[:], posi[:], [posB], [kfb])
        C1 = 6.28125
        C2 = 2 * math.pi - C1
        r1, r1b = scratch()
        dve_stt(r1[:], kf[:], -C1, ang[:], ALU.mult, ALU.add, [kfb, angb], [r1b])
        r, rb = scratch()
        dve_stt(r[:], kf[:], -C2, r1[:], ALU.mult, ALU.add, [kfb, r1b], [rb])
        m, mb = scratch()
        dve_ts(m[:], r[:], math.pi, -2 * math.pi, ALU.is_gt, ALU.mult, [rb], [mb])
        r2, r2b = scratch()
        dve_tt(r2[:], r[:], m[:], ALU.add, [rb, mb], [r2b])
        m2, m2b = scratch()
        dve_ts(m2[:], r2[:], -math.pi, 2 * math.pi, ALU.is_lt, ALU.mult, [r2b], [m2b])
        rs_, rsb = scratch()
        dve_tt(rs_[:], r2[:], m2[:], ALU.add, [r2b, m2b], [rsb])
        m3, m3b = scratch()
        dve_ts(m3[:], rs_[:], math.pi / 2, -2 * math.pi, ALU.is_gt, ALU.mult, [rsb], [m3b])
        rc, rcb = scratch()
        dve_stt(rc[:], rs_[:], math.pi / 2, m3[:], ALU.add, ALU.add, [rsb, m3b], [rcb])
        PI_S = 3.1415925
        rs2, rs2b = scratch()
        dve_ts(rs2[:], rs_[:], -PI_S, PI_S, ALU.max, ALU.min, [rsb], [rs2b])
        rc2, rc2b = scratch()
        dve_ts(rc2[:], rc[:], -PI_S, PI_S, ALU.max, ALU.min, [rcb], [rc2b])
        act(sinT[:], rs2[:], AF.Sin, [rs2b, ropeB], [sinB], scale=ropec[:, 1:2])
        act(cosT[:], rc2[:], AF.Sin, [rc2b], [cosB])

    def mixer(l, seq, it, skip_norm=False):
        o = l * NCOL
        d0 = l * 8
        def cons_a(rs, rsb):
            for c in range(8):
                dve_stt(hT[:, c, :], xbuf[:, c, :], cv[:, o + C_NMIX + c:o + C_NMIX + c + 1], rs[:], ALU.mult, ALU.mult,
                        [xB[c], cvB, rsb], [hB[c]])
        if not skip_norm:
            multi_norm([([(xbuf[:, c, :], [xB[c]]) for c in range(8)], D, EPS, cons_a)])
        if l == 0 and it == 0 and seq == 0:
            DBG('h0', hT[:], [128, 8, TT], BF16, hB)
        for j in range(5):
            psa, pba = proj_unit(hT, hB)
            psb_, pbb = proj_unit(hT, hB)
            t1, t1b = scratch()
            dve_tt(t1[:], psa[:], cosT[:], ALU.mult, [pba, cosB], [t1b])
            t2, t2b = scratch()
            dve_tt(t2[:], psb_[:], sinT[:], ALU.mult, [pbb, sinB], [t2b])
            if j < 4:
                pool_tt(qT[:, j, :], t1[:], t2[:], ALU.add, [t1b, t2b], [qB[j]])
            else:
                pool_tt(kT[l][:, 128:128 + TT], t1[:], t2[:], ALU.add, [t1b, t2b], [kB[l]])
        w, wb = next_unit()
        ps, pb = mmps()
        for blk in range(4):
            mm_group(ps[:, blk * 128:(blk + 1) * 128], pb,
                     [(hT[:, kc, blk * 128:(blk + 1) * 128], w[:, kc * 128:(kc + 1) * 128], [wb, hB[kc]]) for kc in range(8)])
        psv = ps[:].rearrange("p (b k d) -> p b k d", b=4, k=2, d=64)
        act(vaug[l][:, 1:5, 0, 0:64], psv[:, :, 0, :], AF.Copy, [pb], [vB[l]])
        act(vaug[l][:, 1:5, 1, 64:128], psv[:, :, 1, :], AF.Copy, [pb], [vB[l]])
        for ch in range(2):
            ps, pb = proj_unit(hT, hB)
            act(cbs[:, ch, :], ps[:], AF.Copy, [pb], [cbB[ch]])
        for ch in range(2):
            ps, pb = proj_unit(hT, hB)
            cx, cxb = scratch()
            act(cx[:], ps[:], AF.Copy, [pb], [cxb])
            ps2, pb2 = proj_unit(hT, hB)
            pool_copy(prodbuf[:, ch, 0:2], prodc[l][:, ch, :], [pcB[l][ch]], [prB[ch]])
            dve_tt(prodbuf[:, ch, 2:TT + 2], ps2[:], cx[:], ALU.mult, [pb2, cxb], [prB[ch]])
            pool_copy(prodc[l][:, ch, :], prodbuf[:, ch, TT:TT + 2], [prB[ch]], [pcB[l][ch]])
        for ch in range(2):
            ps, pb = proj_unit(hT, hB)
            pool_copy(lxbuf[:, ch, 0:3], lxc[l][:, ch, :], [lcB[l][ch]], [lxB[ch]])
            act(lxbuf[:, ch, 3:TT + 3], ps[:], AF.Copy, [pb], [lxB[ch]])
            pool_copy(lxc[l][:, ch, :], lxbuf[:, ch, TT:TT + 3], [lxB[ch]], [lcB[l][ch]])
        GK = 0.7978845608028654 * 0.044715
        for ch in range(2):
            ps, pb = proj_unit(hT, hB)
            x2, x2b = scratch()
            act(x2[:], ps[:], AF.Square, [pb], [x2b])
            lgs, lgb = scratch()
            act(lgs[:], ps[:], AF.Copy, [pb], [lgb])
            zp, zpb = scratch()
            dve_stt(zp[:], x2[:], 1.0 / 0.044715, lgs[:], ALU.add, ALU.mult, [x2b, lgb], [zpb])
            th, thb = scratch()
            act(th[:], zp[:], AF.Tanh, [zpb], [thb], scale=GK)
            dve_stt(glb[:, ch, :], th[:], 1.0, lgs[:], ALU.add, ALU.mult, [thb, lgb], [glB[ch]])
        gw, gwb = next_unit()
        if l == 0 and it == 0 and seq == 0:
            DBG('qT', qT[:], [128, 4, TT], BF16, qB)
            DBG('kT', kT[0][:], [128, 128 + TT], BF16, [kB[0]])
            DBG('vaug', vaug[0][:], [128, 5, 2, 128], BF16, [vB[0]])
            DBG('cbs', cbs[:], [128, 2, TT], F32, cbB)
            DBG('prod', prodbuf[:], [128, 2, TT + 2], F32, prB)
            DBG('lxb', lxbuf[:], [128, 2, TT + 3], F32, lxB)
            DBG('gl', glb[:], [128, 2, TT], F32, glB)
            DBG('cos', cosT[:], [128, TT], F32, [cosB])
            DBG('sin', sinT[:], [128, TT], F32, [sinB])
        tiT = [gvb[0], gvb[1]]; tiBf = [gvB[0], gvB[1]]
        aaT = [gvb[2], gvb[3]]; aaBf = [gvB[2], gvB[3]]
        wqT = [ubuf[0], ubuf[1]]; wqBf = [ubB[0], ubB[1]]
        trT = [gvb[4], gvb[5]]; trBf = [gvB[4], gvB[5]]
        def gn_consumer(chunks):
            def cons(rs, rsb):
                for j in chunks:
                    dve_stt(hT[:, j, :], mixbuf[:, j, :], cv[:, o + C_GN + j:o + C_GN + j + 1], rs[:], ALU.mult, ALU.mult,
                            mixdeps(j) + [cvB, rsb], [hB[j]])
            return cons

        def st_sconv(ch):
            wc = lambda j: cv[:, o + C_SCW + j * 2 + ch:o + C_SCW + j * 2 + ch + 1]
            t0, t0b = scratch()
            pool_ts(t0[:], prodbuf[:, ch, 2:TT + 2], wc(2), 0.0, ALU.mult, ALU.add, [prB[ch], cvB], [t0b])
            t1, t1b = scratch()
            dve_stt(t1[:], prodbuf[:, ch, 1:TT + 1], wc(1), t0[:], ALU.mult, ALU.add, [prB[ch], cvB, t0b], [t1b])
            t2, t2b = scratch()
            dve_stt(t2[:], prodbuf[:, ch, 0:TT], wc(0), t1[:], ALU.mult, ALU.add, [prB[ch], cvB, t1b], [t2b])
            dve_tt(mixbuf[:, 4 + ch, :], t2[:], cbs[:, ch, :], ALU.mult, [t2b, cbB[ch]], mixdeps(4 + ch))

        def st_lru_conv():
            for ch in range(2):
                wc = lambda j: cv[:, o + C_LCW + j * 2 + ch:o + C_LCW + j * 2 + ch + 1]
                t0, t0b = scratch()
                pool_ts(t0[:], lxbuf[:, ch, 3:TT + 3], wc(3), cv[:, o + C_LCB + ch:o + C_LCB + ch + 1], ALU.mult, ALU.add,
                        [lxB[ch], cvB], [t0b])
                t1, t1b = scratch()
                dve_stt(t1[:], lxbuf[:, ch, 2:TT + 2], wc(2), t0[:], ALU.mult, ALU.add, [lxB[ch], cvB, t0b], [t1b])
                t2, t2b = scratch()
                dve_stt(t2[:], lxbuf[:, ch, 1:TT + 1], wc(1), t1[:], ALU.mult, ALU.add, [lxB[ch], cvB, t1b], [t2b])
                dve_stt(lru_in[:, ch, :], lxbuf[:, ch, 0:TT], wc(0), t2[:], ALU.mult, ALU.add, [lxB[ch], cvB, t2b], [liB[ch]])
                act(lrub[:, ch, :], lru_in[:, ch, :], AF.Copy, [liB[ch]], [lbB[ch]])

        def st_lru_gates():
            pss = []
            for ch in range(2):
                psr, pbr = mmps()
                mm_group(psr[:], pbr, [(gw[:, ch * 128:(ch + 1) * 128], lrub[:, ch, :], [gwb, lbB[ch]])])
                act(trT[ch][:], psr[:], AF.Tanh, [pbr, derB], [trBf[ch]], bias=der[:, d0 + ch:d0 + ch + 1], scale=0.5)
                psi, pbi = mmps()
                mm_group(psi[:], pbi, [(gw[:, (2 + ch) * 128:(3 + ch) * 128], lrub[:, ch, :], [gwb, lbB[ch]])])
                act(tiT[ch][:], psi[:], AF.Tanh, [pbi, derB], [tiBf[ch]], bias=der[:, d0 + 2 + ch:d0 + 3 + ch], scale=0.5)
            for ch in range(2):
                act(aaT[ch][:], trT[ch][:], AF.Exp, [trBf[ch], derB], [aaBf[ch]],
                    bias=der[:, d0 + 4 + ch:d0 + 5 + ch], scale=der[:, d0 + 4 + ch:d0 + 5 + ch])
                a2, a2b = scratch()
                act(a2[:], trT[ch][:], AF.Exp, [trBf[ch], derB], [a2b],
                    bias=der[:, d0 + 6 + ch:d0 + 7 + ch], scale=der[:, d0 + 6 + ch:d0 + 7 + ch])
                dve_ts(wqT[ch][:, 0:TT], a2[:], -0.25, 0.25, ALU.mult, ALU.add, [a2b], [wqBf[ch]])
            for ch in range(2):
                act(wqT[ch][:, 0:TT], wqT[ch][:, 0:TT], AF.Sqrt, [wqBf[ch]], [wqBf[ch]])

        def st_lru_tail(ch):
            u1, u1b = scratch()
            dve_stt(u1[:], tiT[ch][:], 1.0, lru_in[:, ch, :], ALU.add, ALU.mult, [tiBf[ch], liB[ch]], [u1b])
            uu, uub = scratch()
            dve_tt(uu[:], u1[:], wqT[ch][:, 0:TT], ALU.mult, [u1b, wqBf[ch]], [uub])
            hh, hhb = scratch()
            TR.op("dve", lambda e, hh=hh, aa=aaT[ch], uu=uu, ch=ch: e.tensor_tensor_scan(
                out=hh[:], data0=aa[:], data1=uu[:], initial=hstate[l][:, ch:ch + 1], op0=ALU.mult, op1=ALU.add),
                [aaBf[ch], uub, hsB[l][ch]], [hhb])
            dve_tt(mixbuf[:, 6 + ch, :], hh[:], glb[:, ch, :], ALU.mult, [hhb, glB[ch]], mixdeps(6 + ch))
            dve_copy(hstate[l][:, ch:ch + 1], hh[:, TT - 1:TT], [hhb], [hsB[l][ch]])

        def st_side_norms():
            multi_norm([([(mixbuf[:, j, :], mixdeps(j)) for j in (4, 5)], 256, EPS, gn_consumer((4, 5))),
                        ([(mixbuf[:, j, :], mixdeps(j)) for j in (6, 7)], 256, 4 * EPS, gn_consumer((6, 7)))])

        def _two(a, b):
            def f():
                a()
                b()
            return f
        side = [_two(lambda: st_sconv(0), st_lru_conv), _two(lambda: st_sconv(1), st_lru_gates),
                lambda: st_lru_tail(0), lambda: st_lru_tail(1), st_side_norms, None, None, None]
        iters = [(c, kv) for c in range(8) for kv in range(2)]
        ctx = {}

        def geom(c):
            return c // 2, (c % 2 == 0), (it * 8 + c >= 2)

        def emit_S(i):
            c, kv = iters[i]
            b, even, prev_valid = geom(c)
            kp = slice(kv * 64, kv * 64 + 64)
            k2 = st["P"] % 2
            k3 = st["P"] % 2
            st["P"] += 1
            Sps, Sb = PS[4 + k2], psB[4 + k2]
            ctx[i] = (Sps, Sb, PS[6 + k2], psB[6 + k2], Pbuf[k3], PB[k3])
            qv = qT[kp, 0:4, 64 * c:64 * c + 64]
            if prev_valid:
                mm_group(Sps[:, 0:256], Sb, [(kT[l][kp, 128 * b:128 * b + 128], qv, [kB[l]] + qB)])
            mcur = 64 if even else 128
            mm_group(Sps[0:mcur, 256:512], Sb, [(kT[l][kp, 128 * (b + 1):128 * (b + 1) + mcur], qv, [kB[l]] + qB)])

        def emit_exp(i):
            c, kv = iters[i]
            b, even, prev_valid = geom(c)
            Sps, Sb, Ops, Ob, P, Pb_ = ctx[i]
            if prev_valid:
                rows = slice(0, 128) if even else slice(64, 128)
                act(P[rows, 0:256], Sps[rows, 0:256], AF.Exp, [Sb], [Pb_], scale=0.125)
            rows = slice(0, 64) if even else slice(0, 128)
            act(P[rows, 256:512], Sps[rows, 256:512], AF.Exp, [Sb], [Pb_], scale=0.125)

        def emit_O(i):
            c, kv = iters[i]
            b, even, prev_valid = geom(c)
            Sps, Sb, Ops, Ob, P, Pb_ = ctx[i]
            items = []
            if prev_valid:
                if even:
                    items.append((vaug[l][:, b, kv, :], P[:, 0:256], [vB[l], Pb_]))
                else:
                    items.append((vaug[l][64:128, b, kv, :], P[64:128, 0:256], [vB[l], Pb_]))
            if even:
                items.append((vaug[l][0:64, b + 1, kv, :], P[0:64, 256:512], [vB[l], Pb_]))
            else:
                items.append((vaug[l][:, b + 1, kv, :], P[:, 256:512], [vB[l], Pb_]))
            so = l * 512 + kv * 256
            items.append((sinkL[0:1, kv, :], sinkRb[0:1, so:so + 256], [sLB, sRB]))
            mm_group(Ops[:, 0:256], Ob, items)

        def emit_evac(i):
            c, kv = iters[i]
            Sps, Sb, Ops, Ob, P, Pb_ = ctx[i]
            orow = slice(0, 64) if kv == 0 else slice(64, 128)
            drow = slice(64, 128) if kv == 0 else slice(0, 64)
            dve_copy(mixbuf[orow, 0:4, 64 * c:64 * c + 64], Ops[orow, 0:256].rearrange("p (g q) -> p g q", g=4),
                     [Ob], [aoB[c]])
            dve_copy(denall[orow, c, :, :], Ops[drow, 0:256].rearrange("p (g q) -> p g q", g=4), [Ob], [denB[c]])

        emit_S(0)
        for i, (c, kv) in enumerate(iters):
            emit_exp(i)
            if i + 1 < len(iters):
                emit_S(i + 1)
            emit_O(i)
            emit_evac(i)
            if kv == 1 and side[c] is not None:
                side[c]()
        dflat = denall[:].rearrange("p c g q -> p (c g q)")
        act(dflat, dflat, AF.Ln, denB, denB)
        act(dflat, dflat, AF.Exp, denB, denB, scale=-1.0)
        for j in range(4):
            mv = mixbuf[:, j, :].rearrange("p (c q) -> p c q", c=8)
            dve_tt(mv, mv, denall[:, :, j, :], ALU.mult, mixdeps(j) + denB, mixdeps(j))
        if l == 0 and it == 0 and seq == 0:
            DBG('mix', mixbuf[:], [128, 8, TT], F32, mixB + aoB)
            DBG('lru_in', lru_in[:], [128, 2, TT], F32, liB)
        pool_copy(kT[l][:, 0:128], kT[l][:, TT:TT + 128], [kB[l]], [kB[l]])
        pool_copy(vaug[l][:, 0, :, :], vaug[l][:, 4, :, :], [vB[l]], [vB[l]])
        multi_norm([([(mixbuf[:, j, :], mixdeps(j)) for j in (0, 1, 2, 3)], 512, EPS, gn_consumer((0, 1, 2, 3)))])
        for n in range(8):
            ps, pb = proj_unit(hT, hB)
            dve_tt(xbuf[:, n, :], ps[:], xbuf[:, n, :], ALU.add, [pb, xB[n]], [xB[n]])
            stats_accum(n)
        st["pre_stats"] = True

    def ffn(l, seq, it, early=None):
        o = l * NCOL
        if l == 0 and it == 0 and seq == 0:
            DBG('x_mix', xbuf[:], [128, 8, TT], F32, xB)
        def cons_h(rs, rsb):
            for c in range(8):
                dve_stt(hT[:, c, :], xbuf[:, c, :], cv[:, o + C_NFFN + c:o + C_NFFN + c + 1], rs[:], ALU.mult, ALU.mult,
                        [xB[c], cvB, rsb], [hB[c]])
        multi_norm([([(xbuf[:, c, :], [xB[c]]) for c in range(8)], D, EPS, cons_h)])
        st["mmset"] = "ffn"
        pending = None

        def finish_pair(jj, G, V):
            sg, sgb = scratch()
            act(sg[:], gvb[G][:], AF.Silu, [gvB[G]], [sgb])
            dve_tt(actT[:, jj, :], sg[:], gvb[V][:], ALU.mult, [sgb, gvB[V]], [actB[jj]])

        for jj in range(22):
            pair = []
            for s in range(2):
                uidx = jj * 2 + s
                ps, pb = proj_unit(hT, hB)
                ub, ubb = ubuf[st["ub"] % 4], ubB[st["ub"] % 4]
                st["ub"] += 1
                wc = lambda j: cv[:, o + C_FCW + j * 44 + uidx:o + C_FCW + j * 44 + uidx + 1]
                pool_copy(ub[:, 0:2], ucarry[l][:, uidx, :], [ucB[l][uidx]], [ubb])
                act(ub[:, 2:TT + 2], ps[:], AF.Copy, [pb], [ubb])
                t0, t0b = scratch()
                act(t0[:], ps[:], AF.Identity, [pb, cvB], [t0b], bias=cv[:, o + C_FCB + uidx:o + C_FCB + uidx + 1], scale=wc(2))
                pool_copy(ucarry[l][:, uidx, :], ub[:, TT:TT + 2], [ubb], [ucB[l][uidx]])
                t1, t1b = scratch()
                dve_stt(t1[:], ub[:, 1:TT + 1], wc(1), t0[:], ALU.mult, ALU.add, [ubb, cvB, t0b], [t1b])
                gi = (jj % 3) * 2 + s
                dve_stt(gvb[gi][:], ub[:, 0:TT], wc(0), t1[:], ALU.mult, ALU.add, [ubb, cvB, t1b], [gvB[gi]])
                pair.append(gi)
                if s == 0 and pending is not None:
                    finish_pair(*pending)
                    pending = None
            pending = (jj, pair[0], pair[1])
        finish_pair(*pending)
        if early is not None:
            nseq, nit, nxb, nxbl = early
            early_next_stats(nxb, nxbl)
            cossin_tables(nseq, nit)
        for m in range(8):
            ps, pb = mmps()
            for gq in range(3):
                w, wb = next_unit()
                nf = 8 if gq < 2 else 6
                for fi in range(nf):
                    f = gq * 8 + fi
                    first = (f == 0)
                    last = (f == 21)
                    TR.op("pe", (lambda e, ps=ps, w=w, fi=fi, f=f, first=first, last=last:
                                 e.matmul(ps[:], lhsT=w[:, fi * 128:(fi + 1) * 128], rhs=actT[:, f, :], start=first, stop=last)),
                          [wb, actB[f]], [pb], inc=(fi == nf - 1))
            dve_tt(xbuf[:, m, :], ps[:], xbuf[:, m, :], ALU.add, [pb, xB[m]], [xB[m]])
            stats_accum(m)
        st["mmset"] = "mix"
        st["pre_stats"] = True

    out_toks = []
    tiles = [(seq, it) for seq in range(NSEQ) for it in range(NT)]

    def load_x(k):
        sq_, it_ = tiles[k]
        xb_, xbl_ = xbufs[k % 2], xBs[k % 2]
        TR.dma(lambda e, sq_=sq_, t0=it_ * TT, xb_=xb_: e.dma_start(
            out=xb_[:], in_=xT[sq_, :, t0:t0 + TT].rearrange("(c p) t -> p c t", p=128)), f"xin{k % 2}", (), xbl_)

    for k, (seq, it) in enumerate(tiles):
        xbuf, xB = xbufs[k % 2], xBs[k % 2]
        has_next = k + 1 < len(tiles)
        if it == 0:
            TR.new_epoch()
            for l in range(L):
                for ch in range(2):
                    pool_memset(prodc[l][:, ch, :], 0.0, [pcB[l][ch]])
                    pool_memset(lxc[l][:, ch, :], 0.0, [lcB[l][ch]])
                    pool_memset(hstate[l][:, ch:ch + 1], 0.0, [hsB[l][ch]])
                pool_memset(ucarry[l][:], 0.0, ucB[l])
        t0 = it * TT
        if k == 0:
            load_x(0)
            cossin_tables(seq, it)
        if has_next:
            load_x(k + 1)
        for l in range(L):
            mixer(l, seq, it, skip_norm=(l == 0 and k > 0))
            early = None
            if l == L - 1 and has_next:
                early = (tiles[k + 1][0], tiles[k + 1][1], xbufs[(k + 1) % 2], xBs[(k + 1) % 2])
            ffn(l, seq, it, early=early)
            if l == 0 and it == 0 and seq == 0:
                DBG('x_ffn', xbuf[:], [128, 8, TT], F32, xB)
                DBG('act', actT[:], [128, 22, TT], BF16, actB)
        o = (L - 1) * NCOL

        def cons_f(rs, rsb, o=o, xb_=xbuf, xbl_=xB):
            for c in range(8):
                dve_stt(mixbuf[:, c, :], xb_[:, c, :], cv[:, o + C_FIN + c:o + C_FIN + c + 1], rs[:], ALU.mult, ALU.mult,
                        [xbl_[c], cvB, rsb], mixdeps(c))

        groups = [([(xbuf[:, c, :], [xB[c]]) for c in range(8)], D, EPS, cons_f)]
        if has_next:
            def cons_n(rs, rsb, xb_=xbufs[(k + 1) % 2], xbl_=xBs[(k + 1) % 2]):
                for c in range(8):
                    dve_stt(hT[:, c, :], xb_[:, c, :], cv[:, C_NMIX + c:C_NMIX + c + 1], rs[:], ALU.mult, ALU.mult,
                            [xbl_[c], cvB, rsb], [hB[c]])
            groups.append((None, D, EPS, cons_n))
        multi_norm(groups)
        rd_all = []
        for c in range(8):
            rd_all += mixdeps(c)
        tok = TR.dma(lambda e, seq=seq, t0=t0: e.dma_start(
            out=outT[seq, :, t0:t0 + TT].rearrange("(c p) t -> p c t", p=128), in_=mixbuf[:]), "oout", rd_all, ())
        out_toks.append(tok)
    TR.final_wait("sp", [out_toks[-1]])

    sems = {}
    for key in TR.cnt:
        nm = "s_" + "_".join(str(k) for k in key)
        sems[key] = es.enter_context(nc.semaphore(nm))
    block = es.enter_context(nc.Block())

    def run(eng_name, e):
        for waits, fn, inckey, incval in TR.plan[eng_name]:
            for key, val in waits:
                e.wait_ge(sems[key], val)
            if fn is None:
                continue
            ins = fn(e)
            if inckey is not None:
                ins.then_inc(sems[inckey], incval)

    @block.sync
    def _(e):
        run("sp", e)

    @block.tensor
    def _(e):
        run("pe", e)

    @block.scalar
    def _(e):
        run("act", e)

    @block.vector
    def _(e):
        run("dve", e)

    @block.gpsimd
    def _(e):
        run("pool", e)

    es.close()
    stats = {e: len(TR.plan[e]) for e in ENGS}
    stats["sems"] = len(sems)
    stats["maxcnt"] = max(TR.cnt.values())
    stats['dbg'] = list(dbg_outs)
    return nc, stats


def run_cores(inputs, NSEQ, S, n_cores=8, debug=False):
    wst, cvec, sinkr, ropec = _prep_weights(inputs)
    x = np.asarray(inputs["x"], np.float32)
    pos = np.asarray(inputs["positions"], np.int32)
    nc, stats = build_program(NSEQ, S, debug)
    in_maps = []
    for c in range(n_cores):
        xs = x[c * NSEQ:(c + 1) * NSEQ]
        in_maps.append({
            "xT": np.ascontiguousarray(xs.transpose(0, 2, 1)),
            "pos": np.ascontiguousarray(pos[c * NSEQ:(c + 1) * NSEQ]),
            "wst": wst, "cvec": cvec, "ropec": ropec, "sinkr": sinkr, "ident": np.eye(128, dtype=np.float32),
        })
    res = run_bass_kernel_spmd(nc, in_maps, core_ids=list(range(n_cores)))
    outs = [np.asarray(r["outT"]).transpose(0, 2, 1) for r in res.results]
    if debug:
        stats['dbgvals'] = [{n: np.asarray(r['dbg_' + n]) for n in stats['dbg']} for r in res.results]
    return np.ascontiguousarray(np.concatenate(outs, axis=0)).astype(np.float32), stats


def kernel(**inputs):
    out, _ = run_cores(inputs, NSEQ=4, S=2048)
    return out
```

```python
import math
from contextlib import ExitStack

import numpy as np
import concourse.bass as bass
import concourse.mybir as mybir
from concourse.bass_utils import run_bass_kernel_spmd

F32 = mybir.dt.float32
BF16 = mybir.dt.bfloat16
I32 = mybir.dt.int32
AF = mybir.ActivationFunctionType
ALU = mybir.AluOpType

D = 1024
L = 2
TT = 512
NU = 98
NCOL = 230
NSLOT = 8
NSC = 10
EPS = 1e-6
ENGS = ("sp", "pe", "act", "dve", "pool")

C_NMIX, C_NFFN, C_GN, C_SCW, C_LCW, C_LCB, C_BA, C_BX, C_LAM, C_FCW, C_FCB, C_FIN = 0, 8, 16, 24, 30, 38, 40, 42, 44, 46, 178, 222


def _unit_from_cols(W, cols):
    cols = np.asarray(cols)
    sel = np.zeros((W.shape[0], 128), np.float32)
    ok = cols >= 0
    sel[:, ok] = W[:, cols[ok]]
    kc = W.shape[0] // 128
    u = np.zeros((128, 8, 128), np.float32)
    u[:, :kc, :] = sel.reshape(kc, 128, 128).transpose(1, 0, 2)
    return u.reshape(128, 1024)


def _mix_perm():
    perm = np.zeros(1024, np.int64)
    for j in range(4):
        for p in range(128):
            perm[j * 128 + p] = j * 64 + p if p < 64 else (j + 4) * 64 + (p - 64)
    perm[512:] = np.arange(512, 1024)
    return perm


def _prep_weights(inp):
    perm = _mix_perm()
    wst = np.zeros((L * NU, 128, 1024), np.float32)
    cvec = np.zeros((128, L * NCOL), np.float32)
    sinkr = np.zeros((1, L * 512), np.float32)
    for l in range(L):
        Win = np.asarray(inp["w_in"][l], np.float32)
        units = []

        def head_cols(base, h):
            return [base + h * 64 + d for d in range(64)]

        def swap_cols(base, h):
            out = []
            for d in range(64):
                if d < 8:
                    out.append(base + h * 64 + d + 8)
                elif d < 16:
                    out.append(base + h * 64 + d - 8)
                else:
                    out.append(-1)
            return out

        for j in range(4):
            units.append(_unit_from_cols(Win, head_cols(0, j) + head_cols(0, j + 4)))
            units.append(_unit_from_cols(Win, swap_cols(0, j) + swap_cols(0, j + 4)))
        units.append(_unit_from_cols(Win, head_cols(512, 0) + head_cols(512, 1)))
        units.append(_unit_from_cols(Win, swap_cols(512, 0) + swap_cols(512, 1)))
        units.append(_unit_from_cols(Win, list(range(640, 768))))
        for ch in range(2):
            units.append(_unit_from_cols(Win, list(range(768 + ch * 128, 768 + (ch + 1) * 128))))
        for ch in range(2):
            units.append(_unit_from_cols(Win, list(range(1280 + ch * 128, 1280 + (ch + 1) * 128))))
            units.append(_unit_from_cols(Win, list(range(1024 + ch * 128, 1024 + (ch + 1) * 128))))
        for ch in range(2):
            units.append(_unit_from_cols(Win, list(range(1536 + ch * 128, 1536 + (ch + 1) * 128))))
        for ch in range(2):
            units.append(_unit_from_cols(Win, list(range(1792 + ch * 128, 1792 + (ch + 1) * 128))))
        g = np.zeros((128, 1024), np.float32)
        for gi, W in enumerate((inp["lru_wa"][l], inp["lru_wx"][l])):
            W = np.asarray(W, np.float32)
            for ch in range(2):
                col0 = (gi * 2 + ch) * 128
                g[0:64, col0:col0 + 64] = W[2 * ch]
                g[64:128, col0 + 64:col0 + 128] = W[2 * ch + 1]
        units.append(g)
        Wo = np.asarray(inp["w_out"][l], np.float32)[perm, :]
        for n in range(8):
            units.append(_unit_from_cols(Wo, list(range(n * 128, (n + 1) * 128))))
        Wu = np.asarray(inp["ffn_w_up"][l], np.float32)
        for jj in range(22):
            for s in range(2):
                units.append(_unit_from_cols(Wu, list(range(s * 2816 + jj * 128, s * 2816 + (jj + 1) * 128))))
        Wd = np.asarray(inp["ffn_w_down"][l], np.float32)
        for m in range(8):
            for gq in range(3):
                u = np.zeros((128, 8, 128), np.float32)
                for fi in range(8):
                    f = gq * 8 + fi
                    if f < 22:
                        u[:, fi, :] = Wd[f * 128:(f + 1) * 128, m * 128:(m + 1) * 128]
                units.append(u.reshape(128, 1024))
        assert len(units) == NU
        wst[l * NU:(l + 1) * NU] = np.stack(units)

        cv = cvec[:, l * NCOL:(l + 1) * NCOL]

        def colmat(v):
            return np.asarray(v, np.float32).reshape(-1, 128).T

        cv[:, C_NMIX:C_NMIX + 8] = colmat(inp["norm_mix_g"][l])
        cv[:, C_NFFN:C_NFFN + 8] = colmat(inp["norm_ffn_g"][l])
        gn = np.concatenate([np.asarray(inp["gn_attn_g"][l]), np.asarray(inp["gn_sconv_g"][l]),
                             np.asarray(inp["gn_lru_g"][l])]).astype(np.float32)[perm]
        cv[:, C_GN:C_GN + 8] = colmat(gn)
        for j in range(3):
            cv[:, C_SCW + j * 2:C_SCW + j * 2 + 2] = colmat(inp["sconv_w"][l][j])
        for j in range(4):
            cv[:, C_LCW + j * 2:C_LCW + j * 2 + 2] = colmat(inp["lru_conv_w"][l][j])
        cv[:, C_LCB:C_LCB + 2] = colmat(inp["lru_conv_b"][l])
        cv[:, C_BA:C_BA + 2] = colmat(np.asarray(inp["lru_ba"][l]).reshape(-1))
        cv[:, C_BX:C_BX + 2] = colmat(np.asarray(inp["lru_bx"][l]).reshape(-1))
        cv[:, C_LAM:C_LAM + 2] = colmat(inp["lru_lambda"][l])
        fperm = np.zeros(5632, np.int64)
        for uidx in range(44):
            jj, s = uidx // 2, uidx % 2
            fperm[uidx * 128:(uidx + 1) * 128] = s * 2816 + jj * 128 + np.arange(128)
        for j in range(3):
            cv[:, C_FCW + j * 44:C_FCW + (j + 1) * 44] = colmat(np.asarray(inp["ffn_conv_w"][l][j])[fperm])
        cv[:, C_FCB:C_FCB + 44] = colmat(np.asarray(inp["ffn_conv_b"][l])[fperm])
        cv[:, C_FIN:C_FIN + 8] = colmat(inp["final_norm_g"])
        sk = np.asarray(inp["attn_sinks"][l], np.float32)
        for kv in range(2):
            for gq in range(4):
                o = l * 512 + kv * 256 + gq * 64
                sinkr[0, o:o + 64] = sk[kv * 4 + gq]
    ropec = np.zeros((128, 2), np.float32)
    ropec[:, 1] = 1.0
    for p in range(128):
        d = p % 64
        if d < 16:
            ropec[p, 0] = np.float32(500000.0) ** np.float32(-(2 * (d % 8)) / 16.0)
            ropec[p, 1] = -1.0 if d < 8 else 1.0
    return wst, cvec, sinkr, ropec


class Buf:
    __slots__ = ("name", "w", "r")

    def __init__(self, name):
        self.name = name
        self.w = None
        self.r = {}


class Tracker:
    def __init__(self):
        self.plan = {e: [] for e in ENGS}
        self.cnt = {}
        self.seen = {e: {} for e in ENGS}
        self.epoch = 0

    def new_epoch(self):
        self.epoch += 1

    def _waits(self, eng, reads, writes):
        toks = []
        for b in reads:
            if b.w is not None:
                toks.append(b.w)
        for b in writes:
            if b.w is not None:
                toks.append(b.w)
            toks.extend(b.r.values())
        waits = []
        seen = self.seen[eng]
        for key, val in toks:
            if eng == "pe" and key[0] == "pe":
                continue
            if seen.get(key, 0) < val:
                seen[key] = val
                waits.append((key, val))
        best = {}
        for key, val in waits:
            best[key] = max(best.get(key, 0), val)
        return list(best.items())

    def _mark(self, rkey, tok, reads, writes):
        for b in reads:
            b.r[rkey] = tok
        for b in writes:
            b.w = tok
            b.r = {}

    def op(self, eng, fn, reads=(), writes=(), inc=True):
        waits = self._waits(eng, reads, writes)
        key = (eng, self.epoch)
        if inc:
            self.cnt[key] = self.cnt.get(key, 0) + 1
            tok = (key, self.cnt[key])
        else:
            tok = (key, self.cnt.get(key, 0) + 1)
        self.plan[eng].append((waits, fn, key if inc else None, 1))
        self._mark(eng, tok, reads, writes)
        return tok

    def dma(self, fn, semname, reads=(), writes=(), queue="sp"):
        waits = self._waits(queue, reads, writes)
        key = ("dma", semname)
        prev = self.cnt.get(key, 0)
        if prev and self.seen[queue].get(key, 0) < prev:
            self.seen[queue][key] = prev
            waits.append((key, prev))
        self.cnt[key] = prev + 16
        tok = (key, prev + 16)
        self.plan[queue].append((waits, fn, key, 16))
        self._mark(key, tok, reads, writes)
        return tok

    def final_wait(self, queue, toks):
        self.plan[queue].append((list(toks), None, None, 0))


def build_program(NSEQ, S, debug=False):
    NT = S // TT
    dbg_outs = {}
    nc = bass.Bass("TRN2", target_bir_lowering=False)
    xT = nc.dram_tensor("xT", [NSEQ, D, S], F32, kind="ExternalInput").ap()
    posd = nc.dram_tensor("pos", [NSEQ, S], I32, kind="ExternalInput").ap()
    wst = nc.dram_tensor("wst", [L * NU, 128, 1024], F32, kind="ExternalInput").ap()
    cvd = nc.dram_tensor("cvec", [128, L * NCOL], F32, kind="ExternalInput").ap()
    roped = nc.dram_tensor("ropec", [128, 2], F32, kind="ExternalInput").ap()
    sinkd = nc.dram_tensor("sinkr", [1, L * 512], F32, kind="ExternalInput").ap()
    identd = nc.dram_tensor("ident", [128, 128], F32, kind="ExternalInput").ap()
    outT = nc.dram_tensor("outT", [NSEQ, D, S], F32, kind="ExternalOutput").ap()
    wbf = nc.dram_tensor("wbf", [L * NU, 128, 1024], BF16, kind="Internal").ap()

    TR = Tracker()
    es = ExitStack()

    def sb(name, shape, dt):
        return es.enter_context(nc.sbuf_tensor(name, shape, dt))

    def bufs(name, n):
        return [Buf(f"{name}{i}") for i in range(n)]

    xbuf = sb("xbuf", [128, 8, TT], F32); xB = bufs("x", 8)
    xbuf_b = sb("xbuf_b", [128, 8, TT], F32); xB_b = bufs("xb_", 8)
    xbufs = [xbuf, xbuf_b]; xBs = [xB, xB_b]
    hT = sb("hT", [128, 8, TT], BF16); hB = bufs("h", 8)
    sqb = sb("sqb", [128, 8, TT], BF16); sqB = bufs("sq", 8)
    qT = sb("qT", [128, 4, TT], BF16); qB = bufs("q", 4)
    kT = [sb(f"kT{l}", [128, 128 + TT], BF16) for l in range(L)]; kB = bufs("k", L)
    vaug = [sb(f"vaug{l}", [128, 5, 2, 128], BF16) for l in range(L)]; vB = bufs("v", L)
    cbs = sb("cbs", [128, 2, TT], F32); cbB = bufs("cb", 2)
    prodbuf = sb("prodbuf", [128, 2, TT + 2], F32); prB = bufs("pr", 2)
    prodc = [sb(f"prodc{l}", [128, 2, 2], F32) for l in range(L)]; pcB = [bufs(f"pc{l}_", 2) for l in range(L)]
    lxbuf = sb("lxbuf", [128, 2, TT + 3], F32); lxB = bufs("lx", 2)
    lxc = [sb(f"lxc{l}", [128, 2, 3], F32) for l in range(L)]; lcB = [bufs(f"lc{l}_", 2) for l in range(L)]
    mixbuf = sb("mixbuf", [128, 8, TT], F32); mixB = bufs("mix", 8); aoB = bufs("ao", 8)
    SC = [sb(f"sc{i}", [128, TT], F32) for i in range(NSC)]; scB = bufs("sc", NSC)
    lru_in = sb("lru_in", [128, 2, TT], F32); liB = bufs("li", 2)
    lrub = sb("lrub", [128, 2, TT], BF16); lbB = bufs("lb", 2)
    glb = sb("glb", [128, 2, TT], F32); glB = bufs("gl", 2)
    hstate = [sb(f"hstate{l}", [128, 2], F32) for l in range(L)]; hsB = [bufs(f"hs{l}_", 2) for l in range(L)]
    Pbuf = [sb(f"P{i}", [128, 512], BF16) for i in range(2)]; PB = bufs("P", 2)
    denall = sb("denall", [128, 8, 4, 64], F32); denB = bufs("den", 8)
    cosT = sb("cosT", [128, TT], F32); cosB = Buf("cos")
    sinT = sb("sinT", [128, TT], F32); sinB = Buf("sin")
    posi = sb("posi", [128, TT], I32); posB = Buf("posi")
    ubuf = [sb(f"ub{i}", [128, TT + 2], F32) for i in range(4)]; ubB = bufs("ub", 4)
    ucarry = [sb(f"ucarry{l}", [128, 44, 2], F32) for l in range(L)]; ucB = [bufs(f"uc{l}_", 44) for l in range(L)]
    gvb = [sb(f"gv{i}", [128, TT], F32) for i in range(6)]; gvB = bufs("gv", 6)
    actT = sb("actT", [128, 22, TT], BF16); actB = bufs("act", 22)
    wring = [sb(f"wr{i}", [128, 1024], BF16) for i in range(NSLOT)]; wrB = bufs("wr", NSLOT)
    cv = sb("cv", [128, L * NCOL], F32); cvB = Buf("cv")
    ropec = sb("ropecs", [128, 2], F32); ropeB = Buf("ropec")
    der = sb("der", [128, L * 8], F32); derB = Buf("der")
    dtmp = sb("dtmp", [128, 8], F32); dtB = Buf("dtmp")
    sink32 = sb("sink32", [1, L * 512], F32); s32B = Buf("sink32")
    sinkRb = sb("sinkRb", [1, L * 512], BF16); sRB = Buf("sinkRb")
    sinkL = sb("sinkL", [1, 2, 128], BF16); sLB = Buf("sinkL")
    ones = sb("ones", [128, 128], BF16); onesB = Buf("ones")
    cpow = sb("cpow", [128, 4], F32); cpB = Buf("cpow")
    ident = sb("ident_sb", [128, 128], F32); identB = Buf("ident")
    onesf = sb("onesf", [128, 128], F32); onesfB = Buf("onesf")
    stat = [sb(f"stat{i}", [128, 8], F32) for i in range(2)]; statB = bufs("stat", 2)
    dg = [sb(f"dg{i}", [128, 4, 128], F32) for i in range(2)]; dgB = bufs("dg", 2)
    PS = [es.enter_context(nc.psum_tensor(f"ps{i}", [128, 512], F32)) for i in range(8)]
    psB = bufs("ps", 8)
    wbfB = bufs("wbf", L * NU)

    st = {"sc": 0, "mm": 0, "P": 0, "ub": 0, "stat": 0}

    def DBG(name, tile, shape, dt, bl):
        if not debug or name in dbg_outs:
            return
        d = nc.dram_tensor("dbg_" + name, list(shape), dt, kind="ExternalOutput").ap()
        dbg_outs[name] = d
        TR.dma(lambda e: e.dma_start(out=d, in_=tile), "dbg_" + name, bl, ())

    def scratch():
        i = st["sc"] % NSC
        st["sc"] += 1
        return SC[i], scB[i]

    MM_SETS = {"mix": [0, 1, 2], "ffn": [0, 1, 2, 4, 5, 6, 7]}
    st["mmset"] = "mix"

    def mmps():
        bs = MM_SETS[st["mmset"]]
        i = bs[st["mm"] % len(bs)]
        st["mm"] += 1
        return PS[i], psB[i]

    PSN, PSNB = PS[3], psB[3]

    def mixdeps(j):
        return [mixB[j]] + (aoB if j < 4 else [])

    def act(out, in_, func, reads, writes, bias=None, scale=None):
        kw = {}
        if bias is not None:
            kw["bias"] = bias
        if scale is not None:
            kw["scale"] = scale
        TR.op("act", lambda e: e.activation(out=out, in_=in_, func=func, **kw), reads, writes)

    def dve_tt(out, in0, in1, op, reads, writes):
        TR.op("dve", lambda e: e.tensor_tensor(out=out, in0=in0, in1=in1, op=op), reads, writes)

    def dve_stt(out, in0, scalar, in1, op0, op1, reads, writes):
        TR.op("dve", lambda e: e.scalar_tensor_tensor(out=out, in0=in0, scalar=scalar, in1=in1, op0=op0, op1=op1),
              reads, writes)

    def dve_ts(out, in0, s1, s2, op0, op1, reads, writes):
        if s2 is None:
            TR.op("dve", lambda e: e.tensor_scalar(out=out, in0=in0, scalar1=s1, scalar2=None, op0=op0), reads, writes)
        else:
            TR.op("dve", lambda e: e.tensor_scalar(out=out, in0=in0, scalar1=s1, scalar2=s2, op0=op0, op1=op1),
                  reads, writes)

    def dve_copy(out, in_, reads, writes):
        TR.op("dve", lambda e: e.tensor_copy(out=out, in_=in_), reads, writes)

    def pool_tt(out, in0, in1, op, reads, writes):
        TR.op("pool", lambda e: e.tensor_tensor(out=out, in0=in0, in1=in1, op=op), reads, writes)

    def pool_ts(out, in0, s1, s2, op0, op1, reads, writes):
        if s2 is None:
            TR.op("pool", lambda e: e.tensor_scalar(out=out, in0=in0, scalar1=s1, scalar2=None, op0=op0), reads, writes)
        else:
            TR.op("pool", lambda e: e.tensor_scalar(out=out, in0=in0, scalar1=s1, scalar2=s2, op0=op0, op1=op1),
                  reads, writes)

    def pool_copy(out, in_, reads, writes):
        TR.op("pool", lambda e: e.tensor_copy(out=out, in_=in_), reads, writes)

    def pool_memset(ap, val, writes):
        TR.op("pool", lambda e: e.memset(ap, val), (), writes)

    def mm_group(out, outB, items, extra_reads=()):
        n = len(items)
        for i, (lhsT, rhs, rb) in enumerate(items):
            last = i == n - 1
            TR.op("pe", (lambda e, lhsT=lhsT, rhs=rhs, i=i, last=last:
                         e.matmul(out, lhsT=lhsT, rhs=rhs, start=(i == 0), stop=last)),
                  list(rb) + list(extra_reads), [outB], inc=last)

    total_units = NSEQ * NT * L * NU
    wstate = {"issued": 0, "consumed": 0}

    def issue_load(n):
        u = n % (L * NU)
        s = n % NSLOT
        TR.dma(lambda e, u=u, s=s: e.dma_start(out=wring[s][:], in_=wbf[u]), f"w{s}", [wbfB[u]], [wrB[s]])

    def next_unit():
        n = wstate["consumed"]
        while wstate["issued"] < min(total_units, n + NSLOT):
            issue_load(wstate["issued"])
            wstate["issued"] += 1
        wstate["consumed"] += 1
        s = n % NSLOT
        return wring[s], wrB[s]

    def proj_unit(rhs_tile, rhs_bufs):
        w, wb = next_unit()
        ps, pb = mmps()
        mm_group(ps[:], pb, [(w[:, kc * 128:(kc + 1) * 128], rhs_tile[:, kc, :], [wb, rhs_bufs[kc]]) for kc in range(8)])
        return ps, pb

    TR.dma(lambda e: e.dma_start(out=cv[:], in_=cvd[:, :]), "c0", (), [cvB])
    TR.dma(lambda e: e.dma_start(out=ropec[:], in_=roped[:, :]), "c1", (), [ropeB])
    TR.dma(lambda e: e.dma_start(out=sink32[:], in_=sinkd[:, :]), "c2", (), [s32B])
    pool_memset(ones[:], 1.0, [onesB])
    pool_memset(cpow[:], -0.5, [cpB])
    pool_memset(onesf[:], 1.0, [onesfB])
    TR.dma(lambda e: e.dma_start(out=ident[:], in_=identd[:, :]), "c3", (), [identB])
    pool_memset(sinkL[:, 0, 0:64], 0.0, [sLB]); pool_memset(sinkL[:, 0, 64:128], 1.0, [sLB])
    pool_memset(sinkL[:, 1, 0:64], 1.0, [sLB]); pool_memset(sinkL[:, 1, 64:128], 0.0, [sLB])
    for l in range(L):
        pool_memset(vaug[l][:], 1.0, [vB[l]])
    act(sinkRb[:], sink32[:], AF.Exp, [s32B], [sRB])
    for l in range(L):
        o = l * NCOL
        d0 = l * 8
        dve_ts(der[:, d0:d0 + 2], cv[:, o + C_BA:o + C_BA + 2], 0.5, None, ALU.mult, None, [cvB], [derB])
        dve_ts(der[:, d0 + 2:d0 + 4], cv[:, o + C_BX:o + C_BX + 2], 0.5, None, ALU.mult, None, [cvB], [derB])
        act(dtmp[:, 0:2], cv[:, o + C_LAM:o + C_LAM + 2], AF.Exp, [cvB], [dtB], scale=-1.0)
        dve_ts(dtmp[:, 2:4], dtmp[:, 0:2], -0.25, 1.0 / 3.0, ALU.mult, ALU.add, [dtB], [dtB])
        dve_tt(dtmp[:, 4:6], dtmp[:, 2:4], dtmp[:, 0:2], ALU.mult, [dtB], [dtB])
        dve_ts(dtmp[:, 2:4], dtmp[:, 4:6], -1.0, 0.5, ALU.mult, ALU.add, [dtB], [dtB])
        dve_tt(dtmp[:, 4:6], dtmp[:, 2:4], dtmp[:, 0:2], ALU.mult, [dtB], [dtB])
        dve_ts(dtmp[:, 2:4], dtmp[:, 4:6], -1.0, 1.0, ALU.mult, ALU.add, [dtB], [dtB])
        dve_tt(dtmp[:, 4:6], dtmp[:, 2:4], dtmp[:, 0:2], ALU.mult, [dtB], [dtB])
        dve_ts(der[:, d0 + 6:d0 + 8], dtmp[:, 4:6], -8.0, None, ALU.mult, None, [dtB], [derB])
        dve_ts(der[:, d0 + 4:d0 + 6], dtmp[:, 4:6], -4.0, None, ALU.mult, None, [dtB], [derB])

    def st32_slot(i):
        t, tb = (mixbuf, lambda j: mixdeps(j)) if i < 4 else (xbuf, lambda j: [xB[j]])
        k = i % 4
        return t[:, 2 * k:2 * k + 2, :], tb(2 * k) + tb(2 * k + 1)

    def st16_slot(i):
        t, tb = (sqb, sqB) if i < 4 else (hT, hB)
        k = i % 4
        return t[:, 2 * k:2 * k + 2, :], [tb[2 * k], tb[2 * k + 1]]

    NPU = L * NU
    cast_eng = ["act", "dve", "act", "dve", "act", "pool", "dve", "act"]

    def pre_load(u):
        a32, b32 = st32_slot(u % 8)
        TR.dma(lambda e, u=u, a32=a32: e.dma_start(out=a32, in_=wst[u].rearrange("p (a t) -> p a t", a=2)),
               f"pin{u % 8}", (), b32)

    for u in range(min(8, NPU)):
        pre_load(u)
    for u in range(NPU):
        a32, b32 = st32_slot(u % 8)
        a16, b16 = st16_slot(u % 8)
        ce = cast_eng[u % 8]
        if ce == "act":
            act(a16, a32, AF.Copy, b32, b16)
        elif ce == "dve":
            dve_copy(a16, a32, b32, b16)
        else:
            pool_copy(a16, a32, b32, b16)
        TR.dma(lambda e, u=u, a16=a16: e.dma_start(out=wbf[u].rearrange("p (a t) -> p a t", a=2), in_=a16),
               f"pout{u % 8}", b16, [wbfB[u]])
        if u + 8 < NPU:
            pre_load(u + 8)

    PSNS = [Buf(f"psn{i}") for i in range(4)]

    accx = sb("accx", [128, 4], F32); accB = Buf("accx")

    def stats_square(n):
        act(sqb[:, n, :], xbuf[:, n, :], AF.Square, [xB[n]], [sqB[n]])

    def stats_reduce(n):
        for blk in range(4):
            TR.op("pe", (lambda e, n=n, blk=blk: e.matmul(PSN[:, 12 + blk:13 + blk], lhsT=sqb[:, n, blk * 128:(blk + 1) * 128],
                                                         rhs=ones[:, 0:1], start=True, stop=True)),
                  [sqB[n], onesB], [PSNS[3]], inc=(blk == 3))
        if n == 0:
            dve_copy(accx[:], PSN[:, 12:16], [PSNS[3]], [accB])
        else:
            dve_tt(accx[:], PSN[:, 12:16], accx[:], ALU.add, [PSNS[3], accB], [accB])

    def stats_accum(n, lag=2):
        stats_square(n)
        if n >= lag:
            stats_reduce(n - lag)
        if n == 7:
            for j in range(8 - lag, 8):
                stats_reduce(j)

    def multi_norm(groups):
        base = 0
        infos = []
        pre = st.pop("pre_stats", False)
        modes = []
        for gi, (src_list, nfeat, eps, consume) in enumerate(groups):
            if pre and gi == 0:
                modes.append("acc")
                continue
            if src_list is None:
                modes.append("early")
                continue
            modes.append("now")
            n = len(src_list)
            for i, (ap, bl) in enumerate(src_list):
                act(sqb[:, base + i, :], ap, AF.Square, bl, [sqB[base + i]])
            infos.append((gi, base, n))
            base += n
        for gi, b0, n in infos:
            for blk in range(4):
                mm_group(PSN[:, gi * 4 + blk:gi * 4 + blk + 1], PSNS[gi],
                         [(sqb[:, b0 + i, blk * 128:(blk + 1) * 128], ones[:, 0:1], [sqB[b0 + i], onesB]) for i in range(n)])
        ks = []
        for gi, (src_list, nfeat, eps, consume) in enumerate(groups):
            k = st["stat"] % 2
            st["stat"] += 1
            ks.append(k)
            if modes[gi] == "acc":
                dve_ts(stat[k][:, 0:4], accx[:], 1.0 / nfeat, eps, ALU.mult, ALU.add, [accB], [statB[k]])
            elif modes[gi] == "early":
                dve_ts(stat[k][:, 0:4], PSN[:, 8:12], 1.0 / nfeat, eps, ALU.mult, ALU.add, [PSNS[2]], [statB[k]])
            else:
                dve_ts(stat[k][:, 0:4], PSN[:, gi * 4:gi * 4 + 4], 1.0 / nfeat, eps, ALU.mult, ALU.add, [PSNS[gi]], [statB[k]])
        for k in ks:
            pool_tt(stat[k][:, 4:8], stat[k][:, 0:4], cpow[:, 0:4], ALU.pow, [statB[k], cpB], [statB[k]])
        for k in ks:
            for blk in range(4):
                dve_ts(dg[k][:, blk, :], ident[:], stat[k][:, 4 + blk:5 + blk], None, ALU.mult, None, [identB, statB[k]], [dgB[k]])
        outs = []
        for k in ks:
            ps, pb = mmps()
            mm_group(ps[:], pb, [(onesf[:], dg[k][:].rearrange("p b t -> p (b t)"), [onesfB, dgB[k]])])
            outs.append((ps, pb))
        for (src_list, nfeat, eps, consume), (ps, pb) in zip(groups, outs):
            consume(ps, pb)

    def early_next_stats(xb, xbl):
        for c in range(8):
            act(sqb[:, c, :], xb[:, c, :], AF.Square, [xbl[c]], [sqB[c]])
        for blk in range(4):
            mm_group(PSN[:, 8 + blk:9 + blk], PSNS[2],
                     [(sqb[:, c, blk * 128:(blk + 1) * 128], ones[:, 0:1], [sqB[c], onesB]) for c in range(8)])

    def cossin_tables(seq, it):
        t0 = it * TT
        TR.dma(lambda e: e.dma_start(out=posi[:], in_=posd[seq:seq + 1, t0:t0 + TT].partition_broadcast(128)),
               "pos", (), [posB])
        pf, pfb = scratch()
        dve_copy(pf[:], posi[:], [posB], [pfb])
        ang, angb = scratch()
        dve_ts(ang[:], pf[:], ropec[:, 0:1], None, ALU.mult, None, [pfb, ropeB], [angb])
        dve_ts(posi[:], ang[:], 1.0 / (2 * math.pi), None, ALU.mult, None, [angb], [posB])
        kf, kfb = scratch()
        dve_copy(kf[:], posi[:], [posB], [kfb])
        C1 = 6.28125
        C2 = 2 * math.pi - C1
        r1, r1b = scratch()
        dve_stt(r1[:], kf[:], -C1, ang[:], ALU.mult, ALU.add, [kfb, angb], [r1b])
        r, rb = scratch()
        dve_stt(r[:], kf[:], -C2, r1[:], ALU.mult, ALU.add, [kfb, r1b], [rb])
        m, mb = scratch()
        dve_ts(m[:], r[:], math.pi, -2 * math.pi, ALU.is_gt, ALU.mult, [rb], [mb])
        r2, r2b = scratch()
        dve_tt(r2[:], r[:], m[:], ALU.add, [rb, mb], [r2b])
        m2, m2b = scratch()
        dve_ts(m2[:], r2[:], -math.pi, 2 * math.pi, ALU.is_lt, ALU.mult, [r2b], [m2b])
        rs_, rsb = scratch()
        dve_tt(rs_[:], r2[:], m2[:], ALU.add, [r2b, m2b], [rsb])
        m3, m3b = scratch()
        dve_ts(m3[:], rs_[:], math.pi / 2, -2 * math.pi, ALU.is_gt, ALU.mult, [rsb], [m3b])
        rc, rcb = scratch()
        dve_stt(rc[:], rs_[:], math.pi / 2, m3[:], ALU.add, ALU.add, [rsb, m3b], [rcb])
        PI_S = 3.1415925
        rs2, rs2b = scratch()
        dve_ts(rs2[:], rs_[:], -PI_S, PI_S, ALU.max, ALU.min, [rsb], [rs2b])
        rc2, rc2b = scratch()
        dve_ts(rc2[:], rc[:], -PI_S, PI_S, ALU.max, ALU.min, [rcb], [rc2b])
        act(sinT[:], rs2[:], AF.Sin, [rs2b, ropeB], [sinB], scale=ropec[:, 1:2])
        act(cosT[:], rc2[:], AF.Sin, [rc2b], [cosB])

    def mixer(l, seq, it, skip_norm=False):
        o = l * NCOL
        d0 = l * 8
        def cons_a(rs, rsb):
            for c in range(8):
                dve_stt(hT[:, c, :], xbuf[:, c, :], cv[:, o + C_NMIX + c:o + C_NMIX + c + 1], rs[:], ALU.mult, ALU.mult,
                        [xB[c], cvB, rsb], [hB[c]])
        if not skip_norm:
            multi_norm([([(xbuf[:, c, :], [xB[c]]) for c in range(8)], D, EPS, cons_a)])
        if l == 0 and it == 0 and seq == 0:
            DBG('h0', hT[:], [128, 8, TT], BF16, hB)
        for j in range(5):
            psa, pba = proj_unit(hT, hB)
            psb_, pbb = proj_unit(hT, hB)
            t1, t1b = scratch()
            dve_tt(t1[:], psa[:], cosT[:], ALU.mult, [pba, cosB], [t1b])
            t2, t2b = scratch()
            dve_tt(t2[:], psb_[:], sinT[:], ALU.mult, [pbb, sinB], [t2b])
            if j < 4:
                pool_tt(qT[:, j, :], t1[:], t2[:], ALU.add, [t1b, t2b], [qB[j]])
            else:
                pool_tt(kT[l][:, 128:128 + TT], t1[:], t2[:], ALU.add, [t1b, t2b], [kB[l]])
        w, wb = next_unit()
        ps, pb = mmps()
        for blk in range(4):
            mm_group(ps[:, blk * 128:(blk + 1) * 128], pb,
                     [(hT[:, kc, blk * 128:(blk + 1) * 128], w[:, kc * 128:(kc + 1) * 128], [wb, hB[kc]]) for kc in range(8)])
        psv = ps[:].rearrange("p (b k d) -> p b k d", b=4, k=2, d=64)
        act(vaug[l][:, 1:5, 0, 0:64], psv[:, :, 0, :], AF.Copy, [pb], [vB[l]])
        act(vaug[l][:, 1:5, 1, 64:128], psv[:, :, 1, :], AF.Copy, [pb], [vB[l]])
        GK = 0.7978845608028654 * 0.044715
        gwh = {}

        def b2_cb(ch):
            ps, pb = proj_unit(hT, hB)
            act(cbs[:, ch, :], ps[:], AF.Copy, [pb], [cbB[ch]])

        def b2_cxcc(ch):
            ps, pb = proj_unit(hT, hB)
            cx, cxb = scratch()
            act(cx[:], ps[:], AF.Copy, [pb], [cxb])
            ps2, pb2 = proj_unit(hT, hB)
            pool_copy(prodbuf[:, ch, 0:2], prodc[l][:, ch, :], [pcB[l][ch]], [prB[ch]])
            dve_tt(prodbuf[:, ch, 2:TT + 2], ps2[:], cx[:], ALU.mult, [pb2, cxb], [prB[ch]])
            pool_copy(prodc[l][:, ch, :], prodbuf[:, ch, TT:TT + 2], [prB[ch]], [pcB[l][ch]])

        def b2_lx(ch):
            ps, pb = proj_unit(hT, hB)
            pool_copy(lxbuf[:, ch, 0:3], lxc[l][:, ch, :], [lcB[l][ch]], [lxB[ch]])
            act(lxbuf[:, ch, 3:TT + 3], ps[:], AF.Copy, [pb], [lxB[ch]])
            pool_copy(lxc[l][:, ch, :], lxbuf[:, ch, TT:TT + 3], [lxB[ch]], [lcB[l][ch]])

        def b2_lg(ch):
            ps, pb = proj_unit(hT, hB)
            x2, x2b = scratch()
            act(x2[:], ps[:], AF.Square, [pb], [x2b])
            lgs, lgb = scratch()
            act(lgs[:], ps[:], AF.Copy, [pb], [lgb])
            zp, zpb = scratch()
            dve_stt(zp[:], x2[:], 1.0 / 0.044715, lgs[:], ALU.add, ALU.mult, [x2b, lgb], [zpb])
            th, thb = scratch()
            act(th[:], zp[:], AF.Tanh, [zpb], [thb], scale=GK)
            dve_stt(glb[:, ch, :], th[:], 1.0, lgs[:], ALU.add, ALU.mult, [thb, lgb], [glB[ch]])

        def b2_gates():
            gwh["w"], gwh["b"] = next_unit()

        b2 = [lambda: b2_cb(0), lambda: b2_cb(1), lambda: b2_cxcc(0), lambda: b2_cxcc(1),
              lambda: b2_lx(0), lambda: b2_lx(1), lambda: b2_lg(0), lambda: b2_lg(1), b2_gates]
        tiT = [gvb[0], gvb[1]]; tiBf = [gvB[0], gvB[1]]
        aaT = [gvb[2], gvb[3]]; aaBf = [gvB[2], gvB[3]]
        wqT = [ubuf[0], ubuf[1]]; wqBf = [ubB[0], ubB[1]]
        trT = [gvb[4], gvb[5]]; trBf = [gvB[4], gvB[5]]
        def gn_consumer(chunks):
            def cons(rs, rsb):
                for j in chunks:
                    dve_stt(hT[:, j, :], mixbuf[:, j, :], cv[:, o + C_GN + j:o + C_GN + j + 1], rs[:], ALU.mult, ALU.mult,
                            mixdeps(j) + [cvB, rsb], [hB[j]])
            return cons

        def st_sconv(ch):
            wc = lambda j: cv[:, o + C_SCW + j * 2 + ch:o + C_SCW + j * 2 + ch + 1]
            t0, t0b = scratch()
            pool_ts(t0[:], prodbuf[:, ch, 2:TT + 2], wc(2), 0.0, ALU.mult, ALU.add, [prB[ch], cvB], [t0b])
            t1, t1b = scratch()
            dve_stt(t1[:], prodbuf[:, ch, 1:TT + 1], wc(1), t0[:], ALU.mult, ALU.add, [prB[ch], cvB, t0b], [t1b])
            t2, t2b = scratch()
            dve_stt(t2[:], prodbuf[:, ch, 0:TT], wc(0), t1[:], ALU.mult, ALU.add, [prB[ch], cvB, t1b], [t2b])
            dve_tt(mixbuf[:, 4 + ch, :], t2[:], cbs[:, ch, :], ALU.mult, [t2b, cbB[ch]], mixdeps(4 + ch))

        def st_lru_conv():
            for ch in range(2):
                wc = lambda j: cv[:, o + C_LCW + j * 2 + ch:o + C_LCW + j * 2 + ch + 1]
                t0, t0b = scratch()
                pool_ts(t0[:], lxbuf[:, ch, 3:TT + 3], wc(3), cv[:, o + C_LCB + ch:o + C_LCB + ch + 1], ALU.mult, ALU.add,
                        [lxB[ch], cvB], [t0b])
                t1, t1b = scratch()
                dve_stt(t1[:], lxbuf[:, ch, 2:TT + 2], wc(2), t0[:], ALU.mult, ALU.add, [lxB[ch], cvB, t0b], [t1b])
                t2, t2b = scratch()
                dve_stt(t2[:], lxbuf[:, ch, 1:TT + 1], wc(1), t1[:], ALU.mult, ALU.add, [lxB[ch], cvB, t1b], [t2b])
                dve_stt(lru_in[:, ch, :], lxbuf[:, ch, 0:TT], wc(0), t2[:], ALU.mult, ALU.add, [lxB[ch], cvB, t2b], [liB[ch]])
                act(lrub[:, ch, :], lru_in[:, ch, :], AF.Copy, [liB[ch]], [lbB[ch]])

        def st_lru_gates():
            pss = []
            for ch in range(2):
                psr, pbr = mmps()
                mm_group(psr[:], pbr, [(gwh['w'][:, ch * 128:(ch + 1) * 128], lrub[:, ch, :], [gwh['b'], lbB[ch]])])
                act(trT[ch][:], psr[:], AF.Tanh, [pbr, derB], [trBf[ch]], bias=der[:, d0 + ch:d0 + ch + 1], scale=0.5)
                psi, pbi = mmps()
                mm_group(psi[:], pbi, [(gwh['w'][:, (2 + ch) * 128:(3 + ch) * 128], lrub[:, ch, :], [gwh['b'], lbB[ch]])])
                act(tiT[ch][:], psi[:], AF.Tanh, [pbi, derB], [tiBf[ch]], bias=der[:, d0 + 2 + ch:d0 + 3 + ch], scale=0.5)
            for ch in range(2):
                act(aaT[ch][:], trT[ch][:], AF.Exp, [trBf[ch], derB], [aaBf[ch]],
                    bias=der[:, d0 + 4 + ch:d0 + 5 + ch], scale=der[:, d0 + 4 + ch:d0 + 5 + ch])
                a2, a2b = scratch()
                act(a2[:], trT[ch][:], AF.Exp, [trBf[ch], derB], [a2b],
                    bias=der[:, d0 + 6 + ch:d0 + 7 + ch], scale=der[:, d0 + 6 + ch:d0 + 7 + ch])
                dve_ts(wqT[ch][:, 0:TT], a2[:], -0.25, 0.25, ALU.mult, ALU.add, [a2b], [wqBf[ch]])
            for ch in range(2):
                act(wqT[ch][:, 0:TT], wqT[ch][:, 0:TT], AF.Sqrt, [wqBf[ch]], [wqBf[ch]])

        def st_lru_tail(ch):
            u1, u1b = scratch()
            dve_stt(u1[:], tiT[ch][:], 1.0, lru_in[:, ch, :], ALU.add, ALU.mult, [tiBf[ch], liB[ch]], [u1b])
            uu, uub = scratch()
            dve_tt(uu[:], u1[:], wqT[ch][:, 0:TT], ALU.mult, [u1b, wqBf[ch]], [uub])
            hh, hhb = scratch()
            TR.op("dve", lambda e, hh=hh, aa=aaT[ch], uu=uu, ch=ch: e.tensor_tensor_scan(
                out=hh[:], data0=aa[:], data1=uu[:], initial=hstate[l][:, ch:ch + 1], op0=ALU.mult, op1=ALU.add),
                [aaBf[ch], uub, hsB[l][ch]], [hhb])
            dve_tt(mixbuf[:, 6 + ch, :], hh[:], glb[:, ch, :], ALU.mult, [hhb, glB[ch]], mixdeps(6 + ch))
            dve_copy(hstate[l][:, ch:ch + 1], hh[:, TT - 1:TT], [hhb], [hsB[l][ch]])

        def st_side_norms():
            multi_norm([([(mixbuf[:, j, :], mixdeps(j)) for j in (4, 5)], 256, EPS, gn_consumer((4, 5))),
                        ([(mixbuf[:, j, :], mixdeps(j)) for j in (6, 7)], 256, 4 * EPS, gn_consumer((6, 7)))])

        def _two(a, b):
            def f():
                a()
                b()
            return f
        side = [None, _two(lambda: st_sconv(0), lambda: st_sconv(1)), st_lru_conv, None, st_lru_gates,
                lambda: st_lru_tail(0), lambda: st_lru_tail(1), st_side_norms]
        iters = [(c, kv) for c in range(8) for kv in range(2)]
        ctx = {}

        def geom(c):
            return c // 2, (c % 2 == 0), (it * 8 + c >= 2)

        def emit_S(i):
            c, kv = iters[i]
            b, even, prev_valid = geom(c)
            kp = slice(kv * 64, kv * 64 + 64)
            k2 = st["P"] % 2
            k3 = st["P"] % 2
            st["P"] += 1
            Sps, Sb = PS[4 + k2], psB[4 + k2]
            ctx[i] = (Sps, Sb, PS[6 + k2], psB[6 + k2], Pbuf[k3], PB[k3])
            qv = qT[kp, 0:4, 64 * c:64 * c + 64]
            if prev_valid:
                mm_group(Sps[:, 0:256], Sb, [(kT[l][kp, 128 * b:128 * b + 128], qv, [kB[l]] + qB)])
            mcur = 64 if even else 128
            mm_group(Sps[0:mcur, 256:512], Sb, [(kT[l][kp, 128 * (b + 1):128 * (b + 1) + mcur], qv, [kB[l]] + qB)])

        def emit_exp(i):
            c, kv = iters[i]
            b, even, prev_valid = geom(c)
            Sps, Sb, Ops, Ob, P, Pb_ = ctx[i]
            if prev_valid:
                rows = slice(0, 128) if even else slice(64, 128)
                act(P[rows, 0:256], Sps[rows, 0:256], AF.Exp, [Sb], [Pb_], scale=0.125)
            rows = slice(0, 64) if even else slice(0, 128)
            act(P[rows, 256:512], Sps[rows, 256:512], AF.Exp, [Sb], [Pb_], scale=0.125)

        def emit_O(i):
            c, kv = iters[i]
            b, even, prev_valid = geom(c)
            Sps, Sb, Ops, Ob, P, Pb_ = ctx[i]
            items = []
            if prev_valid:
                if even:
                    items.append((vaug[l][:, b, kv, :], P[:, 0:256], [vB[l], Pb_]))
                else:
                    items.append((vaug[l][64:128, b, kv, :], P[64:128, 0:256], [vB[l], Pb_]))
            if even:
                items.append((vaug[l][0:64, b + 1, kv, :], P[0:64, 256:512], [vB[l], Pb_]))
            else:
                items.append((vaug[l][:, b + 1, kv, :], P[:, 256:512], [vB[l], Pb_]))
            so = l * 512 + kv * 256
            items.append((sinkL[0:1, kv, :], sinkRb[0:1, so:so + 256], [sLB, sRB]))
            mm_group(Ops[:, 0:256], Ob, items)

        def emit_evac(i):
            c, kv = iters[i]
            Sps, Sb, Ops, Ob, P, Pb_ = ctx[i]
            orow = slice(0, 64) if kv == 0 else slice(64, 128)
            drow = slice(64, 128) if kv == 0 else slice(0, 64)
            dve_copy(mixbuf[orow, 0:4, 64 * c:64 * c + 64], Ops[orow, 0:256].rearrange("p (g q) -> p g q", g=4),
                     [Ob], [aoB[c]])
            dve_copy(denall[orow, c, :, :], Ops[drow, 0:256].rearrange("p (g q) -> p g q", g=4), [Ob], [denB[c]])

        emit_S(0)
        for i, (c, kv) in enumerate(iters):
            emit_exp(i)
            if i + 1 < len(iters):
                emit_S(i + 1)
            if i < len(b2):
                b2[i]()
            emit_O(i)
            emit_evac(i)
            if kv == 1 and side[c] is not None:
                side[c]()
        dflat = denall[:].rearrange("p c g q -> p (c g q)")
        act(dflat, dflat, AF.Ln, denB, denB)
        act(dflat, dflat, AF.Exp, denB, denB, scale=-1.0)
        for j in range(4):
            mv = mixbuf[:, j, :].rearrange("p (c q) -> p c q", c=8)
            dve_tt(mv, mv, denall[:, :, j, :], ALU.mult, mixdeps(j) + denB, mixdeps(j))
        if l == 0 and it == 0 and seq == 0:
            DBG('qT', qT[:], [128, 4, TT], BF16, qB)
            DBG('kT', kT[0][:], [128, 128 + TT], BF16, [kB[0]])
            DBG('vaug', vaug[0][:], [128, 5, 2, 128], BF16, [vB[0]])
            DBG('cbs', cbs[:], [128, 2, TT], F32, cbB)
            DBG('prod', prodbuf[:], [128, 2, TT + 2], F32, prB)
            DBG('lxb', lxbuf[:], [128, 2, TT + 3], F32, lxB)
            DBG('gl', glb[:], [128, 2, TT], F32, glB)
            DBG('cos', cosT[:], [128, TT], F32, [cosB])
            DBG('sin', sinT[:], [128, TT], F32, [sinB])
        if l == 0 and it == 0 and seq == 0:
            DBG('mix', mixbuf[:], [128, 8, TT], F32, mixB + aoB)
            DBG('lru_in', lru_in[:], [128, 2, TT], F32, liB)
        pool_copy(kT[l][:, 0:128], kT[l][:, TT:TT + 128], [kB[l]], [kB[l]])
        pool_copy(vaug[l][:, 0, :, :], vaug[l][:, 4, :, :], [vB[l]], [vB[l]])
        multi_norm([([(mixbuf[:, j, :], mixdeps(j)) for j in (0, 1, 2, 3)], 512, EPS, gn_consumer((0, 1, 2, 3)))])
        for n in range(8):
            ps, pb = proj_unit(hT, hB)
            dve_tt(xbuf[:, n, :], ps[:], xbuf[:, n, :], ALU.add, [pb, xB[n]], [xB[n]])
            stats_accum(n)
        st["pre_stats"] = True

    def ffn(l, seq, it, early=None):
        o = l * NCOL
        if l == 0 and it == 0 and seq == 0:
            DBG('x_mix', xbuf[:], [128, 8, TT], F32, xB)
        def cons_h(rs, rsb):
            for c in range(8):
                dve_stt(hT[:, c, :], xbuf[:, c, :], cv[:, o + C_NFFN + c:o + C_NFFN + c + 1], rs[:], ALU.mult, ALU.mult,
                        [xB[c], cvB, rsb], [hB[c]])
        multi_norm([([(xbuf[:, c, :], [xB[c]]) for c in range(8)], D, EPS, cons_h)])
        st["mmset"] = "ffn"
        pending = None

        def finish_pair(jj, G, V):
            sg, sgb = scratch()
            act(sg[:], gvb[G][:], AF.Silu, [gvB[G]], [sgb])
            dve_tt(actT[:, jj, :], sg[:], gvb[V][:], ALU.mult, [sgb, gvB[V]], [actB[jj]])

        for jj in range(22):
            pair = []
            for s in range(2):
                uidx = jj * 2 + s
                ps, pb = proj_unit(hT, hB)
                ub, ubb = ubuf[st["ub"] % 4], ubB[st["ub"] % 4]
                st["ub"] += 1
                wc = lambda j: cv[:, o + C_FCW + j * 44 + uidx:o + C_FCW + j * 44 + uidx + 1]
                pool_copy(ub[:, 0:2], ucarry[l][:, uidx, :], [ucB[l][uidx]], [ubb])
                act(ub[:, 2:TT + 2], ps[:], AF.Copy, [pb], [ubb])
                t0, t0b = scratch()
                act(t0[:], ps[:], AF.Identity, [pb, cvB], [t0b], bias=cv[:, o + C_FCB + uidx:o + C_FCB + uidx + 1], scale=wc(2))
                pool_copy(ucarry[l][:, uidx, :], ub[:, TT:TT + 2], [ubb], [ucB[l][uidx]])
                t1, t1b = scratch()
                dve_stt(t1[:], ub[:, 1:TT + 1], wc(1), t0[:], ALU.mult, ALU.add, [ubb, cvB, t0b], [t1b])
                gi = (jj % 3) * 2 + s
                dve_stt(gvb[gi][:], ub[:, 0:TT], wc(0), t1[:], ALU.mult, ALU.add, [ubb, cvB, t1b], [gvB[gi]])
                pair.append(gi)
                if s == 0 and pending is not None:
                    finish_pair(*pending)
                    pending = None
            pending = (jj, pair[0], pair[1])
        finish_pair(*pending)
        if early is not None:
            nseq, nit, nxb, nxbl = early
            early_next_stats(nxb, nxbl)
            cossin_tables(nseq, nit)
        for m in range(8):
            ps, pb = mmps()
            for gq in range(3):
                w, wb = next_unit()
                nf = 8 if gq < 2 else 6
                for fi in range(nf):
                    f = gq * 8 + fi
                    first = (f == 0)
                    last = (f == 21)
                    TR.op("pe", (lambda e, ps=ps, w=w, fi=fi, f=f, first=first, last=last:
                                 e.matmul(ps[:], lhsT=w[:, fi * 128:(fi + 1) * 128], rhs=actT[:, f, :], start=first, stop=last)),
                          [wb, actB[f]], [pb], inc=(fi == nf - 1))
            dve_tt(xbuf[:, m, :], ps[:], xbuf[:, m, :], ALU.add, [pb, xB[m]], [xB[m]])
            stats_accum(m)
        st["mmset"] = "mix"
        st["pre_stats"] = True

    out_toks = []
    tiles = [(seq, it) for seq in range(NSEQ) for it in range(NT)]

    def load_x(k):
        sq_, it_ = tiles[k]
        xb_, xbl_ = xbufs[k % 2], xBs[k % 2]
        TR.dma(lambda e, sq_=sq_, t0=it_ * TT, xb_=xb_: e.dma_start(
            out=xb_[:], in_=xT[sq_, :, t0:t0 + TT].rearrange("(c p) t -> p c t", p=128)), f"xin{k % 2}", (), xbl_)

    for k, (seq, it) in enumerate(tiles):
        xbuf, xB = xbufs[k % 2], xBs[k % 2]
        has_next = k + 1 < len(tiles)
        if it == 0:
            TR.new_epoch()
            for l in range(L):
                for ch in range(2):
                    pool_memset(prodc[l][:, ch, :], 0.0, [pcB[l][ch]])
                    pool_memset(lxc[l][:, ch, :], 0.0, [lcB[l][ch]])
                    pool_memset(hstate[l][:, ch:ch + 1], 0.0, [hsB[l][ch]])
                pool_memset(ucarry[l][:], 0.0, ucB[l])
        t0 = it * TT
        if k == 0:
            load_x(0)
            cossin_tables(seq, it)
        if has_next:
            load_x(k + 1)
        for l in range(L):
            mixer(l, seq, it, skip_norm=(l == 0 and k > 0))
            early = None
            if l == L - 1 and has_next:
                early = (tiles[k + 1][0], tiles[k + 1][1], xbufs[(k + 1) % 2], xBs[(k + 1) % 2])
            ffn(l, seq, it, early=early)
            if l == 0 and it == 0 and seq == 0:
                DBG('x_ffn', xbuf[:], [128, 8, TT], F32, xB)
                DBG('act', actT[:], [128, 22, TT], BF16, actB)
        o = (L - 1) * NCOL

        def cons_f(rs, rsb, o=o, xb_=xbuf, xbl_=xB):
            for c in range(8):
                dve_stt(mixbuf[:, c, :], xb_[:, c, :], cv[:, o + C_FIN + c:o + C_FIN + c + 1], rs[:], ALU.mult, ALU.mult,
                        [xbl_[c], cvB, rsb], mixdeps(c))

        groups = [([(xbuf[:, c, :], [xB[c]]) for c in range(8)], D, EPS, cons_f)]
        if has_next:
            def cons_n(rs, rsb, xb_=xbufs[(k + 1) % 2], xbl_=xBs[(k + 1) % 2]):
                for c in range(8):
                    dve_stt(hT[:, c, :], xb_[:, c, :], cv[:, C_NMIX + c:C_NMIX + c + 1], rs[:], ALU.mult, ALU.mult,
                            [xbl_[c], cvB, rsb], [hB[c]])
            groups.append((None, D, EPS, cons_n))
        multi_norm(groups)
        rd_all = []
        for c in range(8):
            rd_all += mixdeps(c)
        tok = TR.dma(lambda e, seq=seq, t0=t0: e.dma_start(
            out=outT[seq, :, t0:t0 + TT].rearrange("(c p) t -> p c t", p=128), in_=mixbuf[:]), "oout", rd_all, ())
        out_toks.append(tok)
    TR.final_wait("sp", [out_toks[-1]])

    sems = {}
    for key in TR.cnt:
        nm = "s_" + "_".join(str(k) for k in key)
        sems[key] = es.enter_context(nc.semaphore(nm))
    block = es.enter_context(nc.Block())

    def run(eng_name, e):
        for waits, fn, inckey, incval in TR.plan[eng_name]:
            for key, val in waits:
                e.wait_ge(sems[key], val)
            if fn is None:
                continue
            ins = fn(e)
            if inckey is not None:
                ins.then_inc(sems[inckey], incval)

    @block.sync
    def _(e):
        run("sp", e)

    @block.tensor
    def _(e):
        run("pe", e)

    @block.scalar
    def _(e):
        run("act", e)

    @block.vector
    def _(e):
        run("dve", e)

    @block.gpsimd
    def _(e):
        run("pool", e)

    es.close()
    stats = {e: len(TR.plan[e]) for e in ENGS}
    stats["sems"] = len(sems)
    stats["maxcnt"] = max(TR.cnt.values())
    stats['dbg'] = list(dbg_outs)
    return nc, stats


def run_cores(inputs, NSEQ, S, n_cores=8, debug=False):
    wst, cvec, sinkr, ropec = _prep_weights(inputs)
    x = np.asarray(inputs["x"], np.float32)
    pos = np.asarray(inputs["positions"], np.int32)
    nc, stats = build_program(NSEQ, S, debug)
    in_maps = []
    for c in range(n_cores):
        xs = x[c * NSEQ:(c + 1) * NSEQ]
        in_maps.append({
            "xT": np.ascontiguousarray(xs.transpose(0, 2, 1)),
            "pos": np.ascontiguousarray(pos[c * NSEQ:(c + 1) * NSEQ]),
            "wst": wst, "cvec": cvec, "ropec": ropec, "sinkr": sinkr, "ident": np.eye(128, dtype=np.float32),
        })
    res = run_bass_kernel_spmd(nc, in_maps, core_ids=list(range(n_cores)))
    outs = [np.asarray(r["outT"]).transpose(0, 2, 1) for r in res.results]
    if debug:
        stats['dbgvals'] = [{n: np.asarray(r['dbg_' + n]) for n in stats['dbg']} for r in res.results]
    return np.ascontiguousarray(np.concatenate(outs, axis=0)).astype(np.float32), stats


def kernel(**inputs):
    out, _ = run_cores(inputs, NSEQ=4, S=2048)
    return out
```

```python
import math
from contextlib import ExitStack

import numpy as np
import concourse.bass as bass
import concourse.mybir as mybir
from concourse.bass_utils import run_bass_kernel_spmd

F32 = mybir.dt.float32
BF16 = mybir.dt.bfloat16
I32 = mybir.dt.int32
AF = mybir.ActivationFunctionType
ALU = mybir.AluOpType

D = 1024
L = 2
TT = 512
NU = 98
NCOL = 230
NSLOT = 8
NSC = 10
EPS = 1e-6
ENGS = ("sp", "pe", "act", "dve", "pool")

C_NMIX, C_NFFN, C_GN, C_SCW, C_LCW, C_LCB, C_BA, C_BX, C_LAM, C_FCW, C_FCB, C_FIN = 0, 8, 16, 24, 30, 38, 40, 42, 44, 46, 178, 222


def _unit_from_cols(W, cols):
    cols = np.asarray(cols)
    sel = np.zeros((W.shape[0], 128), np.float32)
    ok = cols >= 0
    sel[:, ok] = W[:, cols[ok]]
    kc = W.shape[0] // 128
    u = np.zeros((128, 8, 128), np.float32)
    u[:, :kc, :] = sel.reshape(kc, 128, 128).transpose(1, 0, 2)
    return u.reshape(128, 1024)


def _mix_perm():
    perm = np.zeros(1024, np.int64)
    for j in range(4):
        for p in range(128):
            perm[j * 128 + p] = j * 64 + p if p < 64 else (j + 4) * 64 + (p - 64)
    perm[512:] = np.arange(512, 1024)
    return perm


def _prep_weights(inp):
    perm = _mix_perm()
    wst = np.zeros((L * NU, 128, 1024), np.float32)
    cvec = np.zeros((128, L * NCOL), np.float32)
    sinkr = np.zeros((1, L * 512), np.float32)
    for l in range(L):
        Win = np.asarray(inp["w_in"][l], np.float32)
        units = []

        def head_cols(base, h):
            return [base + h * 64 + d for d in range(64)]

        def swap_cols(base, h):
            out = []
            for d in range(64):
                if d < 8:
                    out.append(base + h * 64 + d + 8)
                elif d < 16:
                    out.append(base + h * 64 + d - 8)
                else:
                    out.append(-1)
            return out

        for j in range(4):
            units.append(_unit_from_cols(Win, head_cols(0, j) + head_cols(0, j + 4)))
            units.append(_unit_from_cols(Win, swap_cols(0, j) + swap_cols(0, j + 4)))
        units.append(_unit_from_cols(Win, head_cols(512, 0) + head_cols(512, 1)))
        units.append(_unit_from_cols(Win, swap_cols(512, 0) + swap_cols(512, 1)))
        units.append(_unit_from_cols(Win, list(range(640, 768))))
        for ch in range(2):
            units.append(_unit_from_cols(Win, list(range(768 + ch * 128, 768 + (ch + 1) * 128))))
        for ch in range(2):
            units.append(_unit_from_cols(Win, list(range(1280 + ch * 128, 1280 + (ch + 1) * 128))))
            units.append(_unit_from_cols(Win, list(range(1024 + ch * 128, 1024 + (ch + 1) * 128))))
        for ch in range(2):
            units.append(_unit_from_cols(Win, list(range(1536 + ch * 128, 1536 + (ch + 1) * 128))))
        for ch in range(2):
            units.append(_unit_from_cols(Win, list(range(1792 + ch * 128, 1792 + (ch + 1) * 128))))
        g = np.zeros((128, 1024), np.float32)
        for gi, W in enumerate((inp["lru_wa"][l], inp["lru_wx"][l])):
            W = np.asarray(W, np.float32)
            for ch in range(2):
                col0 = (gi * 2 + ch) * 128
                g[0:64, col0:col0 + 64] = W[2 * ch]
                g[64:128, col0 + 64:col0 + 128] = W[2 * ch + 1]
        units.append(g)
        Wo = np.asarray(inp["w_out"][l], np.float32)[perm, :]
        for n in range(8):
            units.append(_unit_from_cols(Wo, list(range(n * 128, (n + 1) * 128))))
        Wu = np.asarray(inp["ffn_w_up"][l], np.float32)
        for jj in range(22):
            for s in range(2):
                units.append(_unit_from_cols(Wu, list(range(s * 2816 + jj * 128, s * 2816 + (jj + 1) * 128))))
        Wd = np.asarray(inp["ffn_w_down"][l], np.float32)
        for m in range(8):
            for gq in range(3):
                u = np.zeros((128, 8, 128), np.float32)
                for fi in range(8):
                    f = gq * 8 + fi
                    if f < 22:
                        u[:, fi, :] = Wd[f * 128:(f + 1) * 128, m * 128:(m + 1) * 128]
                units.append(u.reshape(128, 1024))
        assert len(units) == NU
        wst[l * NU:(l + 1) * NU] = np.stack(units)

        cv = cvec[:, l * NCOL:(l + 1) * NCOL]

        def colmat(v):
            return np.asarray(v, np.float32).reshape(-1, 128).T

        cv[:, C_NMIX:C_NMIX + 8] = colmat(inp["norm_mix_g"][l])
        cv[:, C_NFFN:C_NFFN + 8] = colmat(inp["norm_ffn_g"][l])
        gn = np.concatenate([np.asarray(inp["gn_attn_g"][l]), np.asarray(inp["gn_sconv_g"][l]),
                             np.asarray(inp["gn_lru_g"][l])]).astype(np.float32)[perm]
        cv[:, C_GN:C_GN + 8] = colmat(gn)
        for j in range(3):
            cv[:, C_SCW + j * 2:C_SCW + j * 2 + 2] = colmat(inp["sconv_w"][l][j])
        for j in range(4):
            cv[:, C_LCW + j * 2:C_LCW + j * 2 + 2] = colmat(inp["lru_conv_w"][l][j])
        cv[:, C_LCB:C_LCB + 2] = colmat(inp["lru_conv_b"][l])
        cv[:, C_BA:C_BA + 2] = colmat(np.asarray(inp["lru_ba"][l]).reshape(-1))
        cv[:, C_BX:C_BX + 2] = colmat(np.asarray(inp["lru_bx"][l]).reshape(-1))
        cv[:, C_LAM:C_LAM + 2] = colmat(inp["lru_lambda"][l])
        fperm = np.zeros(5632, np.int64)
        for uidx in range(44):
            jj, s = uidx // 2, uidx % 2
            fperm[uidx * 128:(uidx + 1) * 128] = s * 2816 + jj * 128 + np.arange(128)
        for j in range(3):
            cv[:, C_FCW + j * 44:C_FCW + (j + 1) * 44] = colmat(np.asarray(inp["ffn_conv_w"][l][j])[fperm])
        cv[:, C_FCB:C_FCB + 44] = colmat(np.asarray(inp["ffn_conv_b"][l])[fperm])
        cv[:, C_FIN:C_FIN + 8] = colmat(inp["final_norm_g"])
        sk = np.asarray(inp["attn_sinks"][l], np.float32)
        for kv in range(2):
            for gq in range(4):
                o = l * 512 + kv * 256 + gq * 64
                sinkr[0, o:o + 64] = sk[kv * 4 + gq]
    ropec = np.zeros((128, 2), np.float32)
    ropec[:, 1] = 1.0
    for p in range(128):
        d = p % 64
        if d < 16:
            ropec[p, 0] = np.float32(500000.0) ** np.float32(-(2 * (d % 8)) / 16.0)
            ropec[p, 1] = -1.0 if d < 8 else 1.0
    return wst, cvec, sinkr, ropec


class Buf:
    __slots__ = ("name", "w", "r")

    def __init__(self, name):
        self.name = name
        self.w = None
        self.r = {}


class Tracker:
    def __init__(self):
        self.plan = {e: [] for e in ENGS}
        self.cnt = {}
        self.seen = {e: {} for e in ENGS}
        self.epoch = 0

    def new_epoch(self):
        self.epoch += 1

    def _waits(self, eng, reads, writes):
        toks = []
        for b in reads:
            if b.w is not None:
                toks.append(b.w)
        for b in writes:
            if b.w is not None:
                toks.append(b.w)
            toks.extend(b.r.values())
        waits = []
        seen = self.seen[eng]
        for key, val in toks:
            if eng == "pe" and key[0] == "pe":
                continue
            if seen.get(key, 0) < val:
                seen[key] = val
                waits.append((key, val))
        best = {}
        for key, val in waits:
            best[key] = max(best.get(key, 0), val)
        return list(best.items())

    def _mark(self, rkey, tok, reads, writes):
        for b in reads:
            b.r[rkey] = tok
        for b in writes:
            b.w = tok
            b.r = {}

    def op(self, eng, fn, reads=(), writes=(), inc=True):
        waits = self._waits(eng, reads, writes)
        key = (eng, self.epoch)
        if inc:
            self.cnt[key] = self.cnt.get(key, 0) + 1
            tok = (key, self.cnt[key])
        else:
            tok = (key, self.cnt.get(key, 0) + 1)
        self.plan[eng].append((waits, fn, key if inc else None, 1))
        self._mark(eng, tok, reads, writes)
        return tok

    def dma(self, fn, semname, reads=(), writes=(), queue="sp"):
        waits = self._waits(queue, reads, writes)
        key = ("dma", semname)
        prev = self.cnt.get(key, 0)
        if prev and self.seen[queue].get(key, 0) < prev:
            self.seen[queue][key] = prev
            waits.append((key, prev))
        self.cnt[key] = prev + 16
        tok = (key, prev + 16)
        self.plan[queue].append((waits, fn, key, 16))
        self._mark(key, tok, reads, writes)
        return tok

    def final_wait(self, queue, toks):
        self.plan[queue].append((list(toks), None, None, 0))


def build_program(NSEQ, S, debug=False):
    NT = S // TT
    dbg_outs = {}
    nc = bass.Bass("TRN2", target_bir_lowering=False)
    xT = nc.dram_tensor("xT", [NSEQ, D, S], F32, kind="ExternalInput").ap()
    posd = nc.dram_tensor("pos", [NSEQ, S], I32, kind="ExternalInput").ap()
    wst = nc.dram_tensor("wst", [L * NU, 128, 1024], F32, kind="ExternalInput").ap()
    cvd = nc.dram_tensor("cvec", [128, L * NCOL], F32, kind="ExternalInput").ap()
    roped = nc.dram_tensor("ropec", [128, 2], F32, kind="ExternalInput").ap()
    sinkd = nc.dram_tensor("sinkr", [1, L * 512], F32, kind="ExternalInput").ap()
    identd = nc.dram_tensor("ident", [128, 128], F32, kind="ExternalInput").ap()
    outT = nc.dram_tensor("outT", [NSEQ, D, S], F32, kind="ExternalOutput").ap()
    wbf = nc.dram_tensor("wbf", [L * NU, 128, 1024], BF16, kind="Internal").ap()

    TR = Tracker()
    es = ExitStack()

    def sb(name, shape, dt):
        return es.enter_context(nc.sbuf_tensor(name, shape, dt))

    def bufs(name, n):
        return [Buf(f"{name}{i}") for i in range(n)]

    xbuf = sb("xbuf", [128, 8, TT], F32); xB = bufs("x", 8)
    xbuf_b = sb("xbuf_b", [128, 8, TT], F32); xB_b = bufs("xb_", 8)
    xbufs = [xbuf, xbuf_b]; xBs = [xB, xB_b]
    hT = sb("hT", [128, 8, TT], BF16); hB = bufs("h", 8)
    sqb = sb("sqb", [128, 8, TT], BF16); sqB = bufs("sq", 8)
    qT = sb("qT", [128, 4, TT], BF16); qB = bufs("q", 4)
    kT = [sb(f"kT{l}", [128, 128 + TT], BF16) for l in range(L)]; kB = bufs("k", L)
    vaug = [sb(f"vaug{l}", [128, 5, 2, 128], BF16) for l in range(L)]; vB = bufs("v", L)
    cbs = sb("cbs", [128, 2, TT], F32); cbB = bufs("cb", 2)
    prodbuf = sb("prodbuf", [128, 2, TT + 2], F32); prB = bufs("pr", 2)
    prodc = [sb(f"prodc{l}", [128, 2, 2], F32) for l in range(L)]; pcB = [bufs(f"pc{l}_", 2) for l in range(L)]
    lxbuf = sb("lxbuf", [128, 2, TT + 3], F32); lxB = bufs("lx", 2)
    lxc = [sb(f"lxc{l}", [128, 2, 3], F32) for l in range(L)]; lcB = [bufs(f"lc{l}_", 2) for l in range(L)]
    mixbuf = sb("mixbuf", [128, 8, TT], F32); mixB = bufs("mix", 8); aoB = bufs("ao", 8)
    SC = [sb(f"sc{i}", [128, TT], F32) for i in range(NSC)]; scB = bufs("sc", NSC)
    lru_in = sb("lru_in", [128, 2, TT], F32); liB = bufs("li", 2)
    lrub = sb("lrub", [128, 2, TT], BF16); lbB = bufs("lb", 2)
    glb = sb("glb", [128, 2, TT], F32); glB = bufs("gl", 2)
    hstate = [sb(f"hstate{l}", [128, 2], F32) for l in range(L)]; hsB = [bufs(f"hs{l}_", 2) for l in range(L)]
    Pbuf = [sb(f"P{i}", [128, 512], BF16) for i in range(2)]; PB = bufs("P", 2)
    denall = sb("denall", [128, 8, 4, 64], F32); denB = bufs("den", 8)
    cosT = sb("cosT", [128, TT], F32); cosB = Buf("cos")
    sinT = sb("sinT", [128, TT], F32); sinB = Buf("sin")
    posi = sb("posi", [128, TT], I32); posB = Buf("posi")
    ubuf = [sb(f"ub{i}", [128, TT + 2], F32) for i in range(4)]; ubB = bufs("ub", 4)
    ucarry = [sb(f"ucarry{l}", [128, 44, 2], F32) for l in range(L)]; ucB = [bufs(f"uc{l}_", 44) for l in range(L)]
    gvb = [sb(f"gv{i}", [128, TT], F32) for i in range(6)]; gvB = bufs("gv", 6)
    actT = sb("actT", [128, 22, TT], BF16); actB = bufs("act", 22)
    wring = [sb(f"wr{i}", [128, 1024], BF16) for i in range(NSLOT)]; wrB = bufs("wr", NSLOT)
    cv = sb("cv", [128, L * NCOL], F32); cvB = Buf("cv")
    ropec = sb("ropecs", [128, 2], F32); ropeB = Buf("ropec")
    der = sb("der", [128, L * 8], F32); derB = Buf("der")
    dtmp = sb("dtmp", [128, 8], F32); dtB = Buf("dtmp")
    sink32 = sb("sink32", [1, L * 512], F32); s32B = Buf("sink32")
    sinkRb = sb("sinkRb", [1, L * 512], BF16); sRB = Buf("sinkRb")
    sinkL = sb("sinkL", [1, 2, 128], BF16); sLB = Buf("sinkL")
    ones = sb("ones", [128, 128], BF16); onesB = Buf("ones")
    cpow = sb("cpow", [128, 4], F32); cpB = Buf("cpow")
    ident = sb("ident_sb", [128, 128], F32); identB = Buf("ident")
    onesf = sb("onesf", [128, 128], F32); onesfB = Buf("onesf")
    stat = [sb(f"stat{i}", [128, 8], F32) for i in range(2)]; statB = bufs("stat", 2)
    dg = [sb(f"dg{i}", [128, 4, 128], F32) for i in range(2)]; dgB = bufs("dg", 2)
    PS = [es.enter_context(nc.psum_tensor(f"ps{i}", [128, 512], F32)) for i in range(8)]
    psB = bufs("ps", 8)
    wbfB = bufs("wbf", L * NU)

    st = {"sc": 0, "mm": 0, "P": 0, "ub": 0, "stat": 0}

    def DBG(name, tile, shape, dt, bl):
        if not debug or name in dbg_outs:
            return
        d = nc.dram_tensor("dbg_" + name, list(shape), dt, kind="ExternalOutput").ap()
        dbg_outs[name] = d
        TR.dma(lambda e: e.dma_start(out=d, in_=tile), "dbg_" + name, bl, ())

    def scratch():
        i = st["sc"] % NSC
        st["sc"] += 1
        return SC[i], scB[i]

    MM_SETS = {"mix": [0, 1, 2], "ffn": [0, 1, 2, 4, 5, 6, 7]}
    st["mmset"] = "mix"

    def mmps():
        bs = MM_SETS[st["mmset"]]
        i = bs[st["mm"] % len(bs)]
        st["mm"] += 1
        return PS[i], psB[i]

    PSN, PSNB = PS[3], psB[3]

    def mixdeps(j):
        return [mixB[j]] + (aoB if j < 4 else [])

    def act(out, in_, func, reads, writes, bias=None, scale=None):
        kw = {}
        if bias is not None:
            kw["bias"] = bias
        if scale is not None:
            kw["scale"] = scale
        TR.op("act", lambda e: e.activation(out=out, in_=in_, func=func, **kw), reads, writes)

    def dve_tt(out, in0, in1, op, reads, writes):
        TR.op("dve", lambda e: e.tensor_tensor(out=out, in0=in0, in1=in1, op=op), reads, writes)

    def dve_stt(out, in0, scalar, in1, op0, op1, reads, writes):
        TR.op("dve", lambda e: e.scalar_tensor_tensor(out=out, in0=in0, scalar=scalar, in1=in1, op0=op0, op1=op1),
              reads, writes)

    def dve_ts(out, in0, s1, s2, op0, op1, reads, writes):
        if s2 is None:
            TR.op("dve", lambda e: e.tensor_scalar(out=out, in0=in0, scalar1=s1, scalar2=None, op0=op0), reads, writes)
        else:
            TR.op("dve", lambda e: e.tensor_scalar(out=out, in0=in0, scalar1=s1, scalar2=s2, op0=op0, op1=op1),
                  reads, writes)

    def dve_copy(out, in_, reads, writes):
        TR.op("dve", lambda e: e.tensor_copy(out=out, in_=in_), reads, writes)

    def pool_tt(out, in0, in1, op, reads, writes):
        TR.op("pool", lambda e: e.tensor_tensor(out=out, in0=in0, in1=in1, op=op), reads, writes)

    def pool_ts(out, in0, s1, s2, op0, op1, reads, writes):
        if s2 is None:
            TR.op("pool", lambda e: e.tensor_scalar(out=out, in0=in0, scalar1=s1, scalar2=None, op0=op0), reads, writes)
        else:
            TR.op("pool", lambda e: e.tensor_scalar(out=out, in0=in0, scalar1=s1, scalar2=s2, op0=op0, op1=op1),
                  reads, writes)

    def pool_copy(out, in_, reads, writes):
        TR.op("pool", lambda e: e.tensor_copy(out=out, in_=in_), reads, writes)

    def pool_memset(ap, val, writes):
        TR.op("pool", lambda e: e.memset(ap, val), (), writes)

    def mm_group(out, outB, items, extra_reads=()):
        n = len(items)
        for i, (lhsT, rhs, rb) in enumerate(items):
            last = i == n - 1
            TR.op("pe", (lambda e, lhsT=lhsT, rhs=rhs, i=i, last=last:
                         e.matmul(out, lhsT=lhsT, rhs=rhs, start=(i == 0), stop=last)),
                  list(rb) + list(extra_reads), [outB], inc=last)

    total_units = NSEQ * NT * L * NU
    wstate = {"issued": 0, "consumed": 0}

    def issue_load(n):
        u = n % (L * NU)
        s = n % NSLOT
        TR.dma(lambda e, u=u, s=s: e.dma_start(out=wring[s][:], in_=wbf[u]), f"w{s}", [wbfB[u]], [wrB[s]])

    def next_unit():
        n = wstate["consumed"]
        while wstate["issued"] < min(total_units, n + NSLOT):
            issue_load(wstate["issued"])
            wstate["issued"] += 1
        wstate["consumed"] += 1
        s = n % NSLOT
        return wring[s], wrB[s]

    def proj_unit(rhs_tile, rhs_bufs):
        w, wb = next_unit()
        ps, pb = mmps()
        mm_group(ps[:], pb, [(w[:, kc * 128:(kc + 1) * 128], rhs_tile[:, kc, :], [wb, rhs_bufs[kc]]) for kc in range(8)])
        return ps, pb

    TR.dma(lambda e: e.dma_start(out=cv[:], in_=cvd[:, :]), "c0", (), [cvB])
    TR.dma(lambda e: e.dma_start(out=ropec[:], in_=roped[:, :]), "c1", (), [ropeB])
    TR.dma(lambda e: e.dma_start(out=sink32[:], in_=sinkd[:, :]), "c2", (), [s32B])
    pool_memset(ones[:], 1.0, [onesB])
    pool_memset(cpow[:], -0.5, [cpB])
    pool_memset(onesf[:], 1.0, [onesfB])
    TR.dma(lambda e: e.dma_start(out=ident[:], in_=identd[:, :]), "c3", (), [identB])
    pool_memset(sinkL[:, 0, 0:64], 0.0, [sLB]); pool_memset(sinkL[:, 0, 64:128], 1.0, [sLB])
    pool_memset(sinkL[:, 1, 0:64], 1.0, [sLB]); pool_memset(sinkL[:, 1, 64:128], 0.0, [sLB])
    for l in range(L):
        pool_memset(vaug[l][:], 1.0, [vB[l]])
    act(sinkRb[:], sink32[:], AF.Exp, [s32B], [sRB])
    for l in range(L):
        o = l * NCOL
        d0 = l * 8
        dve_ts(der[:, d0:d0 + 2], cv[:, o + C_BA:o + C_BA + 2], 0.5, None, ALU.mult, None, [cvB], [derB])
        dve_ts(der[:, d0 + 2:d0 + 4], cv[:, o + C_BX:o + C_BX + 2], 0.5, None, ALU.mult, None, [cvB], [derB])
        act(dtmp[:, 0:2], cv[:, o + C_LAM:o + C_LAM + 2], AF.Exp, [cvB], [dtB], scale=-1.0)
        dve_ts(dtmp[:, 2:4], dtmp[:, 0:2], -0.25, 1.0 / 3.0, ALU.mult, ALU.add, [dtB], [dtB])
        dve_tt(dtmp[:, 4:6], dtmp[:, 2:4], dtmp[:, 0:2], ALU.mult, [dtB], [dtB])
        dve_ts(dtmp[:, 2:4], dtmp[:, 4:6], -1.0, 0.5, ALU.mult, ALU.add, [dtB], [dtB])
        dve_tt(dtmp[:, 4:6], dtmp[:, 2:4], dtmp[:, 0:2], ALU.mult, [dtB], [dtB])
        dve_ts(dtmp[:, 2:4], dtmp[:, 4:6], -1.0, 1.0, ALU.mult, ALU.add, [dtB], [dtB])
        dve_tt(dtmp[:, 4:6], dtmp[:, 2:4], dtmp[:, 0:2], ALU.mult, [dtB], [dtB])
        dve_ts(der[:, d0 + 6:d0 + 8], dtmp[:, 4:6], -8.0, None, ALU.mult, None, [dtB], [derB])
        dve_ts(der[:, d0 + 4:d0 + 6], dtmp[:, 4:6], -4.0, None, ALU.mult, None, [dtB], [derB])

    def st32_slot(i):
        t, tb = (mixbuf, lambda j: mixdeps(j)) if i < 4 else (xbuf, lambda j: [xB[j]])
        k = i % 4
        return t[:, 2 * k:2 * k + 2, :], tb(2 * k) + tb(2 * k + 1)

    def st16_slot(i):
        t, tb = (sqb, sqB) if i < 4 else (hT, hB)
        k = i % 4
        return t[:, 2 * k:2 * k + 2, :], [tb[2 * k], tb[2 * k + 1]]

    NPU = L * NU
    cast_eng = ["act", "dve", "act", "dve", "act", "pool", "dve", "act"]

    def pre_load(u):
        a32, b32 = st32_slot(u % 8)
        TR.dma(lambda e, u=u, a32=a32: e.dma_start(out=a32, in_=wst[u].rearrange("p (a t) -> p a t", a=2)),
               f"pin{u % 8}", (), b32)

    for u in range(min(8, NPU)):
        pre_load(u)
    for u in range(NPU):
        a32, b32 = st32_slot(u % 8)
        a16, b16 = st16_slot(u % 8)
        ce = cast_eng[u % 8]
        if ce == "act":
            act(a16, a32, AF.Copy, b32, b16)
        elif ce == "dve":
            dve_copy(a16, a32, b32, b16)
        else:
            pool_copy(a16, a32, b32, b16)
        TR.dma(lambda e, u=u, a16=a16: e.dma_start(out=wbf[u].rearrange("p (a t) -> p a t", a=2), in_=a16),
               f"pout{u % 8}", b16, [wbfB[u]])
        if u + 8 < NPU:
            pre_load(u + 8)

    PSNS = [Buf(f"psn{i}") for i in range(4)]

    accx = sb("accx", [128, 4], F32); accB = Buf("accx")

    def stats_square(n):
        act(sqb[:, n, :], xbuf[:, n, :], AF.Square, [xB[n]], [sqB[n]])

    def stats_reduce(n):
        for blk in range(4):
            TR.op("pe", (lambda e, n=n, blk=blk: e.matmul(PSN[:, 12 + blk:13 + blk], lhsT=sqb[:, n, blk * 128:(blk + 1) * 128],
                                                         rhs=ones[:, 0:1], start=True, stop=True)),
                  [sqB[n], onesB], [PSNS[3]], inc=(blk == 3))
        if n == 0:
            dve_copy(accx[:], PSN[:, 12:16], [PSNS[3]], [accB])
        else:
            dve_tt(accx[:], PSN[:, 12:16], accx[:], ALU.add, [PSNS[3], accB], [accB])

    def stats_accum(n, lag=2):
        stats_square(n)
        if n >= lag:
            stats_reduce(n - lag)
        if n == 7:
            for j in range(8 - lag, 8):
                stats_reduce(j)

    def multi_norm(groups):
        base = 0
        infos = []
        pre = st.pop("pre_stats", False)
        modes = []
        for gi, (src_list, nfeat, eps, consume) in enumerate(groups):
            if pre and gi == 0:
                modes.append("acc")
                continue
            if src_list is None:
                modes.append("early")
                continue
            modes.append("now")
            n = len(src_list)
            for i, (ap, bl) in enumerate(src_list):
                act(sqb[:, base + i, :], ap, AF.Square, bl, [sqB[base + i]])
            infos.append((gi, base, n))
            base += n
        for gi, b0, n in infos:
            for blk in range(4):
                mm_group(PSN[:, gi * 4 + blk:gi * 4 + blk + 1], PSNS[gi],
                         [(sqb[:, b0 + i, blk * 128:(blk + 1) * 128], ones[:, 0:1], [sqB[b0 + i], onesB]) for i in range(n)])
        ks = []
        for gi, (src_list, nfeat, eps, consume) in enumerate(groups):
            k = st["stat"] % 2
            st["stat"] += 1
            ks.append(k)
            if modes[gi] == "acc":
                dve_ts(stat[k][:, 0:4], accx[:], 1.0 / nfeat, eps, ALU.mult, ALU.add, [accB], [statB[k]])
            elif modes[gi] == "early":
                dve_ts(stat[k][:, 0:4], PSN[:, 8:12], 1.0 / nfeat, eps, ALU.mult, ALU.add, [PSNS[2]], [statB[k]])
            else:
                dve_ts(stat[k][:, 0:4], PSN[:, gi * 4:gi * 4 + 4], 1.0 / nfeat, eps, ALU.mult, ALU.add, [PSNS[gi]], [statB[k]])
        for k in ks:
            pool_tt(stat[k][:, 4:8], stat[k][:, 0:4], cpow[:, 0:4], ALU.pow, [statB[k], cpB], [statB[k]])
        for k in ks:
            for blk in range(4):
                dve_ts(dg[k][:, blk, :], ident[:], stat[k][:, 4 + blk:5 + blk], None, ALU.mult, None, [identB, statB[k]], [dgB[k]])
        outs = []
        for k in ks:
            ps, pb = mmps()
            mm_group(ps[:], pb, [(onesf[:], dg[k][:].rearrange("p b t -> p (b t)"), [onesfB, dgB[k]])])
            outs.append((ps, pb))
        for (src_list, nfeat, eps, consume), (ps, pb) in zip(groups, outs):
            consume(ps, pb)

    def early_next_squares(xb, xbl):
        for c in range(8):
            act(sqb[:, c, :], xb[:, c, :], AF.Square, [xbl[c]], [sqB[c]])

    def early_next_stats(xb, xbl):
        for blk in range(4):
            mm_group(PSN[:, 8 + blk:9 + blk], PSNS[2],
                     [(sqb[:, c, blk * 128:(blk + 1) * 128], ones[:, 0:1], [sqB[c], onesB]) for c in range(8)])

    def cossin_tables(seq, it):
        t0 = it * TT
        TR.dma(lambda e: e.dma_start(out=posi[:], in_=posd[seq:seq + 1, t0:t0 + TT].partition_broadcast(128)),
               "pos", (), [posB])
        pf, pfb = scratch()
        dve_copy(pf[:], posi[:], [posB], [pfb])
        ang, angb = scratch()
        dve_ts(ang[:], pf[:], ropec[:, 0:1], None, ALU.mult, None, [pfb, ropeB], [angb])
        dve_ts(posi[:], ang[:], 1.0 / (2 * math.pi), None, ALU.mult, None, [angb], [posB])
        kf, kfb = scratch()
        dve_copy(kf[:], posi[:], [posB], [kfb])
        C1 = 6.28125
        C2 = 2 * math.pi - C1
        r1, r1b = scratch()
        dve_stt(r1[:], kf[:], -C1, ang[:], ALU.mult, ALU.add, [kfb, angb], [r1b])
        r, rb = scratch()
        dve_stt(r[:], kf[:], -C2, r1[:], ALU.mult, ALU.add, [kfb, r1b], [rb])
        m, mb = scratch()
        dve_ts(m[:], r[:], math.pi, -2 * math.pi, ALU.is_gt, ALU.mult, [rb], [mb])
        r2, r2b = scratch()
        dve_tt(r2[:], r[:], m[:], ALU.add, [rb, mb], [r2b])
        m2, m2b = scratch()
        dve_ts(m2[:], r2[:], -math.pi, 2 * math.pi, ALU.is_lt, ALU.mult, [r2b], [m2b])
        rs_, rsb = scratch()
        dve_tt(rs_[:], r2[:], m2[:], ALU.add, [r2b, m2b], [rsb])
        m3, m3b = scratch()
        dve_ts(m3[:], rs_[:], math.pi / 2, -2 * math.pi, ALU.is_gt, ALU.mult, [rsb], [m3b])
        rc, rcb = scratch()
        dve_stt(rc[:], rs_[:], math.pi / 2, m3[:], ALU.add, ALU.add, [rsb, m3b], [rcb])
        PI_S = 3.1415925
        rs2, rs2b = scratch()
        dve_ts(rs2[:], rs_[:], -PI_S, PI_S, ALU.max, ALU.min, [rsb], [rs2b])
        rc2, rc2b = scratch()
        dve_ts(rc2[:], rc[:], -PI_S, PI_S, ALU.max, ALU.min, [rcb], [rc2b])
        act(sinT[:], rs2[:], AF.Sin, [rs2b, ropeB], [sinB], scale=ropec[:, 1:2])
        act(cosT[:], rc2[:], AF.Sin, [rc2b], [cosB])

    def mixer(l, seq, it, skip_norm=False):
        o = l * NCOL
        d0 = l * 8
        def cons_a(rs, rsb):
            for c in range(8):
                dve_stt(hT[:, c, :], xbuf[:, c, :], cv[:, o + C_NMIX + c:o + C_NMIX + c + 1], rs[:], ALU.mult, ALU.mult,
                        [xB[c], cvB, rsb], [hB[c]])
        if not skip_norm:
            multi_norm([([(xbuf[:, c, :], [xB[c]]) for c in range(8)], D, EPS, cons_a)])
        if l == 0 and it == 0 and seq == 0:
            DBG('h0', hT[:], [128, 8, TT], BF16, hB)
        for j in range(5):
            psa, pba = proj_unit(hT, hB)
            psb_, pbb = proj_unit(hT, hB)
            t1, t1b = scratch()
            dve_tt(t1[:], psa[:], cosT[:], ALU.mult, [pba, cosB], [t1b])
            t2, t2b = scratch()
            dve_tt(t2[:], psb_[:], sinT[:], ALU.mult, [pbb, sinB], [t2b])
            if j < 4:
                pool_tt(qT[:, j, :], t1[:], t2[:], ALU.add, [t1b, t2b], [qB[j]])
            else:
                pool_tt(kT[l][:, 128:128 + TT], t1[:], t2[:], ALU.add, [t1b, t2b], [kB[l]])
        w, wb = next_unit()
        ps, pb = mmps()
        for blk in range(4):
            mm_group(ps[:, blk * 128:(blk + 1) * 128], pb,
                     [(hT[:, kc, blk * 128:(blk + 1) * 128], w[:, kc * 128:(kc + 1) * 128], [wb, hB[kc]]) for kc in range(8)])
        psv = ps[:].rearrange("p (b k d) -> p b k d", b=4, k=2, d=64)
        act(vaug[l][:, 1:5, 0, 0:64], psv[:, :, 0, :], AF.Copy, [pb], [vB[l]])
        act(vaug[l][:, 1:5, 1, 64:128], psv[:, :, 1, :], AF.Copy, [pb], [vB[l]])
        GK = 0.7978845608028654 * 0.044715
        gwh = {}

        def b2_cb(ch):
            ps, pb = proj_unit(hT, hB)
            act(cbs[:, ch, :], ps[:], AF.Copy, [pb], [cbB[ch]])

        def b2_cxcc(ch):
            ps, pb = proj_unit(hT, hB)
            cx, cxb = scratch()
            act(cx[:], ps[:], AF.Copy, [pb], [cxb])
            ps2, pb2 = proj_unit(hT, hB)
            pool_copy(prodbuf[:, ch, 0:2], prodc[l][:, ch, :], [pcB[l][ch]], [prB[ch]])
            dve_tt(prodbuf[:, ch, 2:TT + 2], ps2[:], cx[:], ALU.mult, [pb2, cxb], [prB[ch]])
            pool_copy(prodc[l][:, ch, :], prodbuf[:, ch, TT:TT + 2], [prB[ch]], [pcB[l][ch]])

        def b2_lx(ch):
            ps, pb = proj_unit(hT, hB)
            pool_copy(lxbuf[:, ch, 0:3], lxc[l][:, ch, :], [lcB[l][ch]], [lxB[ch]])
            act(lxbuf[:, ch, 3:TT + 3], ps[:], AF.Copy, [pb], [lxB[ch]])
            pool_copy(lxc[l][:, ch, :], lxbuf[:, ch, TT:TT + 3], [lxB[ch]], [lcB[l][ch]])

        def b2_lg(ch):
            ps, pb = proj_unit(hT, hB)
            x2, x2b = scratch()
            act(x2[:], ps[:], AF.Square, [pb], [x2b])
            lgs, lgb = scratch()
            act(lgs[:], ps[:], AF.Copy, [pb], [lgb])
            zp, zpb = scratch()
            dve_stt(zp[:], x2[:], 1.0 / 0.044715, lgs[:], ALU.add, ALU.mult, [x2b, lgb], [zpb])
            th, thb = scratch()
            act(th[:], zp[:], AF.Tanh, [zpb], [thb], scale=GK)
            dve_stt(glb[:, ch, :], th[:], 1.0, lgs[:], ALU.add, ALU.mult, [thb, lgb], [glB[ch]])

        def b2_gates():
            gwh["w"], gwh["b"] = next_unit()

        b2 = [lambda: b2_cb(0), lambda: b2_cb(1), lambda: b2_cxcc(0), lambda: b2_cxcc(1),
              lambda: b2_lx(0), lambda: b2_lx(1), lambda: b2_lg(0), lambda: b2_lg(1), b2_gates]
        tiT = [gvb[0], gvb[1]]; tiBf = [gvB[0], gvB[1]]
        aaT = [gvb[2], gvb[3]]; aaBf = [gvB[2], gvB[3]]
        wqT = [ubuf[0], ubuf[1]]; wqBf = [ubB[0], ubB[1]]
        trT = [gvb[4], gvb[5]]; trBf = [gvB[4], gvB[5]]
        def gn_consumer(chunks):
            def cons(rs, rsb):
                for j in chunks:
                    dve_stt(hT[:, j, :], mixbuf[:, j, :], cv[:, o + C_GN + j:o + C_GN + j + 1], rs[:], ALU.mult, ALU.mult,
                            mixdeps(j) + [cvB, rsb], [hB[j]])
            return cons

        def st_sconv(ch):
            wc = lambda j: cv[:, o + C_SCW + j * 2 + ch:o + C_SCW + j * 2 + ch + 1]
            t0, t0b = scratch()
            pool_ts(t0[:], prodbuf[:, ch, 2:TT + 2], wc(2), 0.0, ALU.mult, ALU.add, [prB[ch], cvB], [t0b])
            t1, t1b = scratch()
            dve_stt(t1[:], prodbuf[:, ch, 1:TT + 1], wc(1), t0[:], ALU.mult, ALU.add, [prB[ch], cvB, t0b], [t1b])
            t2, t2b = scratch()
            dve_stt(t2[:], prodbuf[:, ch, 0:TT], wc(0), t1[:], ALU.mult, ALU.add, [prB[ch], cvB, t1b], [t2b])
            dve_tt(mixbuf[:, 4 + ch, :], t2[:], cbs[:, ch, :], ALU.mult, [t2b, cbB[ch]], mixdeps(4 + ch))

        def st_lru_conv():
            for ch in range(2):
                wc = lambda j: cv[:, o + C_LCW + j * 2 + ch:o + C_LCW + j * 2 + ch + 1]
                t0, t0b = scratch()
                pool_ts(t0[:], lxbuf[:, ch, 3:TT + 3], wc(3), cv[:, o + C_LCB + ch:o + C_LCB + ch + 1], ALU.mult, ALU.add,
                        [lxB[ch], cvB], [t0b])
                t1, t1b = scratch()
                dve_stt(t1[:], lxbuf[:, ch, 2:TT + 2], wc(2), t0[:], ALU.mult, ALU.add, [lxB[ch], cvB, t0b], [t1b])
                t2, t2b = scratch()
                dve_stt(t2[:], lxbuf[:, ch, 1:TT + 1], wc(1), t1[:], ALU.mult, ALU.add, [lxB[ch], cvB, t1b], [t2b])
                dve_stt(lru_in[:, ch, :], lxbuf[:, ch, 0:TT], wc(0), t2[:], ALU.mult, ALU.add, [lxB[ch], cvB, t2b], [liB[ch]])
                act(lrub[:, ch, :], lru_in[:, ch, :], AF.Copy, [liB[ch]], [lbB[ch]])

        def st_lru_gates():
            pss = []
            for ch in range(2):
                psr, pbr = mmps()
                mm_group(psr[:], pbr, [(gwh['w'][:, ch * 128:(ch + 1) * 128], lrub[:, ch, :], [gwh['b'], lbB[ch]])])
                act(trT[ch][:], psr[:], AF.Tanh, [pbr, derB], [trBf[ch]], bias=der[:, d0 + ch:d0 + ch + 1], scale=0.5)
                psi, pbi = mmps()
                mm_group(psi[:], pbi, [(gwh['w'][:, (2 + ch) * 128:(3 + ch) * 128], lrub[:, ch, :], [gwh['b'], lbB[ch]])])
                act(tiT[ch][:], psi[:], AF.Tanh, [pbi, derB], [tiBf[ch]], bias=der[:, d0 + 2 + ch:d0 + 3 + ch], scale=0.5)
            for ch in range(2):
                act(aaT[ch][:], trT[ch][:], AF.Exp, [trBf[ch], derB], [aaBf[ch]],
                    bias=der[:, d0 + 4 + ch:d0 + 5 + ch], scale=der[:, d0 + 4 + ch:d0 + 5 + ch])
                a2, a2b = scratch()
                act(a2[:], trT[ch][:], AF.Exp, [trBf[ch], derB], [a2b],
                    bias=der[:, d0 + 6 + ch:d0 + 7 + ch], scale=der[:, d0 + 6 + ch:d0 + 7 + ch])
                dve_ts(wqT[ch][:, 0:TT], a2[:], -0.25, 0.25, ALU.mult, ALU.add, [a2b], [wqBf[ch]])
            for ch in range(2):
                act(wqT[ch][:, 0:TT], wqT[ch][:, 0:TT], AF.Sqrt, [wqBf[ch]], [wqBf[ch]])

        def st_lru_tail(ch):
            u1, u1b = scratch()
            dve_stt(u1[:], tiT[ch][:], 1.0, lru_in[:, ch, :], ALU.add, ALU.mult, [tiBf[ch], liB[ch]], [u1b])
            uu, uub = scratch()
            dve_tt(uu[:], u1[:], wqT[ch][:, 0:TT], ALU.mult, [u1b, wqBf[ch]], [uub])
            hh, hhb = scratch()
            TR.op("dve", lambda e, hh=hh, aa=aaT[ch], uu=uu, ch=ch: e.tensor_tensor_scan(
                out=hh[:], data0=aa[:], data1=uu[:], initial=hstate[l][:, ch:ch + 1], op0=ALU.mult, op1=ALU.add),
                [aaBf[ch], uub, hsB[l][ch]], [hhb])
            dve_tt(mixbuf[:, 6 + ch, :], hh[:], glb[:, ch, :], ALU.mult, [hhb, glB[ch]], mixdeps(6 + ch))
            dve_copy(hstate[l][:, ch:ch + 1], hh[:, TT - 1:TT], [hhb], [hsB[l][ch]])

        def st_side_norms():
            multi_norm([([(mixbuf[:, j, :], mixdeps(j)) for j in (4, 5)], 256, EPS, gn_consumer((4, 5))),
                        ([(mixbuf[:, j, :], mixdeps(j)) for j in (6, 7)], 256, 4 * EPS, gn_consumer((6, 7)))])

        def _two(a, b):
            def f():
                a()
                b()
            return f
        side = [None, _two(lambda: st_sconv(0), lambda: st_sconv(1)), st_lru_conv, None, st_lru_gates,
                lambda: st_lru_tail(0), lambda: st_lru_tail(1), st_side_norms]
        iters = [(c, kv) for c in range(8) for kv in range(2)]
        ctx = {}

        def geom(c):
            return c // 2, (c % 2 == 0), (it * 8 + c >= 2)

        def emit_S(i):
            c, kv = iters[i]
            b, even, prev_valid = geom(c)
            kp = slice(kv * 64, kv * 64 + 64)
            k2 = st["P"] % 2
            k3 = st["P"] % 2
            st["P"] += 1
            Sps, Sb = PS[4 + k2], psB[4 + k2]
            ctx[i] = (Sps, Sb, PS[6 + k2], psB[6 + k2], Pbuf[k3], PB[k3])
            qv = qT[kp, 0:4, 64 * c:64 * c + 64]
            if prev_valid:
                mm_group(Sps[:, 0:256], Sb, [(kT[l][kp, 128 * b:128 * b + 128], qv, [kB[l]] + qB)])
            mcur = 64 if even else 128
            mm_group(Sps[0:mcur, 256:512], Sb, [(kT[l][kp, 128 * (b + 1):128 * (b + 1) + mcur], qv, [kB[l]] + qB)])

        def emit_exp(i):
            c, kv = iters[i]
            b, even, prev_valid = geom(c)
            Sps, Sb, Ops, Ob, P, Pb_ = ctx[i]
            if prev_valid:
                rows = slice(0, 128) if even else slice(64, 128)
                act(P[rows, 0:256], Sps[rows, 0:256], AF.Exp, [Sb], [Pb_], scale=0.125)
            rows = slice(0, 64) if even else slice(0, 128)
            act(P[rows, 256:512], Sps[rows, 256:512], AF.Exp, [Sb], [Pb_], scale=0.125)

        def emit_O(i):
            c, kv = iters[i]
            b, even, prev_valid = geom(c)
            Sps, Sb, Ops, Ob, P, Pb_ = ctx[i]
            items = []
            if prev_valid:
                if even:
                    items.append((vaug[l][:, b, kv, :], P[:, 0:256], [vB[l], Pb_]))
                else:
                    items.append((vaug[l][64:128, b, kv, :], P[64:128, 0:256], [vB[l], Pb_]))
            if even:
                items.append((vaug[l][0:64, b + 1, kv, :], P[0:64, 256:512], [vB[l], Pb_]))
            else:
                items.append((vaug[l][:, b + 1, kv, :], P[:, 256:512], [vB[l], Pb_]))
            so = l * 512 + kv * 256
            items.append((sinkL[0:1, kv, :], sinkRb[0:1, so:so + 256], [sLB, sRB]))
            mm_group(Ops[:, 0:256], Ob, items)

        def emit_evac(i):
            c, kv = iters[i]
            Sps, Sb, Ops, Ob, P, Pb_ = ctx[i]
            orow = slice(0, 64) if kv == 0 else slice(64, 128)
            drow = slice(64, 128) if kv == 0 else slice(0, 64)
            dve_copy(mixbuf[orow, 0:4, 64 * c:64 * c + 64], Ops[orow, 0:256].rearrange("p (g q) -> p g q", g=4),
                     [Ob], [aoB[c]])
            dve_copy(denall[orow, c, :, :], Ops[drow, 0:256].rearrange("p (g q) -> p g q", g=4), [Ob], [denB[c]])

        emit_S(0)
        for i, (c, kv) in enumerate(iters):
            emit_exp(i)
            if i + 1 < len(iters):
                emit_S(i + 1)
            if i < len(b2):
                b2[i]()
            emit_O(i)
            emit_evac(i)
            if kv == 1 and side[c] is not None:
                side[c]()
        dflat = denall[:].rearrange("p c g q -> p (c g q)")
        act(dflat, dflat, AF.Ln, denB, denB)
        act(dflat, dflat, AF.Exp, denB, denB, scale=-1.0)
        for j in range(4):
            mv = mixbuf[:, j, :].rearrange("p (c q) -> p c q", c=8)
            dve_tt(mv, mv, denall[:, :, j, :], ALU.mult, mixdeps(j) + denB, mixdeps(j))
        if l == 0 and it == 0 and seq == 0:
            DBG('qT', qT[:], [128, 4, TT], BF16, qB)
            DBG('kT', kT[0][:], [128, 128 + TT], BF16, [kB[0]])
            DBG('vaug', vaug[0][:], [128, 5, 2, 128], BF16, [vB[0]])
            DBG('cbs', cbs[:], [128, 2, TT], F32, cbB)
            DBG('prod', prodbuf[:], [128, 2, TT + 2], F32, prB)
            DBG('lxb', lxbuf[:], [128, 2, TT + 3], F32, lxB)
            DBG('gl', glb[:], [128, 2, TT], F32, glB)
            DBG('cos', cosT[:], [128, TT], F32, [cosB])
            DBG('sin', sinT[:], [128, TT], F32, [sinB])
        if l == 0 and it == 0 and seq == 0:
            DBG('mix', mixbuf[:], [128, 8, TT], F32, mixB + aoB)
            DBG('lru_in', lru_in[:], [128, 2, TT], F32, liB)
        pool_copy(kT[l][:, 0:128], kT[l][:, TT:TT + 128], [kB[l]], [kB[l]])
        pool_copy(vaug[l][:, 0, :, :], vaug[l][:, 4, :, :], [vB[l]], [vB[l]])
        multi_norm([([(mixbuf[:, j, :], mixdeps(j)) for j in (0, 1, 2, 3)], 512, EPS, gn_consumer((0, 1, 2, 3)))])
        for n in range(8):
            ps, pb = proj_unit(hT, hB)
            dve_tt(xbuf[:, n, :], ps[:], xbuf[:, n, :], ALU.add, [pb, xB[n]], [xB[n]])
            stats_accum(n)
        st["pre_stats"] = True

    def ffn(l, seq, it, early=None):
        o = l * NCOL
        if l == 0 and it == 0 and seq == 0:
            DBG('x_mix', xbuf[:], [128, 8, TT], F32, xB)
        def cons_h(rs, rsb):
            for c in range(8):
                dve_stt(hT[:, c, :], xbuf[:, c, :], cv[:, o + C_NFFN + c:o + C_NFFN + c + 1], rs[:], ALU.mult, ALU.mult,
                        [xB[c], cvB, rsb], [hB[c]])
        multi_norm([([(xbuf[:, c, :], [xB[c]]) for c in range(8)], D, EPS, cons_h)])
        st["mmset"] = "ffn"
        pending = None

        def finish_pair(jj, G, V):
            sg, sgb = scratch()
            act(sg[:], gvb[G][:], AF.Silu, [gvB[G]], [sgb])
            dve_tt(actT[:, jj, :], sg[:], gvb[V][:], ALU.mult, [sgb, gvB[V]], [actB[jj]])

        for jj in range(22):
            pair = []
            for s in range(2):
                uidx = jj * 2 + s
                ps, pb = proj_unit(hT, hB)
                ub, ubb = ubuf[st["ub"] % 4], ubB[st["ub"] % 4]
                st["ub"] += 1
                wc = lambda j: cv[:, o + C_FCW + j * 44 + uidx:o + C_FCW + j * 44 + uidx + 1]
                pool_copy(ub[:, 0:2], ucarry[l][:, uidx, :], [ucB[l][uidx]], [ubb])
                act(ub[:, 2:TT + 2], ps[:], AF.Copy, [pb], [ubb])
                t0, t0b = scratch()
                act(t0[:], ps[:], AF.Identity, [pb, cvB], [t0b], bias=cv[:, o + C_FCB + uidx:o + C_FCB + uidx + 1], scale=wc(2))
                pool_copy(ucarry[l][:, uidx, :], ub[:, TT:TT + 2], [ubb], [ucB[l][uidx]])
                t1, t1b = scratch()
                dve_stt(t1[:], ub[:, 1:TT + 1], wc(1), t0[:], ALU.mult, ALU.add, [ubb, cvB, t0b], [t1b])
                gi = (jj % 3) * 2 + s
                dve_stt(gvb[gi][:], ub[:, 0:TT], wc(0), t1[:], ALU.mult, ALU.add, [ubb, cvB, t1b], [gvB[gi]])
                pair.append(gi)
                if s == 0 and pending is not None:
                    finish_pair(*pending)
                    pending = None
            pending = (jj, pair[0], pair[1])
            if early is not None and jj == 15:
                early_next_squares(early[2], early[3])
        finish_pair(*pending)
        if early is not None:
            nseq, nit, nxb, nxbl = early
            early_next_stats(nxb, nxbl)
            cossin_tables(nseq, nit)
        for m in range(8):
            ps, pb = mmps()
            for gq in range(3):
                w, wb = next_unit()
                nf = 8 if gq < 2 else 6
                for fi in range(nf):
                    f = gq * 8 + fi
                    first = (f == 0)
                    last = (f == 21)
                    TR.op("pe", (lambda e, ps=ps, w=w, fi=fi, f=f, first=first, last=last:
                                 e.matmul(ps[:], lhsT=w[:, fi * 128:(fi + 1) * 128], rhs=actT[:, f, :], start=first, stop=last)),
                          [wb, actB[f]], [pb], inc=(fi == nf - 1))
            dve_tt(xbuf[:, m, :], ps[:], xbuf[:, m, :], ALU.add, [pb, xB[m]], [xB[m]])
            stats_accum(m)
        st["mmset"] = "mix"
        st["pre_stats"] = True

    out_toks = []
    tiles = [(seq, it) for seq in range(NSEQ) for it in range(NT)]

    def load_x(k):
        sq_, it_ = tiles[k]
        xb_, xbl_ = xbufs[k % 2], xBs[k % 2]
        TR.dma(lambda e, sq_=sq_, t0=it_ * TT, xb_=xb_: e.dma_start(
            out=xb_[:], in_=xT[sq_, :, t0:t0 + TT].rearrange("(c p) t -> p c t", p=128)), f"xin{k % 2}", (), xbl_)

    for k, (seq, it) in enumerate(tiles):
        xbuf, xB = xbufs[k % 2], xBs[k % 2]
        has_next = k + 1 < len(tiles)
        if it == 0:
            TR.new_epoch()
            for l in range(L):
                for ch in range(2):
                    pool_memset(prodc[l][:, ch, :], 0.0, [pcB[l][ch]])
                    pool_memset(lxc[l][:, ch, :], 0.0, [lcB[l][ch]])
                    pool_memset(hstate[l][:, ch:ch + 1], 0.0, [hsB[l][ch]])
                pool_memset(ucarry[l][:], 0.0, ucB[l])
        t0 = it * TT
        if k == 0:
            load_x(0)
            cossin_tables(seq, it)
        if has_next:
            load_x(k + 1)
        for l in range(L):
            mixer(l, seq, it, skip_norm=(l == 0 and k > 0))
            early = None
            if l == L - 1 and has_next:
                early = (tiles[k + 1][0], tiles[k + 1][1], xbufs[(k + 1) % 2], xBs[(k + 1) % 2])
            ffn(l, seq, it, early=early)
            if l == 0 and it == 0 and seq == 0:
                DBG('x_ffn', xbuf[:], [128, 8, TT], F32, xB)
                DBG('act', actT[:], [128, 22, TT], BF16, actB)
        o = (L - 1) * NCOL

        def cons_f(rs, rsb, o=o, xb_=xbuf, xbl_=xB):
            for c in range(8):
                dve_stt(mixbuf[:, c, :], xb_[:, c, :], cv[:, o + C_FIN + c:o + C_FIN + c + 1], rs[:], ALU.mult, ALU.mult,
                        [xbl_[c], cvB, rsb], mixdeps(c))

        groups = [([(xbuf[:, c, :], [xB[c]]) for c in range(8)], D, EPS, cons_f)]
        if has_next:
            def cons_n(rs, rsb, xb_=xbufs[(k + 1) % 2], xbl_=xBs[(k + 1) % 2]):
                for c in range(8):
                    dve_stt(hT[:, c, :], xb_[:, c, :], cv[:, C_NMIX + c:C_NMIX + c + 1], rs[:], ALU.mult, ALU.mult,
                            [xbl_[c], cvB, rsb], [hB[c]])
            groups.append((None, D, EPS, cons_n))
        multi_norm(groups)
        rd_all = []
        for c in range(8):
            rd_all += mixdeps(c)
        tok = TR.dma(lambda e, seq=seq, t0=t0: e.dma_start(
            out=outT[seq, :, t0:t0 + TT].rearrange("(c p) t -> p c t", p=128), in_=mixbuf[:]), "oout", rd_all, ())
        out_toks.append(tok)
    TR.final_wait("sp", [out_toks[-1]])

    sems = {}
    for key in TR.cnt:
        nm = "s_" + "_".join(str(k) for k in key)
        sems[key] = es.enter_context(nc.semaphore(nm))
    block = es.enter_context(nc.Block())

    def run(eng_name, e):
        for waits, fn, inckey, incval in TR.plan[eng_name]:
            for key, val in waits:
                e.wait_ge(sems[key], val)
            if fn is None:
                continue
            ins = fn(e)
            if inckey is not None:
                ins.then_inc(sems[inckey], incval)

    @block.sync
    def _(e):
        run("sp", e)

    @block.tensor
    def _(e):
        run("pe", e)

    @block.scalar
    def _(e):
        run("act", e)

    @block.vector
    def _(e):
        run("dve", e)

    @block.gpsimd
    def _(e):
        run("pool", e)

    es.close()
    stats = {e: len(TR.plan[e]) for e in ENGS}
    stats["sems"] = len(sems)
    stats["maxcnt"] = max(TR.cnt.values())
    stats['dbg'] = list(dbg_outs)
    return nc, stats


def run_cores(inputs, NSEQ, S, n_cores=8, debug=False):
    wst, cvec, sinkr, ropec = _prep_weights(inputs)
    x = np.asarray(inputs["x"], np.float32)
    pos = np.asarray(inputs["positions"], np.int32)
    nc, stats = build_program(NSEQ, S, debug)
    in_maps = []
    for c in range(n_cores):
        xs = x[c * NSEQ:(c + 1) * NSEQ]
        in_maps.append({
            "xT": np.ascontiguousarray(xs.transpose(0, 2, 1)),
            "pos": np.ascontiguousarray(pos[c * NSEQ:(c + 1) * NSEQ]),
            "wst": wst, "cvec": cvec, "ropec": ropec, "sinkr": sinkr, "ident": np.eye(128, dtype=np.float32),
        })
    res = run_bass_kernel_spmd(nc, in_maps, core_ids=list(range(n_cores)))
    outs = [np.asarray(r["outT"]).transpose(0, 2, 1) for r in res.results]
    if debug:
        stats['dbgvals'] = [{n: np.asarray(r['dbg_' + n]) for n in stats['dbg']} for r in res.results]
    return np.ascontiguousarray(np.concatenate(outs, axis=0)).astype(np.float32), stats


def kernel(**inputs):
    out, _ = run_cores(inputs, NSEQ=4, S=2048)
    return out
```
